# Optimizing a Trainium2 kernel written in Bass

```python
import jax, jax.numpy as jnp
from jax import lax
import numpy as np

D_MODEL = 1024
BATCH = 2
SEQ = 8192
DEPTH = 2

CHUNK = 64
N_META = 16
D_MIX = D_MODEL
D_POOL = D_MIX // 4
D_CONV = D_MIX // 4
D_RET = D_MIX - D_POOL - D_CONV
POOL_WINDOWS = (2, 4, 8, 16)
N_POOL_GROUPS = len(POOL_WINDOWS)
POOL_GROUP = D_POOL // N_POOL_GROUPS
CONV_WIDTH = 31
RET_HEADS = 4
RET_HEAD_DIM = D_RET // RET_HEADS
ROPE_BASE = 10000.0
D_FF = ((8 * D_MODEL // 3 + 63) // 64) * 64
D_IN = D_POOL + 2 * D_CONV + 4 * D_RET
DEEPNORM_ALPHA = (2.0 * DEPTH) ** 0.25
DEEPNORM_BETA = (8.0 * DEPTH) ** -0.25
LN_EPS = 1e-5

kernel_name = "hybrid_pool_conv_retention_deepnorm_trunk"


def layer_norm(x, g, b):
    xf = x.astype(jnp.float32)
    mu = jnp.mean(xf, axis=-1, keepdims=True)
    var = jnp.mean(jnp.square(xf - mu), axis=-1, keepdims=True)
    return ((xf - mu) * lax.rsqrt(var + LN_EPS) * g + b).astype(x.dtype)


def swiglu_ffn(x, w13, w2):
    a, u = jnp.split(x @ w13, 2, axis=-1)
    return (jax.nn.silu(a) * u) @ w2


def pool_mixer(xp, w_pool, scale):
    B, L, _ = xp.shape
    xf = xp.astype(jnp.float32)
    cs = jnp.concatenate([jnp.zeros((B, 1, D_POOL), jnp.float32), jnp.cumsum(xf, axis=1)], axis=1)
    t = jnp.arange(L)
    outs = []
    for gi, w in enumerate(POOL_WINDOWS):
        lo, hi = gi * POOL_GROUP, (gi + 1) * POOL_GROUP
        csg = cs[..., lo:hi]
        start = jnp.maximum(t + 1 - w, 0)
        win_sum = csg[:, 1:] - csg[:, start]
        count = (t + 1 - start).astype(jnp.float32)
        outs.append(win_sum / count[None, :, None] - xf[..., lo:hi])
    y = jnp.stack(outs, axis=2)
    y = jnp.einsum('blgc,gcd->blgd', y, w_pool.astype(jnp.float32)).reshape(B, L, D_POOL)
    return (y * scale).astype(xp.dtype)


def conv_module(a, gate, w_dw, b_dw, ln_g, ln_b, w_pw):
    u = a * jax.nn.sigmoid(gate)
    y = lax.conv_general_dilated(u, w_dw[:, None, :], window_strides=(1,),
                                 padding=[(CONV_WIDTH - 1, 0)],
                                 dimension_numbers=('NWC', 'WIO', 'NWC'),
                                 feature_group_count=D_CONV) + b_dw
    y = jax.nn.silu(layer_norm(y, ln_g, ln_b))
    return y @ w_pw


def rope(x, cos, sin):
    x1, x2 = jnp.split(x, 2, axis=-1)
    return jnp.concatenate([x1 * cos - x2 * sin, x2 * cos + x1 * sin], axis=-1)


def retention(q, k, v, g, gn_g):
    B, L, _ = q.shape
    f32 = jnp.float32
    pos = jnp.arange(L, dtype=f32)
    inv_freq = ROPE_BASE ** (-jnp.arange(0, RET_HEAD_DIM, 2, dtype=f32) / RET_HEAD_DIM)
    ang = pos[:, None] * inv_freq[None, :]
    cos, sin = jnp.cos(ang), jnp.sin(ang)
    heads = lambda t: t.astype(f32).reshape(B, L, RET_HEADS, RET_HEAD_DIM).transpose(0, 2, 1, 3)
    qh = rope(heads(q), cos, sin)
    kh = rope(heads(k), cos, sin) * (RET_HEAD_DIM ** -0.5)
    vh = heads(v)
    P = (-N_META) % CHUNK
    NC = (L + P) // CHUNK
    chunk = lambda t: jnp.pad(t, ((0, 0), (0, 0), (P, 0), (0, 0))).reshape(B, RET_HEADS, NC, CHUNK, RET_HEAD_DIM)
    qc, kc, vc = chunk(qh), chunk(kh), chunk(vh)
    log_gamma = jnp.log(1.0 - 2.0 ** (-5.0 - jnp.arange(RET_HEADS, dtype=f32)))
    i = jnp.arange(CHUNK, dtype=f32)
    intra_decay = jnp.exp(log_gamma[:, None, None] * jnp.abs(i[:, None] - i[None, :]))
    s = jnp.einsum('bhnid,bhnjd->bhnij', qc, kc) * intra_decay[None, :, None]
    o_intra = jnp.einsum('bhnij,bhnjd->bhnid', s, vc)
    q_decay = jnp.exp(log_gamma[:, None] * (i + 1.0))[None, :, :, None]
    k_decay = jnp.exp(log_gamma[:, None] * (CHUNK - 1.0 - i))[None, :, :, None]
    chunk_decay = jnp.exp(log_gamma * CHUNK)[None, :, None, None]

    def step(state, inp):
        q_n, k_n, v_n = inp
        o = jnp.einsum('bhid,bhde->bhie', q_n * q_decay, state)
        state = state * chunk_decay + jnp.einsum('bhjd,bhje->bhde', k_n * k_decay, v_n)
        return state, o

    to_scan = lambda t: t.transpose(2, 0, 1, 3, 4)
    state0 = jnp.zeros((B, RET_HEADS, RET_HEAD_DIM, RET_HEAD_DIM), f32)
    _, o_cross = lax.scan(step, state0, (to_scan(qc), to_scan(kc), to_scan(vc)))
    o = o_intra + o_cross.transpose(1, 2, 0, 3, 4)
    o = o.reshape(B, RET_HEADS, NC * CHUNK, RET_HEAD_DIM)[:, :, P:]
    mu = jnp.mean(o, axis=-1, keepdims=True)
    var = jnp.mean(jnp.square(o - mu), axis=-1, keepdims=True)
    o = ((o - mu) * lax.rsqrt(var + LN_EPS)).transpose(0, 2, 1, 3).reshape(B, L, D_RET) * gn_g
    return (jax.nn.silu(g.astype(f32)) * o).astype(g.dtype)


def token_mix(h, w_in, pool_w, pool_scale, conv_dw, conv_db, conv_ln_g, conv_ln_b, conv_pw, ret_gn_g, w_out):
    z = h @ w_in
    splits = [D_POOL, D_POOL + D_CONV, D_POOL + 2 * D_CONV,
              D_POOL + 2 * D_CONV + D_RET, D_POOL + 2 * D_CONV + 2 * D_RET,
              D_POOL + 2 * D_CONV + 3 * D_RET]
    xp, ca, cg, q, k, v, g = jnp.split(z, splits, axis=-1)
    y_pool = pool_mixer(xp, pool_w, pool_scale)
    y_conv = conv_module(ca, cg, conv_dw, conv_db, conv_ln_g, conv_ln_b, conv_pw)
    y_ret = retention(q, k, v, g, ret_gn_g)
    return jnp.concatenate([y_pool, y_conv, y_ret], axis=-1) @ w_out


def setup_inputs(seed: int = 0) -> dict:
    key = jax.random.key(seed)
    ks = jax.random.split(key, 24)
    nrm = lambda k, shape, s: jax.random.normal(k, shape, jnp.float32) * s
    ones_n = lambda k, shape: 1.0 + 0.05 * jax.random.normal(k, shape, jnp.float32)
    return {
        "x": nrm(ks[0], (BATCH, SEQ, D_MODEL), 1.0),
        "meta": nrm(ks[1], (N_META, D_MODEL), 1.0),
        "ln_in_g": ones_n(ks[2], (D_MODEL,)),
        "ln_in_b": nrm(ks[3], (D_MODEL,), 0.02),
        "ffn1_w13": nrm(ks[4], (DEPTH, D_MODEL, 2 * D_FF), D_MODEL ** -0.5),
        "ffn1_w2": nrm(ks[5], (DEPTH, D_FF, D_MODEL), DEEPNORM_BETA * D_FF ** -0.5),
        "w_in": nrm(ks[6], (DEPTH, D_MODEL, D_IN), D_MODEL ** -0.5),
        "pool_w": nrm(ks[7], (DEPTH, N_POOL_GROUPS, POOL_GROUP, POOL_GROUP), POOL_GROUP ** -0.5),
        "pool_scale": ones_n(ks[8], (DEPTH, D_POOL)),
        "conv_dw": nrm(ks[9], (DEPTH, CONV_WIDTH, D_CONV), CONV_WIDTH ** -0.5),
        "conv_db": nrm(ks[10], (DEPTH, D_CONV), 0.02),
        "conv_ln_g": ones_n(ks[11], (DEPTH, D_CONV)),
        "conv_ln_b": nrm(ks[12], (DEPTH, D_CONV), 0.02),
        "conv_pw": nrm(ks[13], (DEPTH, D_CONV, D_CONV), D_CONV ** -0.5),
        "ret_gn_g": ones_n(ks[14], (DEPTH, D_RET)),
        "w_out": nrm(ks[15], (DEPTH, D_MIX, D_MODEL), DEEPNORM_BETA * D_MIX ** -0.5),
        "ffn2_w13": nrm(ks[16], (DEPTH, D_MODEL, 2 * D_FF), D_MODEL ** -0.5),
        "ffn2_w2": nrm(ks[17], (DEPTH, D_FF, D_MODEL), DEEPNORM_BETA * D_FF ** -0.5),
        "ln_g": ones_n(ks[18], (DEPTH, 3, D_MODEL)),
        "ln_b": nrm(ks[19], (DEPTH, 3, D_MODEL), 0.02),
    }


def reference(x, meta, ln_in_g, ln_in_b, ffn1_w13, ffn1_w2, w_in, pool_w, pool_scale,
              conv_dw, conv_db, conv_ln_g, conv_ln_b, conv_pw, ret_gn_g, w_out,
              ffn2_w13, ffn2_w2, ln_g, ln_b):
    B = x.shape[0]
    h = jnp.concatenate([jnp.broadcast_to(meta[None].astype(x.dtype), (B, N_META, D_MODEL)), x], axis=1)
    h = layer_norm(h, ln_in_g, ln_in_b)
    for l in range(DEPTH):
        h = layer_norm(DEEPNORM_ALPHA * h + 0.5 * swiglu_ffn(h, ffn1_w13[l], ffn1_w2[l]), ln_g[l, 0], ln_b[l, 0])
        mix = token_mix(h, w_in[l], pool_w[l], pool_scale[l], conv_dw[l], conv_db[l],
                        conv_ln_g[l], conv_ln_b[l], conv_pw[l], ret_gn_g[l], w_out[l])
        h = layer_norm(DEEPNORM_ALPHA * h + mix, ln_g[l, 1], ln_b[l, 1])
        h = layer_norm(DEEPNORM_ALPHA * h + 0.5 * swiglu_ffn(h, ffn2_w13[l], ffn2_w2[l]), ln_g[l, 2], ln_b[l, 2])
    return h[:, N_META:]
```

```python
import numpy as np
from contextlib import ExitStack
import concourse.bass as bass
import concourse.mybir as mybir
from concourse.bass_utils import run_bass_kernel_spmd

F32 = mybir.dt.float32
BF16 = mybir.dt.bfloat16
AF = mybir.ActivationFunctionType
ALU = mybir.AluOpType

D = 1024
NFT = 8
DFF = 2752
NHT = 22
DIN = 2816
NPRE = 16
BLK = 2048
T = NPRE + BLK
TILES = [(0, 16), (16, 512), (528, 512), (1040, 512), (1552, 512)]
HALVES = [(0, 1, 2), (3, 4)]
HALF_OFF = [0, 1040]
HALF_LEN = [1040, 1024]
DEPTH = 2
ALPHA = (2.0 * DEPTH) ** 0.25
LN_EPS = 1e-5
EPS2 = LN_EPS / (ALPHA * ALPHA)
CHUNK = 64
NHEAD = 4
DH = 128
GAMMAS = [1.0 - 2.0 ** (-5.0 - h) for h in range(NHEAD)]
QSCALE = DH ** -0.5

ENGS = ("pe", "act", "dve", "pool", "sp")
SAME_SYNC = True


class Op:
    __slots__ = ("eng", "fn", "deps", "lane", "needs", "val", "stream")


class Prog:
    def __init__(self):
        self.ops = {e: [] for e in ENGS}
        self.lastw = {}
        self.readers = {}
        self.lane_cnt = {}
        self.last_in_stream = {}

    def add(self, eng, fn, reads=(), writes=(), lane=None):
        o = Op()
        o.eng, o.fn, o.lane, o.needs, o.val = eng, fn, lane, False, None
        o.stream = lane if lane is not None else eng
        deps = {}
        for r in reads:
            p = self.lastw.get(r)
            if p is not None:
                deps[id(p)] = p
            if isinstance(r, tuple) and r[0] == "bank":
                for st, p in self.readers.get(r, {}).items():
                    if st != o.stream:
                        deps[id(p)] = p
        for r in writes:
            p = self.lastw.get(r)
            if p is not None:
                deps[id(p)] = p
            for p in self.readers.get(r, {}).values():
                deps[id(p)] = p
        o.deps = list(deps.values())
        for p in o.deps:
            p.needs = True
        for r in writes:
            self.lastw[r] = o
            self.readers[r] = {}
        for r in reads:
            self.readers.setdefault(r, {})[o.stream] = o
        if lane is not None:
            self.lane_cnt[lane] = self.lane_cnt.get(lane, 0) + 16
            o.val = self.lane_cnt[lane]
        self.ops[eng].append(o)
        self.last_in_stream[o.stream] = o
        return o

    def barrier(self):
        lasts = list(self.last_in_stream.values())
        for p in lasts:
            p.needs = True
        for e in ENGS:
            o = Op()
            o.eng, o.fn, o.lane, o.needs, o.val = e, None, None, False, None
            o.stream = e
            o.deps = [p for p in lasts]
            self.ops[e].append(o)
        self.lastw = {}
        self.readers = {}

    def emit(self, nc):
        for e in ENGS:
            c = 0
            for o in self.ops[e]:
                if o.lane is None and o.needs and o.fn is not None:
                    c += 1
                    o.val = c
                elif o.lane is None:
                    o.val = None
        with ExitStack() as es:
            sems = {}
            for e in ENGS[:4]:
                sems[e] = es.enter_context(nc.semaphore("sem_" + e))
            for ln in self.lane_cnt:
                sems[ln] = es.enter_context(nc.semaphore("lane_" + str(ln)))
            block = es.enter_context(nc.Block())

            def run(e, engine):
                known = {}
                for o in self.ops[e]:
                    need = {}
                    for p in o.deps:
                        if p is o:
                            continue
                        if p.lane is not None and p.lane == o.lane:
                            continue
                        if p.lane is None:
                            if p.eng == e and (e == "pe" or not SAME_SYNC):
                                continue
                            if p.val is None:
                                continue
                        key = p.stream
                        if p.val > need.get(key, 0):
                            need[key] = p.val
                    for key, val in need.items():
                        if known.get(key, 0) >= val:
                            continue
                        engine.wait_ge(sems[key], val)
                        known[key] = val
                    if o.fn is None:
                        continue
                    inst = o.fn(engine)
                    if o.lane is not None:
                        inst.then_inc(sems[o.lane], 16)
                    elif o.needs:
                        inst.then_inc(sems[e], 1)

            @block.tensor
            def _(eng):
                run("pe", eng)

            @block.scalar
            def _(eng):
                run("act", eng)

            @block.vector
            def _(eng):
                run("dve", eng)

            @block.gpsimd
            def _(eng):
                run("pool", eng)

            @block.sync
            def _(eng):
                run("sp", eng)


class Arena:
    def __init__(self, t, nelem):
        self.t = t
        self.n = nelem
        self.off = 0
        self.uid = 0

    def mark(self):
        return self.off

    def reset(self, m):
        self.off = m

    def alloc(self, free_shape, dtype, name):
        n = int(np.prod(free_shape))
        if dtype == F32:
            self.off += self.off % 2
            ap = self.t[:, self.off:self.off + 2 * n].bitcast(F32)
            self.off += 2 * n
        else:
            ap = self.t[:, self.off:self.off + n]
            self.off += n
        assert self.off <= self.n, ("SBUF arena overflow", name, self.off, self.n)
        if len(free_shape) == 2:
            ap = ap.rearrange("p (a b) -> p a b", a=free_shape[0])
        elif len(free_shape) == 3:
            ap = ap.rearrange("p (a b c) -> p a b c", a=free_shape[0], b=free_shape[1])
        self.uid += 1
        return ap


class Builder:
    def __init__(self, nc):
        self.nc = nc
        self.P = Prog()
        self.es = ExitStack()
        ARENA_ELEMS = 103000
        at = self.es.enter_context(nc.sbuf_tensor("arena", [128, ARENA_ELEMS], BF16))
        self.A = Arena(at, ARENA_ELEMS)
        self.banks = [self.es.enter_context(nc.psum_tensor(f"bank{i}", [128, 512], F32)) for i in range(8)]
        self.din = {}
        self.dout = {}
        self.rot = {}

    def inp(self, name, shape):
        if name not in self.din:
            self.din[name] = self.nc.dram_tensor(name, list(shape), F32, kind="ExternalInput").ap()
        return self.din[name]

    def outp(self, name, shape):
        self.dout[name] = self.nc.dram_tensor(name, list(shape), F32, kind="ExternalOutput").ap()
        return self.dout[name]

    def scratch(self, name, shape):
        return self.nc.dram_tensor(name, list(shape), F32, kind="Internal").ap()

    def mm(self, out, lhsT, rhs, start, stop, reads, writes):
        return self.P.add("pe", lambda e: e.matmul(out, lhsT, rhs, start=start, stop=stop), reads, writes)

    def tr(self, out, in_, ident, reads, writes):
        return self.P.add("pe", lambda e: e.transpose(out, in_, ident), reads, writes)

    def act(self, out, in_, func, reads, writes, scale=1.0, bias=0.0):
        return self.P.add("act", lambda e: e.activation(out=out, in_=in_, func=func, bias=bias, scale=scale), reads, writes)

    def tt(self, eng, out, in0, in1, op, reads, writes):
        return self.P.add(eng, lambda e: e.tensor_tensor(out=out, in0=in0, in1=in1, op=op), reads, writes)

    def ts(self, eng, out, in0, s1, s2, op0, op1, reads, writes):
        return self.P.add(eng, lambda e: e.tensor_scalar(out=out, in0=in0, scalar1=s1, scalar2=s2, op0=op0, op1=op1), reads, writes)

    def stt(self, eng, out, in0, scalar, in1, op0, op1, reads, writes):
        return self.P.add(eng, lambda e: e.scalar_tensor_tensor(out=out, in0=in0, scalar=scalar, in1=in1, op0=op0, op1=op1), reads, writes)

    def ts1(self, eng, out, in_, scalar, op, reads, writes):
        return self.P.add(eng, lambda e: e.tensor_single_scalar(out=out, in_=in_, scalar=scalar, op=op), reads, writes)

    def copy(self, eng, out, in_, reads, writes):
        return self.P.add(eng, lambda e: e.tensor_copy(out=out, in_=in_), reads, writes)

    def dma(self, eng, out, in_, reads, writes, lane):
        return self.P.add(eng, lambda e: e.dma_start(out=out, in_=in_), reads, writes, lane=lane)

    def rsqrt(self, out, in_, eps, reads, writes):
        self.act(out, in_, AF.Sqrt, reads, writes, scale=1.0, bias=eps)
        self.P.add("dve", lambda e: e.reciprocal(out=out, in_=out), writes, writes)

    def nxt(self, name, n):
        i = self.rot.get(name, 0)
        self.rot[name] = i + 1
        return i % n

    def setup_consts(self):
        A = self.A
        cst = self.inp("consts", [128, 256])
        self.ident_f = A.alloc((128,), F32, "ident_f")
        self.ident_b = A.alloc((128,), BF16, "ident_b")
        self.ones_b = A.alloc((128,), BF16, "ones_b")
        self.dma("sp", self.ident_f, cst[:, 0:128], [], ["ident_f"], "c0")
        self.dma("pool", self.ident_b, cst[:, 0:128], [], ["ident_b"], "c1")
        self.dma("pool", self.ones_b, cst[:, 128:256], [], ["ones_b"], "c1")
        self.par = []
        for l in range(DEPTH):
            p = A.alloc((NPAR,), F32, f"par{l}")
            self.dma("sp", p, self.inp(f"par{l}", [128, NPAR]), [], [f"par{l}"], "c0")
            self.par.append(p)

    def phase_inln(self, x_ap, xpre_ap, h_dst):
        A, P = self.A, self.P
        m0 = A.mark()
        xt = [A.alloc((1024,), F32, "xt") for _ in range(2)]
        xn = [A.alloc((1024,), F32, "xn") for _ in range(2)]
        st = [A.alloc((2, 6), F32, "st") for _ in range(2)]
        mv = [A.alloc((2,), F32, "mv") for _ in range(2)]
        rs = [A.alloc((1,), F32, "rs") for _ in range(2)]
        stage = [A.alloc((8, 128), F32, "stage") for _ in range(2)]
        par = self.par[0]
        subt = [(xpre_ap, 0, 16, 0)] + [(x_ap, s * 128, 128, NPRE + s * 128) for s in range(BLK // 128)]
        for i, (src, r0, n, toff) in enumerate(subt):
            k = i % 2
            self.dma("sp", xt[k][:n, :], src[r0:r0 + n, :], [], [("xt", k)], ("xt", k))
            for c in range(2):
                P.add("dve", lambda e, k=k, c=c, n=n: e.bn_stats(out=st[k][:n, c, :], in_=xt[k][:n, c * 512:(c + 1) * 512]),
                      [("xt", k)], [("st", k, c)])
            P.add("dve", lambda e, k=k, n=n: e.bn_aggr(out=mv[k][:n, :], in_=st[k][:n, :, :].rearrange("p a b -> p (a b)")),
                  [("st", k, 0), ("st", k, 1)], [("mv", k)])
            self.rsqrt(rs[k][:n, :], mv[k][:n, 1:2], LN_EPS, [("mv", k)], [("rs", k)])
            self.ts("dve", xn[k][:n, :], xt[k][:n, :], mv[k][:n, 0:1], rs[k][:n, 0:1], ALU.subtract, ALU.mult,
                    [("xt", k), ("mv", k), ("rs", k)], [("xn", k)])
            for half in range(2):
                bk = self.nxt("inln_bank", 2)
                ps = self.banks[bk]
                for j in range(4):
                    ft = half * 4 + j
                    self.tr(ps[:, j * 128:j * 128 + n], xn[k][:n, ft * 128:(ft + 1) * 128], self.ident_f[:n, :n],
                            [("xn", k), "ident_f"], [("bank", bk)])
                for j in range(4):
                    ft = half * 4 + j
                    self.act(stage[k][:, ft, :n], ps[:, j * 128:j * 128 + n], AF.Identity,
                             [("bank", bk), "par0"], [("stage", k)],
                             scale=par[:, PAR_LNIN_G + ft:PAR_LNIN_G + ft + 1], bias=par[:, PAR_LNIN_B + ft:PAR_LNIN_B + ft + 1])
            self.dma("sp", h_dst[:, :, toff:toff + n], stage[k][:, :, :n], [("stage", k)], [], ("stg", k))
        P.barrier()
        A.reset(m0)

    def load_hb(self, h_src):
        if not hasattr(self, "hb"):
            self.hb = self.A.alloc((NFT, T), BF16, "hb")
        for ti, (o, n) in enumerate(TILES):
            self.dma("pool", self.hb[:, :, o:o + n], h_src[:, :, o:o + n], [],
                     [("hb", ft, ti) for ft in range(NFT)], ("hbld", ti))

    def ln_alloc(self, ntile):
        A = self.A
        self.yp = [A.alloc((8, 512), F32, "yp") for _ in range(ntile)]
        self.sq = [A.alloc((512,), BF16, "sq") for _ in range(2)]
        self.yb = [A.alloc((512,), BF16, "yb") for _ in range(2)]
        self.mean = [A.alloc((512,), F32, "mean") for _ in range(ntile)]
        self.rstd = [A.alloc((512,), F32, "rstd") for _ in range(ntile)]
        self.nmr = [A.alloc((512,), F32, "nmr") for _ in range(ntile)]

    def ln_accum(self, li, fo, n, py, pb, cres):
        yp = self.yp[li]
        self.stt("dve", yp[:, fo, :n], py[:, :n], cres, yp[:, fo, :n], ALU.mult, ALU.add,
                 [("bank", pb), ("yp", li)], [("ypf", li, fo)])
        q = self.nxt("sq", 2)
        self.act(self.sq[q][:, :n], yp[:, fo, :n], AF.Square, [("ypf", li, fo)], [("sq", q)])
        self.act(self.yb[q][:, :n], yp[:, fo, :n], AF.Copy, [("ypf", li, fo)], [("yb", q)])
        bs, bq = 2 + 2 * li, 3 + 2 * li
        self.mm(self.banks[bs][:, :n], self.ones_b, self.yb[q][:, :n], fo == 0, fo == NFT - 1, ["ones_b", ("yb", q)], [("bank", bs)])
        self.mm(self.banks[bq][:, :n], self.ones_b, self.sq[q][:, :n], fo == 0, fo == NFT - 1, ["ones_b", ("sq", q)], [("bank", bq)])

    def stats_finish(self, bs, bq, n, inv, eps, mn, rd, nm, key):
        self.ts1("dve", mn[:, :n], self.banks[bs][:, :n], inv, ALU.mult, [("bank", bs)], [("mean", key)])
        self.tt("dve", nm[:, :n], mn[:, :n], mn[:, :n], ALU.mult, [("mean", key)], [("nmr", key)])
        self.stt("dve", rd[:, :n], self.banks[bq][:, :n], inv, nm[:, :n], ALU.mult, ALU.subtract,
                 [("bank", bq), ("nmr", key)], [("rstd", key)])
        self.rsqrt(rd[:, :n], rd[:, :n], eps, [("rstd", key)], [("rstd", key)])
        self.stt("dve", nm[:, :n], mn[:, :n], -1.0, rd[:, :n], ALU.mult, ALU.mult, [("mean", key), ("rstd", key)], [("nmr", key)])

    def ln_finish(self, l, lnidx, li, ti, h_dst):
        par = self.par[l]
        gcol = PAR_LN_G + lnidx * 8
        bcol = PAR_LN_B + lnidx * 8
        o, n = TILES[ti]
        yp, hb = self.yp[li], self.hb
        bs, bq = 2 + 2 * li, 3 + 2 * li
        mn, rd, nm = self.mean[li], self.rstd[li], self.nmr[li]
        self.stats_finish(bs, bq, n, 1.0 / D, EPS2, mn, rd, nm, li)
        for fo in range(NFT):
            self.tt("dve", yp[:, fo, :n], yp[:, fo, :n], rd[:, :n], ALU.mult, [("ypf", li, fo), ("rstd", li)], [("ypf", li, fo)])
            self.tt("dve", yp[:, fo, :n], yp[:, fo, :n], nm[:, :n], ALU.add, [("ypf", li, fo), ("nmr", li)], [("ypf", li, fo)])
            self.act(hb[:, fo, o:o + n], yp[:, fo, :n], AF.Identity, [("ypf", li, fo), f"par{l}"], [("hb", fo, ti)],
                     scale=par[:, gcol + fo:gcol + fo + 1], bias=par[:, bcol + fo:bcol + fo + 1])
            self.act(yp[:, fo, :n], yp[:, fo, :n], AF.Identity, [("ypf", li, fo), f"par{l}"], [("ypf", li, fo)],
                     scale=par[:, gcol + fo:gcol + fo + 1], bias=par[:, bcol + fo:bcol + fo + 1])
        self.dma("sp", h_dst[:, :, o:o + n], yp[:, :, :n], [("ypf", li, fo) for fo in range(NFT)] + [("yp", li)],
                 [("yp", li)], ("ypst", li))

    def phase_ffn(self, l, w13, w2, lnidx, h_src, h_dst):
        A, P = self.A, self.P
        m0 = A.mark()
        hid = A.alloc((NHT, 1040), BF16, "hid")
        w13s = [A.alloc((8, 2, 128), BF16, "w13s") for _ in range(3)]
        w2s = [A.alloc((NHT, 128), BF16, "w2s") for _ in range(2)]
        self.ln_alloc(3)
        sg = [A.alloc((512,), F32, "sg") for _ in range(2)]
        hb = self.hb
        w13v = w13.rearrange("(kt p) c -> p kt c", p=128)
        CRES = 0.5 / ALPHA
        for hi, tiles in enumerate(HALVES):
            hoff = HALF_OFF[hi]
            for li, ti in enumerate(tiles):
                o, n = TILES[ti]
                self.dma("sp", self.yp[li][:, :, :n], h_src[:, :, o:o + n], [], [("yp", li)], ("ypld", li))
            for m in range(NHT):
                mw = 128 if m < NHT - 1 else 64
                s = self.nxt("w13s", 3)
                self.dma("pool", w13s[s][:, :, 0, :mw], w13v[:, :, m * 128:m * 128 + mw], [], [("w13s", s)], ("w13", s))
                self.dma("pool", w13s[s][:, :, 1, :mw], w13v[:, :, DFF + m * 128:DFF + m * 128 + mw], [], [("w13s", s)], ("w13", s))
                for ti in tiles:
                    o, n = TILES[ti]
                    pb = self.nxt("upbank", 2) * 2
                    pa, pu = self.banks[pb], self.banks[pb + 1]
                    for kt in range(NFT):
                        self.mm(pa[:mw, :n], w13s[s][:, kt, 0, :mw], hb[:, kt, o:o + n], kt == 0, kt == NFT - 1,
                                [("w13s", s), ("hb", kt, ti)], [("bank", pb)])
                    for kt in range(NFT):
                        self.mm(pu[:mw, :n], w13s[s][:, kt, 1, :mw], hb[:, kt, o:o + n], kt == 0, kt == NFT - 1,
                                [("w13s", s), ("hb", kt, ti)], [("bank", pb + 1)])
                    g = self.nxt("sg", 2)
                    self.act(sg[g][:mw, :n], pa[:mw, :n], AF.Silu, [("bank", pb)], [("sg", g)])
                    self.tt("dve", hid[:mw, m, o - hoff:o - hoff + n], sg[g][:mw, :n], pu[:mw, :n], ALU.mult,
                            [("sg", g), ("bank", pb + 1)], [("hid", m, ti)])
            for fo in range(NFT):
                s = self.nxt("w2s", 2)
                self.dma("pool", w2s[s][:, 0:NHT - 1, :], w2[0:(NHT - 1) * 128, fo * 128:(fo + 1) * 128].rearrange("(kt p) c -> p kt c", p=128),
                         [], [("w2s", s)], ("w2", s))
                self.dma("pool", w2s[s][0:64, NHT - 1, :], w2[(NHT - 1) * 128:DFF, fo * 128:(fo + 1) * 128], [], [("w2s", s)], ("w2", s))
                for li, ti in enumerate(tiles):
                    o, n = TILES[ti]
                    pb = self.nxt("dnbank", 2)
                    py = self.banks[pb]
                    for m in range(NHT):
                        mw = 128 if m < NHT - 1 else 64
                        self.mm(py[:, :n], w2s[s][:mw, m, :], hid[:mw, m, o - hoff:o - hoff + n], m == 0, m == NHT - 1,
                                [("w2s", s), ("hid", m, ti)], [("bank", pb)])
                    self.ln_accum(li, fo, n, py, pb, CRES)
            for li, ti in enumerate(tiles):
                self.ln_finish(l, lnidx, li, ti, h_dst)
        P.barrier()
        A.reset(m0)

    def rope(self, ps, pbkey, rp, rkey, n, dst, dkey):
        a, b = self.ropeA, self.ropeB
        self.tt("dve", a[:, :n], ps[:, :n], rp[:, 0, :n], ALU.mult, [pbkey, rkey], ["ropeA"])
        self.tt("dve", b[0:64, :n], ps[64:128, :n], rp[64:128, 1, :n], ALU.mult, [pbkey, rkey], ["ropeB0"])
        self.tt("dve", b[64:128, :n], ps[0:64, :n], rp[0:64, 1, :n], ALU.mult, [pbkey, rkey], ["ropeB1"])
        self.tt("dve", dst, a[:, :n], b[:, :n], ALU.add, ["ropeA", "ropeB0", "ropeB1"], [dkey])

    def load_rope(self, rope_d, ti):
        o, n = TILES[ti]
        k = self.nxt("ropet", 2)
        self.dma("sp", self.ropet[k][:, :, :n], rope_d[:, :, o:o + n], [], [("ropet", k)], ("ropet", k))
        return self.ropet[k], ("ropet", k)

    def phase_kv(self, l, w_in, rope_d, vtab_d, send_d):
        A, P = self.A, self.P
        m0 = A.mark()
        hb = self.hb
        wk = A.alloc((8, 512), BF16, "wk")
        wv = A.alloc((8, 512), BF16, "wv")
        wh = A.alloc((8, 768), BF16, "wh")
        self.ropet = [A.alloc((2, 512), F32, "ropet") for _ in range(2)]
        self.ropeA = A.alloc((512,), F32, "ropeA")
        self.ropeB = A.alloc((512,), F32, "ropeB")
        vtab = A.alloc((17, 2, 4), F32, "vtab")
        kT = A.alloc((4, 512), BF16, "kT")
        vfull = [A.alloc((4, 128), BF16, "vfull") for _ in range(2)]
        kTok = [A.alloc((4, 128), BF16, "kTok") for _ in range(2)]
        send = A.alloc((640,), F32, "send")
        sgm = A.alloc((2, 32), F32, "sgm")
        w_inv = w_in.rearrange("(kt p) c -> p kt c", p=128)
        self.dma("pool", wk, w_inv[:, :, 1280:1792], [], ["wk"], "wk")
        self.dma("pool", wv, w_inv[:, :, 1792:2304], [], ["wv"], "wv")
        self.dma("pool", wh, w_inv[:, :, 0:768], [], ["wh"], "wh")
        self.dma("sp", vtab, vtab_d.rearrange("p (a b c) -> p a b c", a=17, b=2), [], ["vtab"], "vtab")
        b7 = self.banks[7][:, :].bitcast(BF16)
        SB = 4
        sidx = 0
        for ti, (o, n) in enumerate(TILES):
            rp, rkey = self.load_rope(rope_d, ti)
            for h in range(NHEAD):
                pb = self.nxt("kvbank", 2)
                ps = self.banks[pb]
                for kt in range(NFT):
                    self.mm(ps[:, :n], wk[:, kt, h * 128:(h + 1) * 128], hb[:, kt, o:o + n], kt == 0, kt == NFT - 1,
                            ["wk", ("hb", kt, ti)], [("bank", pb)])
                self.rope(ps, ("bank", pb), rp, rkey, n, kT[:, h, :n], ("kT", h))
            nsub = max(1, n // 128)
            for sub in range(nsub):
                ns = min(n, 128)
                c0 = o + sub * 128
                pb = 2 + self.nxt("kvbank2", 2)
                ps = self.banks[pb]
                for kt in range(NFT):
                    self.mm(ps[:ns, :], hb[:, kt, c0:c0 + ns], wv[:, kt, :], kt == 0, kt == NFT - 1,
                            ["wv", ("hb", kt, ti)], [("bank", pb)])
                k2 = self.nxt("vfull", 2)
                self.tt("dve", vfull[k2][:ns, :, :], ps[:ns, :].rearrange("p (h e) -> p h e", h=4),
                        vtab[:ns, sidx, 1, :].unsqueeze(2).to_broadcast([ns, 4, 128]), ALU.mult,
                        [("bank", pb), "vtab"], [("vfull", k2)])
                for h in range(NHEAD):
                    self.tr(b7[:ns, h * 128:(h + 1) * 128], kT[:, h, sub * 128:sub * 128 + ns], self.ident_b,
                            [("kT", h), "ident_b"], [("bank", 7)])
                self.act(kTok[k2][:ns, :, :], b7[:ns, 0:512].rearrange("p (h e) -> p h e", h=4), AF.Copy, [("bank", 7)], [("kTok", k2)])
                for h in range(NHEAD):
                    self.mm(self.banks[SB][:, h * 128:(h + 1) * 128], kTok[k2][:ns, h, :], vfull[k2][:ns, h, :], sidx == 0, sidx == 16,
                            [("kTok", k2), ("vfull", k2)], [("bank", SB)])
                sidx += 1
        HB = 5
        ps = self.banks[HB]
        for mt in range(6):
            for kt in range(NFT):
                self.mm(ps[:, mt * 32:(mt + 1) * 32], wh[:, kt, mt * 128:(mt + 1) * 128], hb[:, kt, T - 32:T], kt == 0, kt == NFT - 1,
                        ["wh", ("hb", kt, 4)], [("bank", HB)])
        self.act(send[:, 0:512], self.banks[SB][:, :], AF.Copy, [("bank", SB)], ["send_s"])
        self.act(sgm[:, :, :], ps[:, 128:192].rearrange("p (a b) -> p a b", a=2), AF.Sigmoid, [("bank", HB)], ["sgm"])
        self.tt("dve", send[:, 512:576].rearrange("p (a b) -> p a b", a=2), sgm[:, :, :], ps[:, 64:128].rearrange("p (a b) -> p a b", a=2),
                ALU.mult, ["sgm", ("bank", HB)], ["send_u"])
        self.act(send[:, 576:640], ps[:, 0:64], AF.Copy, [("bank", HB)], ["send_p"])
        self.dma("sp", send_d[:, :], send, ["send_s", "send_u", "send_p"], [], "send")
        P.barrier()
        A.reset(m0)

    def phase_mix(self, l, w_in, pool_w, conv_pw, w_out, rope_d, vtab_d, tab_d, sprev_srcs, hprev_src, h_src, h_dst):
        A, P = self.A, self.P
        m0 = A.mark()
        hb = self.hb
        par = self.par[l]
        tab = A.alloc((NTAB,), F32, "tab")
        vtab = A.alloc((17, 2, 4), F32, "vtab")
        self.ropet = [A.alloc((2, 512), F32, "ropet") for _ in range(2)]
        self.ropeA = A.alloc((512,), F32, "ropeA")
        self.ropeB = A.alloc((512,), F32, "ropeB")
        sprev = A.alloc((3, 512), F32, "sprev")
        hprev = A.alloc((128,), F32, "hprev")
        S = A.alloc((4, 128), F32, "S")
        Stmp = A.alloc((4, 128), F32, "Stmp")
        Sb = A.alloc((4, 128), BF16, "Sb")
        wsl = [A.alloc((8, 128), BF16, "wsl") for _ in range(4)]
        wv = A.alloc((8, 512), BF16, "wv")
        wbd = A.alloc((2, 128), BF16, "wbd")
        wpw = A.alloc((2, 256), BF16, "wpw")
        XP = A.alloc((2, 15 + 512), F32, "XP")
        S2e = A.alloc((526,), F32, "S2e")
        S4e = A.alloc((524,), F32, "S4e")
        S8e = A.alloc((520,), F32, "S8e")
        Mb = A.alloc((512,), F32, "Mb")
        ypool = A.alloc((2, 512), BF16, "ypool")
        U = A.alloc((2, 30 + 512), F32, "U")
        Utmp = A.alloc((2, 30), F32, "Utmp")
        acc = A.alloc((2, 512), F32, "acc")
        sgt = A.alloc((512,), F32, "sgt")
        cact = A.alloc((2, 512), BF16, "cact")
        qrot = A.alloc((512,), F32, "qrot")
        qT = A.alloc((4, 512), BF16, "qT")
        qdec = A.alloc((4, 512), BF16, "qdec")
        kT = A.alloc((4, 512), BF16, "kT")
        sgate = A.alloc((4, 512), F32, "sgate")
        vbf = A.alloc((4, 4, 128), BF16, "vbf")
        vdec = A.alloc((4, 4, 128), BF16, "vdec")
        kTok = A.alloc((4, 4, 128), BF16, "kTok")
        sdT = A.alloc((4, 4, 128), BF16, "sdT")
        ycat = A.alloc((8, 512), BF16, "ycat")
        t1 = A.alloc((512,), F32, "t1")
        self.ln_alloc(1)
        b7 = self.banks[7][:, :].bitcast(BF16)
        w_inv = w_in.rearrange("(kt p) c -> p kt c", p=128)
        w_outv = w_out.rearrange("(kt p) c -> p kt c", p=128)

        def tcol(c0, w):
            return tab[:, c0:c0 + w]

        self.dma("sp", tab, tab_d[:, :], [], ["tab"], "tab")
        self.dma("sp", vtab, vtab_d.rearrange("p (a b c) -> p a b c", a=17, b=2), [], ["vtab"], "vtab")
        for i in range(3):
            self.dma("sp", sprev[:, i, :], sprev_srcs[i], [], ["sprev"], "sprev")
        self.dma("sp", hprev, hprev_src, [], ["hprev"], "hprev")
        self.dma("pool", wv, w_inv[:, :, 1792:2304], [], ["wv"], "wv")
        P.add("dve", lambda e: e.memset(wbd[:, :, :], 0.0), [], ["wbd"])
        for tl in range(2):
            for a in range(2):
                self.dma("pool", wbd[64 * a:64 * a + 64, tl, 64 * a:64 * a + 64], pool_w[2 * tl + a, :, :], ["wbd"], [("wbd2", tl, a)], "wbd")
        self.dma("pool", wpw, conv_pw.rearrange("(kt p) c -> p kt c", p=128), [], ["wpw"], "wpw")
        P.add("dve", lambda e: e.memset(XP[:, :, 0:15], 0.0), [], ["XPh"])
        P.add("dve", lambda e: e.memset(U[:, :, 0:30], 0.0), [], ["Uh"])
        for i in range(3):
            cb = tab[:, TB_COEF + 4 * i:TB_COEF + 4 * i + 4].unsqueeze(2).to_broadcast([128, 4, 128])
            src = sprev[:, i, :].rearrange("p (h e) -> p h e", h=4)
            if i == 0:
                self.tt("dve", S[:, :, :], src, cb, ALU.mult, ["sprev", "tab"], ["S"])
            else:
                self.tt("dve", Stmp[:, :, :], src, cb, ALU.mult, ["sprev", "tab"], ["Stmp"])
                self.tt("dve", S[:, :, :], S[:, :, :], Stmp[:, :, :], ALU.add, ["S", "Stmp"], ["S"])
        self.act(Sb[:, :, :], S[:, :, :], AF.Copy, ["S"], ["Sb"])
        dtab = tab[:, TB_DTAB:TB_DTAB + 512].rearrange("p (h e) -> p h e", h=4)
        ff = tab[:, TB_FLAG:TB_FLAG + 1]
        nf = tab[:, TB_FLAG + 1:TB_FLAG + 2]

        sidx = 0
        STOP = getattr(self, "mix_stop", 99)
        NT = getattr(self, "mix_tiles", len(TILES))
        for ti, (o, n) in enumerate(TILES[:NT]):
            rp, rkey = self.load_rope(rope_d, ti)
            self.dma("sp", self.yp[0][:, :, :n], h_src[:, :, o:o + n], [], [("yp", 0)], ("ypld", 0))
            nsub = max(1, n // 128)
            ns = min(n, 128)

            def proj(col0):
                s = self.nxt("wsl", 4)
                self.dma("pool", wsl[s], w_inv[:, :, col0:col0 + 128], [], [("wsl", s)], ("wsl", s))
                pb = self.nxt("pjbank", 3)
                ps = self.banks[pb]
                for kt in range(NFT):
                    self.mm(ps[:, :n], wsl[s][:, kt, :], hb[:, kt, o:o + n], kt == 0, kt == NFT - 1,
                            [("wsl", s), ("hb", kt, ti)], [("bank", pb)])
                return ps, ("bank", pb)

            for tl in range(2):
                ps, pk = proj(0 + tl * 128)
                self.act(XP[:, tl, 15:15 + n], ps[:, :n], AF.Copy, [pk], [("XP", tl)])
                x = XP[:, tl, :]
                hk = [("XP", tl), "XPh"]
                self.tt("dve", S2e[:, 0:n + 14], x[:, 1:15 + n], x[:, 0:14 + n], ALU.add, hk, ["S2e"])
                self.tt("dve", S4e[:, 0:n + 12], S2e[:, 2:n + 14], S2e[:, 0:n + 12], ALU.add, ["S2e"], ["S4e"])
                self.tt("dve", S8e[:, 0:n + 8], S4e[:, 4:n + 12], S4e[:, 0:n + 8], ALU.add, ["S4e"], ["S8e"])
                pc = lambda w: tab[:, TB_PC + tl * 4 + w:TB_PC + tl * 4 + w + 1]
                self.ts1("dve", Mb[:, :n], S2e[:, 14:14 + n], pc(0), ALU.mult, ["S2e", "tab"], ["Mb"])
                self.stt("dve", Mb[:, :n], S4e[:, 12:12 + n], pc(1), Mb[:, :n], ALU.mult, ALU.add, ["S4e", "Mb", "tab"], ["Mb"])
                self.stt("dve", Mb[:, :n], S8e[:, 8:8 + n], pc(2), Mb[:, :n], ALU.mult, ALU.add, ["S8e", "Mb", "tab"], ["Mb"])
                self.stt("dve", Mb[:, :n], S8e[:, 8:8 + n], pc(3), Mb[:, :n], ALU.mult, ALU.add, ["S8e", "Mb", "tab"], ["Mb"])
                self.stt("dve", Mb[:, :n], S8e[:, 0:n], pc(3), Mb[:, :n], ALU.mult, ALU.add, ["S8e", "Mb", "tab"], ["Mb"])
                if ti == 0:
                    self.tt("dve", Mb[:, :n], Mb[:, :n], tab[:, TB_PCORR + tl * 16:TB_PCORR + tl * 16 + 16], ALU.mult, ["Mb", "tab"], ["Mb"])
                self.tt("dve", ypool[:, tl, :n], Mb[:, :n], x[:, 15:15 + n], ALU.subtract, ["Mb", ("XP", tl)], [("ypool", tl)])
                self.copy("dve", XP[:, tl, 0:15], XP[:, tl, n:n + 15], [("XP", tl)], ["XPh", ("XP", tl)])
                if ti == 0:
                    self.ts1("dve", XP[:, tl, 0:15], XP[:, tl, 0:15], ff, ALU.mult, ["XPh", ("XP", tl), "tab"], ["XPh", ("XP", tl)])
                    self.stt("dve", XP[:, tl, 0:15], hprev[:, 64 + tl * 32 + 17:64 + tl * 32 + 32], nf, XP[:, tl, 0:15], ALU.mult, ALU.add,
                             ["hprev", "tab", "XPh", ("XP", tl)], ["XPh", ("XP", tl)])
                pb = 6
                self.mm(self.banks[pb][:, :n], wbd[:, tl, :], ypool[:, tl, :n], True, True,
                        [("wbd2", tl, 0), ("wbd2", tl, 1), "wbd", ("ypool", tl)], [("bank", pb)])
                self.act(ycat[:, tl, :n], self.banks[pb][:, :n], AF.Copy, [("bank", pb), f"par{l}"], [("ycat", tl)],
                         scale=par[:, PAR_POOL_SCALE + tl:PAR_POOL_SCALE + tl + 1])
            if STOP <= 1:
                continue
            for tl in range(2):
                psg, pkg = proj(512 + tl * 128)
                self.act(sgt[:, :n], psg[:, :n], AF.Sigmoid, [pkg], ["sgt"])
                psa, pka = proj(256 + tl * 128)
                self.tt("dve", U[:, tl, 30:30 + n], sgt[:, :n], psa[:, :n], ALU.mult, ["sgt", pka], [("U", tl)])
                wc = lambda j: par[:, PAR_CONV_W + tl * 31 + j:PAR_CONV_W + tl * 31 + j + 1]
                uk = [("U", tl), "Uh", f"par{l}"]
                self.ts("dve", acc[:, tl, :n], U[:, tl, 0:n], wc(0), par[:, PAR_CONV_DB + tl:PAR_CONV_DB + tl + 1], ALU.mult, ALU.add,
                        uk, [("acc", tl)])
                for j in range(1, 31):
                    self.stt("dve", acc[:, tl, :n], U[:, tl, j:j + n], wc(j), acc[:, tl, :n], ALU.mult, ALU.add, uk + [("acc", tl)], [("acc", tl)])
                if n >= 30:
                    self.copy("dve", U[:, tl, 0:30], U[:, tl, n:n + 30], [("U", tl)], ["Uh", ("U", tl)])
                else:
                    self.copy("dve", Utmp[:, tl, :], U[:, tl, n:n + 30], [("U", tl), "Uh"], [("Utmp", tl)])
                    self.copy("dve", U[:, tl, 0:30], Utmp[:, tl, :], [("Utmp", tl)], ["Uh", ("U", tl)])
                if ti == 0:
                    self.ts1("dve", U[:, tl, 0:30], U[:, tl, 0:30], ff, ALU.mult, ["Uh", ("U", tl), "tab"], ["Uh", ("U", tl)])
                    self.stt("dve", U[:, tl, 0:30], hprev[:, tl * 32 + 2:tl * 32 + 32], nf, U[:, tl, 0:30], ALU.mult, ALU.add,
                             ["hprev", "tab", "Uh", ("U", tl)], ["Uh", ("U", tl)])
            for tl in range(2):
                q = self.nxt("sq", 2)
                self.act(self.sq[q][:, :n], acc[:, tl, :n], AF.Square, [("acc", tl)], [("sq", q)])
                self.act(self.yb[q][:, :n], acc[:, tl, :n], AF.Copy, [("acc", tl)], [("yb", q)])
                self.mm(self.banks[4][:, :n], self.ones_b, self.yb[q][:, :n], tl == 0, tl == 1, ["ones_b", ("yb", q)], [("bank", 4)])
                self.mm(self.banks[5][:, :n], self.ones_b, self.sq[q][:, :n], tl == 0, tl == 1, ["ones_b", ("sq", q)], [("bank", 5)])
            mn, rd, nm = self.mean[0], self.rstd[0], self.nmr[0]
            self.stats_finish(4, 5, n, 1.0 / 256, LN_EPS, mn, rd, nm, 0)
            for tl in range(2):
                self.tt("dve", acc[:, tl, :n], acc[:, tl, :n], rd[:, :n], ALU.mult, [("acc", tl), ("rstd", 0)], [("acc", tl)])
                self.tt("dve", acc[:, tl, :n], acc[:, tl, :n], nm[:, :n], ALU.add, [("acc", tl), ("nmr", 0)], [("acc", tl)])
                self.act(cact[:, tl, :n], acc[:, tl, :n], AF.Silu, [("acc", tl), f"par{l}"], [("cact", tl)],
                         scale=par[:, PAR_CONV_LN_G + tl:PAR_CONV_LN_G + tl + 1], bias=par[:, PAR_CONV_LN_B + tl:PAR_CONV_LN_B + tl + 1])
            for mt in range(2):
                pb = 6
                for kt in range(2):
                    self.mm(self.banks[pb][:, :n], wpw[:, kt, mt * 128:(mt + 1) * 128], cact[:, kt, :n], kt == 0, kt == 1,
                            ["wpw", ("cact", kt)], [("bank", pb)])
                self.act(ycat[:, 2 + mt, :n], self.banks[pb][:, :n], AF.Copy, [("bank", pb)], [("ycat", 2 + mt)])
            if STOP <= 2:
                continue
            for h in range(NHEAD):
                ps, pk = proj(768 + h * 128)
                self.rope(ps, pk, rp, rkey, n, qrot[:, :n], "qrot")
                self.act(qT[:, h, :n], qrot[:, :n], AF.Copy, ["qrot"], [("qT", h)])
                if n >= 64:
                    self.tt("dve", qdec[:, h, :n].rearrange("p (c i) -> p c i", i=64), qrot[:, :n].rearrange("p (c i) -> p c i", i=64),
                            tab[:, TB_QD + h * 64:TB_QD + h * 64 + 64].unsqueeze(1).to_broadcast([128, n // 64, 64]), ALU.mult,
                            ["qrot", "tab"], [("qdec", h)])
            if STOP <= 2.2:
                continue
            for h in range(NHEAD):
                ps, pk = proj(1280 + h * 128)
                self.rope(ps, pk, rp, rkey, n, kT[:, h, :n], ("kT", h))
            if STOP <= 2.4:
                continue
            for h in range(NHEAD):
                ps, pk = proj(2304 + h * 128)
                self.act(sgate[:, h, :n], ps[:, :n], AF.Silu, [pk], [("sgate", h)])
            if STOP <= 2.6:
                continue
            for sub in range(nsub):
                c0 = o + sub * 128
                pb = 3
                ps = self.banks[pb]
                for kt in range(NFT):
                    self.mm(ps[:ns, :], hb[:, kt, c0:c0 + ns], wv[:, kt, :], kt == 0, kt == NFT - 1,
                            ["wv", ("hb", kt, ti)], [("bank", pb)])
                psv = ps[:ns, :].rearrange("p (h e) -> p h e", h=4)
                self.act(vbf[:ns, sub, :, :], psv, AF.Copy, [("bank", pb)], [("vbf", sub)])
                self.tt("dve", vdec[:ns, sub, :, :], psv, vtab[:ns, sidx + sub, 0, :].unsqueeze(2).to_broadcast([ns, 4, 128]), ALU.mult,
                        [("bank", pb), "vtab"], [("vdec", sub)])
                if STOP <= 2.8:
                    continue
                for h in range(NHEAD):
                    self.tr(b7[:ns, h * 128:(h + 1) * 128], kT[:, h, sub * 128:sub * 128 + ns], self.ident_b,
                            [("kT", h), "ident_b"], [("bank", 7)])
                self.act(kTok[:ns, sub, :, :], b7[:ns, 0:512].rearrange("p (h e) -> p h e", h=4), AF.Copy, [("bank", 7)], [("kTok", sub)])
            if STOP <= 3:
                sidx += nsub
                continue
            for h in range(NHEAD):
                pb = 4 + self.nxt("scbank", 2)
                ps = self.banks[pb]
                for sub in range(nsub):
                    self.mm(ps[:ns, sub * 128:sub * 128 + ns], kT[:, h, sub * 128:sub * 128 + ns], qT[:, h, sub * 128:sub * 128 + ns], True, True,
                            [("kT", h), ("qT", h)], [("bank", pb)])
                dm = tab[:ns, TB_DM2 + h * 128:TB_DM2 + h * 128 + ns]
                self.tt("dve", sdT[:ns, h, 0:nsub, :ns], ps[:ns, 0:nsub * 128].rearrange("p (s i) -> p s i", i=128)[:, :, :ns],
                        dm.unsqueeze(1).to_broadcast([ns, nsub, ns]), ALU.mult, [("bank", pb), "tab"], [("sdT", h)])
            if STOP <= 4:
                sidx += nsub
                continue
            for sub in range(nsub):
                for h in range(NHEAD):
                    ob = self.banks[h]
                    self.mm(ob[:, sub * 128:sub * 128 + ns], vbf[:ns, sub, h, :], sdT[:ns, h, sub, :ns], True, n < 64,
                            [("vbf", sub), ("sdT", h)], [("bank", h)])
                nch = max(1, ns // 64)
                for cc in range(nch):
                    cw = min(ns, 64)
                    r0 = cc * 64
                    if n >= 64:
                        for h in range(NHEAD):
                            ob = self.banks[h]
                            cs = sub * 128 + cc * 64
                            self.mm(ob[:, cs:cs + 64], Sb[:, h, :], qdec[:, h, cs:cs + 64], False, True,
                                    ["Sb", ("qdec", h)], [("bank", h)])
                    for h in range(NHEAD):
                        self.mm(self.banks[6][:, h * 128:(h + 1) * 128], kTok[r0:r0 + cw, sub, h, :], vdec[r0:r0 + cw, sub, h, :], True, True,
                                [("kTok", sub), ("vdec", sub)], [("bank", 6)])
                    if n >= 64:
                        self.tt("dve", S[:, :, :], S[:, :, :], dtab, ALU.mult, ["S", "tab"], ["S"])
                    self.tt("dve", S[:, :, :], S[:, :, :], self.banks[6][:, :].rearrange("p (h e) -> p h e", h=4), ALU.add,
                            ["S", ("bank", 6)], ["S"])
                    self.act(Sb[:, :, :], S[:, :, :], AF.Copy, ["S"], ["Sb"])
            if STOP <= 5:
                sidx += nsub
                continue
            for h in range(NHEAD):
                ob = self.banks[h]
                q = self.nxt("sq", 2)
                self.act(self.sq[q][:, :n], ob[:, :n], AF.Square, [("bank", h)], [("sq", q)])
                self.act(self.yb[q][:, :n], ob[:, :n], AF.Copy, [("bank", h)], [("yb", q)])
                self.mm(self.banks[4][:, :n], self.ones_b, self.yb[q][:, :n], True, True, ["ones_b", ("yb", q)], [("bank", 4)])
                self.mm(self.banks[5][:, :n], self.ones_b, self.sq[q][:, :n], True, True, ["ones_b", ("sq", q)], [("bank", 5)])
                self.stats_finish(4, 5, n, 1.0 / 128, LN_EPS, mn, rd, nm, 0)
                self.tt("dve", t1[:, :n], ob[:, :n], rd[:, :n], ALU.mult, [("bank", h), ("rstd", 0)], ["t1"])
                self.tt("dve", t1[:, :n], t1[:, :n], nm[:, :n], ALU.add, ["t1", ("nmr", 0)], ["t1"])
                self.stt("dve", ycat[:, 4 + h, :n], t1[:, :n], par[:, PAR_GN_G + h:PAR_GN_G + h + 1], sgate[:, h, :n], ALU.mult, ALU.mult,
                         ["t1", ("sgate", h), f"par{l}"], [("ycat", 4 + h)])
            if STOP <= 6:
                sidx += nsub
                continue
            for fo in range(NFT):
                s = self.nxt("wsl", 4)
                self.dma("pool", wsl[s], w_outv[:, :, fo * 128:(fo + 1) * 128], [], [("wsl", s)], ("wsl", s))
                pb = self.nxt("dnbank", 2)
                py = self.banks[pb]
                for kt in range(NFT):
                    self.mm(py[:, :n], wsl[s][:, kt, :], ycat[:, kt, :n], kt == 0, kt == NFT - 1,
                            [("wsl", s), ("ycat", kt)], [("bank", pb)])
                self.ln_accum(0, fo, n, py, pb, 1.0 / ALPHA)
            self.ln_finish(l, 1, 0, ti, h_dst)
            sidx += nsub
        P.barrier()
        A.reset(m0)

    def phase_final(self, h_src, out_d):
        A, P = self.A, self.P
        m0 = A.mark()
        xin = [A.alloc((8, 128), F32, "xin") for _ in range(2)]
        xo = [A.alloc((1024,), F32, "xo") for _ in range(2)]
        for s in range(BLK // 128):
            k = s % 2
            c0 = NPRE + s * 128
            self.dma("sp", xin[k], h_src[:, :, c0:c0 + 128], [], [("xin", k)], ("xin", k))
            for half in range(2):
                bk = self.nxt("finbank", 2)
                ps = self.banks[bk]
                for j in range(4):
                    ft = half * 4 + j
                    self.tr(ps[:, j * 128:(j + 1) * 128], xin[k][:, ft, :], self.ident_f, [("xin", k), "ident_f"], [("bank", bk)])
                self.act(xo[k][:, half * 512:(half + 1) * 512], ps[:, :], AF.Copy, [("bank", bk)], [("xo", k, half)])
            self.dma("sp", out_d[s * 128:(s + 1) * 128, :], xo[k], [("xo", k, 0), ("xo", k, 1)], [("xo", k, 0), ("xo", k, 1)], ("xo", k))
        P.barrier()
        A.reset(m0)


PAR_LN_G = 0
PAR_LN_B = 24
PAR_LNIN_G = 48
PAR_LNIN_B = 56
PAR_POOL_SCALE = 64
PAR_CONV_DB = 66
PAR_CONV_LN_G = 68
PAR_CONV_LN_B = 70
PAR_GN_G = 72
PAR_CONV_W = 76
NPAR = 138

TB_DM2 = 0
TB_QD = 512
TB_DTAB = 768
TB_COEF = 1280
TB_FLAG = 1292
TB_PC = 1294
TB_PCORR = 1302
NTAB = 1334


def pack_params(inp, l):
    p = np.zeros((128, NPAR), np.float32)
    for i in range(3):
        p[:, PAR_LN_G + 8 * i:PAR_LN_G + 8 * i + 8] = inp["ln_g"][l, i].reshape(8, 128).T
        p[:, PAR_LN_B + 8 * i:PAR_LN_B + 8 * i + 8] = inp["ln_b"][l, i].reshape(8, 128).T
    p[:, PAR_LNIN_G:PAR_LNIN_G + 8] = inp["ln_in_g"].reshape(8, 128).T
    p[:, PAR_LNIN_B:PAR_LNIN_B + 8] = inp["ln_in_b"].reshape(8, 128).T
    p[:, PAR_POOL_SCALE:PAR_POOL_SCALE + 2] = inp["pool_scale"][l].reshape(2, 128).T
    p[:, PAR_CONV_DB:PAR_CONV_DB + 2] = inp["conv_db"][l].reshape(2, 128).T
    p[:, PAR_CONV_LN_G:PAR_CONV_LN_G + 2] = inp["conv_ln_g"][l].reshape(2, 128).T
    p[:, PAR_CONV_LN_B:PAR_CONV_LN_B + 2] = inp["conv_ln_b"][l].reshape(2, 128).T
    p[:, PAR_GN_G:PAR_GN_G + 4] = inp["ret_gn_g"][l].reshape(4, 128).T
    cw = inp["conv_dw"][l]
    for tl in range(2):
        p[:, PAR_CONV_W + tl * 31:PAR_CONV_W + tl * 31 + 31] = cw[:, tl * 128:(tl + 1) * 128].T
    return p


def consts_arr():
    c = np.zeros((128, 256), np.float32)
    c[:, 0:128] = np.eye(128, dtype=np.float32)
    c[:, 128:256] = 1.0
    return c


def make_tables(jj):
    first = 1.0 if jj == 0 else 0.0
    g = np.array(GAMMAS, np.float64)
    pos = np.concatenate([np.arange(NPRE), NPRE + BLK * jj + np.arange(BLK)]).astype(np.float32)
    inv_freq = (np.float32(10000.0) ** (-np.arange(0, DH, 2, dtype=np.float32) / np.float32(DH))).astype(np.float32)
    ang = (pos[:, None] * inv_freq[None, :]).astype(np.float32)
    cos, sin = np.cos(ang).T, np.sin(ang).T
    rope = np.zeros((128, 2, T), np.float32)
    rope[0:64, 0], rope[64:128, 0] = cos, cos
    rope[0:64, 1], rope[64:128, 1] = sin, -sin
    vtab = np.zeros((128, 17, 2, 4), np.float64)
    ip = np.arange(16)
    for h in range(4):
        vtab[:16, 0, 0, h] = first * g[h] ** (15 - ip)
        vtab[:16, 0, 1, h] = first * g[h] ** (BLK + 15 - ip)
        for s in range(1, 17):
            nidx = (s - 1) * 128 + np.arange(128)
            vtab[:, s, 0, h] = g[h] ** (63 - (nidx % 64))
            vtab[:, s, 1, h] = g[h] ** (BLK - 1 - nidx)
    tab = np.zeros((128, NTAB), np.float64)
    j = np.arange(128)[:, None]
    i = np.arange(128)[None, :]
    same = (j // 64) == (i // 64)
    for h in range(4):
        tab[:, TB_DM2 + h * 128:TB_DM2 + (h + 1) * 128] = np.where(same, g[h] ** np.abs(i - j), 0.0) * QSCALE
        tab[:, TB_QD + h * 64:TB_QD + (h + 1) * 64] = (g[h] ** (np.arange(64) + 1.0))[None, :] * QSCALE
        tab[:, TB_DTAB + h * 128:TB_DTAB + (h + 1) * 128] = g[h] ** 64
        for s in range(3):
            tab[:, TB_COEF + 4 * s + h] = (g[h] ** (BLK * (jj - 1 - s))) if s < jj else 0.0
    tab[:, TB_FLAG] = first
    tab[:, TB_FLAG + 1] = 1.0 - first
    wins = (2, 4, 8, 16)
    for tl in range(2):
        for p in range(128):
            grp = (tl * 128 + p) // 64
            w = wins[grp]
            tab[p, TB_PC + tl * 4 + grp] = 1.0 / w
            tt_ = np.arange(16)
            tab[p, TB_PCORR + tl * 16:TB_PCORR + tl * 16 + 16] = w / np.minimum(tt_ + 1, w)
    return rope, vtab.reshape(128, 136).astype(np.float32), tab.astype(np.float32)


def _decl_common(B):
    B.setup_consts()
    rope = B.inp("rope", [128, 2, T])
    vtab = B.inp("vtab", [128, 136])
    tab = B.inp("tab", [128, NTAB])
    return rope, vtab, tab


def _w(B, l):
    return dict(
        ffn1_w13=B.inp(f"ffn1_w13_{l}", [D, 2 * DFF]), ffn1_w2=B.inp(f"ffn1_w2_{l}", [DFF, D]),
        ffn2_w13=B.inp(f"ffn2_w13_{l}", [D, 2 * DFF]), ffn2_w2=B.inp(f"ffn2_w2_{l}", [DFF, D]),
        w_in=B.inp(f"w_in_{l}", [D, DIN]), w_out=B.inp(f"w_out_{l}", [D, D]),
        pool_w=B.inp(f"pool_w_{l}", [4, 64, 64]), conv_pw=B.inp(f"conv_pw_{l}", [256, 256]))


def build_launch(kind):
    nc = bass.Bass("TRN2", target_bir_lowering=False)
    B = Builder(nc)
    if kind == 0:
        x = B.inp("x", [BLK, D])
        xpre = B.inp("xpre", [NPRE, D])
        rope, vtab, tab = _decl_common(B)
        w13, w2, w_in = B.inp("ffn1_w13_0", [D, 2 * DFF]), B.inp("ffn1_w2_0", [DFF, D]), B.inp("w_in_0", [D, DIN])
        h0 = B.scratch("h0", [128, NFT, T])
        h1 = B.outp("h1", [128, NFT, T])
        send = B.outp("send", [128, 640])
        B.phase_inln(x, xpre, h0)
        B.load_hb(h0)
        B.phase_ffn(0, w13, w2, 0, h0, h1)
        B.phase_kv(0, w_in, rope, vtab, send)
    else:
        l = kind - 1
        hin = B.inp("hin", [128, NFT, T])
        sprev = B.inp("sprev", [128, 3, 640])
        hprev = B.inp("hprev", [128, 128])
        rope, vtab, tab = _decl_common(B)
        w_in, w_out = B.inp(f"w_in_{l}", [D, DIN]), B.inp(f"w_out_{l}", [D, D])
        pool_w, conv_pw = B.inp(f"pool_w_{l}", [4, 64, 64]), B.inp(f"conv_pw_{l}", [256, 256])
        f2a, f2b = B.inp(f"ffn2_w13_{l}", [D, 2 * DFF]), B.inp(f"ffn2_w2_{l}", [DFF, D])
        hA = B.scratch("hA", [128, NFT, T])
        hB = B.scratch("hB", [128, NFT, T])
        B.load_hb(hin)
        B.phase_mix(l, w_in, pool_w, conv_pw, w_out, rope, vtab, tab, [sprev[:, i, 0:512] for i in range(3)], hprev[:, :], hin, hA)
        B.phase_ffn(l, f2a, f2b, 2, hA, hB)
        if l + 1 < DEPTH:
            w13, w2, w_in2 = B.inp(f"ffn1_w13_{l + 1}", [D, 2 * DFF]), B.inp(f"ffn1_w2_{l + 1}", [DFF, D]), B.inp(f"w_in_{l + 1}", [D, DIN])
            h1 = B.outp("h1", [128, NFT, T])
            send = B.outp("send", [128, 640])
            B.phase_ffn(l + 1, w13, w2, 0, hB, h1)
            B.phase_kv(l + 1, w_in2, rope, vtab, send)
        else:
            out = B.outp("out", [BLK, D])
            B.phase_final(hB, out)
    B.P.emit(nc)
    return nc, B


def build_mix_debug(l, stop, ntiles):
    nc = bass.Bass("TRN2", target_bir_lowering=False)
    B = Builder(nc)
    B.mix_stop, B.mix_tiles = stop, ntiles
    hin = B.inp("hin", [128, NFT, T])
    sprev = B.inp("sprev", [128, 3, 640])
    hprev = B.inp("hprev", [128, 128])
    rope, vtab, tab = _decl_common(B)
    w_in, w_out = B.inp(f"w_in_{l}", [D, DIN]), B.inp(f"w_out_{l}", [D, D])
    pool_w, conv_pw = B.inp(f"pool_w_{l}", [4, 64, 64]), B.inp(f"conv_pw_{l}", [256, 256])
    hA = B.outp("h1", [128, NFT, T])
    B.load_hb(hin)
    B.phase_mix(l, w_in, pool_w, conv_pw, w_out, rope, vtab, tab, [sprev[:, i, 0:512] for i in range(3)], hprev[:, :], hin, hA)
    B.P.emit(nc)
    return nc, B


def build_fused():
    nc = bass.Bass("TRN2", target_bir_lowering=False)
    B = Builder(nc)
    x = B.inp("x", [4 * BLK, D])
    xpre = B.inp("xpre", [4, NPRE, D])
    zeros = B.inp("zeros", [128, 640])
    B.setup_consts()
    ropes = [B.inp(f"rope{b}", [128, 2, T]) for b in range(4)]
    vtabs = [B.inp(f"vtab{b}", [128, 136]) for b in range(4)]
    tabs = [B.inp(f"tab{b}", [128, NTAB]) for b in range(4)]
    W = [_w(B, l) for l in range(DEPTH)]
    out = B.outp("out", [4 * BLK, D])
    hX = [B.scratch(f"hX{b}", [128, NFT, T]) for b in range(4)]
    hY = [B.scratch(f"hY{b}", [128, NFT, T]) for b in range(4)]
    hA = B.scratch("hA", [128, NFT, T])
    hB = B.scratch("hB", [128, NFT, T])
    send = [[B.scratch(f"send{l}_{b}", [128, 640]) for b in range(4)] for l in range(DEPTH)]
    for b in range(4):
        B.phase_inln(x[b * BLK:(b + 1) * BLK, :], xpre[b], hX[b])
    for b in range(4):
        B.load_hb(hX[b])
        B.phase_ffn(0, W[0]["ffn1_w13"], W[0]["ffn1_w2"], 0, hX[b], hY[b])
        B.phase_kv(0, W[0]["w_in"], ropes[b], vtabs[b], send[0][b])
    for l in range(DEPTH):
        for b in range(4):
            B.load_hb(hY[b])
            sp = [send[l][i][:, 0:512] if i < b else zeros[:, 0:512] for i in range(3)]
            hp = send[l][b - 1][:, 512:640] if b > 0 else zeros[:, 512:640]
            B.phase_mix(l, W[l]["w_in"], W[l]["pool_w"], W[l]["conv_pw"], W[l]["w_out"], ropes[b], vtabs[b], tabs[b], sp, hp, hY[b], hA)
            B.phase_ffn(l, W[l]["ffn2_w13"], W[l]["ffn2_w2"], 2, hA, hB)
            if l + 1 < DEPTH:
                B.phase_ffn(l + 1, W[l + 1]["ffn1_w13"], W[l + 1]["ffn1_w2"], 0, hB, hY[b])
                B.phase_kv(l + 1, W[l + 1]["w_in"], ropes[b], vtabs[b], send[l + 1][b])
            else:
                B.phase_final(hB, out[b * BLK:(b + 1) * BLK, :])
    B.P.emit(nc)
    return nc, B


def kernel_fused(inp):
    nc, B = _get("fused")
    cst = consts_arr()
    tabs = [make_tables(jj) for jj in range(4)]
    base = {"consts": cst, "zeros": np.zeros((128, 640), np.float32)}
    for l in range(DEPTH):
        base[f"par{l}"] = pack_params(inp, l)
        for n in ["ffn1_w13", "ffn1_w2", "ffn2_w13", "ffn2_w2", "w_in", "w_out", "pool_w", "conv_pw"]:
            base[f"{n}_{l}"] = np.ascontiguousarray(inp[n][l])
    for b in range(4):
        base[f"rope{b}"], base[f"vtab{b}"], base[f"tab{b}"] = tabs[b]
    xpre = np.zeros((4, NPRE, D), np.float32)
    xpre[0] = inp["meta"]
    maps = []
    for c in range(8):
        m = dict(base)
        m["x"] = np.ascontiguousarray(inp["x"][c % 2])
        m["xpre"] = xpre
        maps.append({k: m[k] for k in B.din})
    res = run_bass_kernel_spmd(nc, maps, core_ids=list(range(8))).results
    return np.stack([res[0]["out"], res[1]["out"]], axis=0).astype(np.float32)


_CACHE = {}


def _get(kind):
    if kind not in _CACHE:
        _CACHE[kind] = build_fused() if kind == "fused" else build_launch(kind)
    return _CACHE[kind]


FUSED = True


def _exchange(sends):
    sprevs, hprevs = [], []
    for c in range(8):
        bi, jj = divmod(c, 4)
        sp = np.zeros((128, 3, 640), np.float32)
        for s in range(jj):
            sp[:, s, :] = sends[bi * 4 + s]
        hp = np.zeros((128, 128), np.float32)
        if jj > 0:
            hp[:] = sends[c - 1][:, 512:640]
        sprevs.append(sp)
        hprevs.append(hp)
    return sprevs, hprevs


def kernel(**inputs):
    inp = {k: np.asarray(v) for k, v in inputs.items()}
    if FUSED:
        return kernel_fused(inp)
    x = inp["x"]
    cst = consts_arr()
    pars = [pack_params(inp, l) for l in range(DEPTH)]
    tabs = [make_tables(jj) for jj in range(4)]
    common = []
    for c in range(8):
        bi, jj = divmod(c, 4)
        rope, vtab, tab = tabs[jj]
        common.append({"consts": cst, "par0": pars[0], "par1": pars[1], "rope": rope, "vtab": vtab, "tab": tab})

    def wsel(names, l):
        return {f"{n}_{l}": np.ascontiguousarray(inp[n][l]) for n in names}

    nc, B = _get(0)
    maps = []
    for c in range(8):
        bi, jj = divmod(c, 4)
        m = dict(common[c])
        m["x"] = np.ascontiguousarray(x[bi, jj * BLK:(jj + 1) * BLK])
        m["xpre"] = inp["meta"] if jj == 0 else np.zeros((NPRE, D), np.float32)
        m.update(wsel(["ffn1_w13", "ffn1_w2", "w_in"], 0))
        maps.append({k: m[k] for k in B.din})
    res = run_bass_kernel_spmd(nc, maps, core_ids=list(range(8))).results
    out = None
    for l in range(DEPTH):
        nc, B = _get(l + 1)
        sprevs, hprevs = _exchange([res[c]["send"] for c in range(8)])
        maps = []
        for c in range(8):
            m = dict(common[c])
            m["hin"] = res[c]["h1"]
            m["sprev"] = sprevs[c]
            m["hprev"] = hprevs[c]
            m.update(wsel(["w_in", "w_out", "pool_w", "conv_pw", "ffn2_w13", "ffn2_w2"], l))
            if l + 1 < DEPTH:
                m.update(wsel(["ffn1_w13", "ffn1_w2", "w_in"], l + 1))
            maps.append({k: m[k] for k in B.din})
        res = run_bass_kernel_spmd(nc, maps, core_ids=list(range(8))).results
    out = np.stack([np.concatenate([res[bi * 4 + jj]["out"] for jj in range(4)], axis=0) for bi in range(2)], axis=0)
    return out.astype(np.float32)
```

```python
import numpy as np
from contextlib import ExitStack
import concourse.bass as bass
import concourse.mybir as mybir
from concourse.bass_utils import run_bass_kernel_spmd

F32 = mybir.dt.float32
BF16 = mybir.dt.bfloat16
AF = mybir.ActivationFunctionType
ALU = mybir.AluOpType

D = 1024
NFT = 8
DFF = 2752
NHT = 22
DIN = 2816
NPRE = 16
BLK = 2048
T = NPRE + BLK
TILES = [(0, 16), (16, 512), (528, 512), (1040, 512), (1552, 512)]
HALVES = [(0, 1, 2), (3, 4)]
HALF_OFF = [0, 1040]
HALF_LEN = [1040, 1024]
DEPTH = 2
ALPHA = (2.0 * DEPTH) ** 0.25
LN_EPS = 1e-5
EPS2 = LN_EPS / (ALPHA * ALPHA)
CHUNK = 64
NHEAD = 4
DH = 128
GAMMAS = [1.0 - 2.0 ** (-5.0 - h) for h in range(NHEAD)]
QSCALE = DH ** -0.5

ENGS = ("pe", "act", "dve", "pool", "sp")
import os
SAME_SYNC = os.environ.get("SAME_SYNC", "1") == "1"
PENG = os.environ.get("PENG", "dve")


class Op:
    __slots__ = ("eng", "fn", "deps", "lane", "needs", "val", "stream")


class Prog:
    def __init__(self):
        self.ops = {e: [] for e in ENGS}
        self.lastw = {}
        self.readers = {}
        self.lane_cnt = {}
        self.last_in_stream = {}

    def add(self, eng, fn, reads=(), writes=(), lane=None):
        o = Op()
        o.eng, o.fn, o.lane, o.needs, o.val = eng, fn, lane, False, None
        o.stream = lane if lane is not None else eng
        deps = {}
        for r in reads:
            p = self.lastw.get(r)
            if p is not None:
                deps[id(p)] = p
            if isinstance(r, tuple) and r[0] == "bank":
                for st, p in self.readers.get(r, {}).items():
                    if st != o.stream:
                        deps[id(p)] = p
        for r in writes:
            p = self.lastw.get(r)
            if p is not None:
                deps[id(p)] = p
            for p in self.readers.get(r, {}).values():
                deps[id(p)] = p
        o.deps = list(deps.values())
        for p in o.deps:
            p.needs = True
        for r in writes:
            self.lastw[r] = o
            self.readers[r] = {}
        for r in reads:
            self.readers.setdefault(r, {})[o.stream] = o
        if lane is not None:
            self.lane_cnt[lane] = self.lane_cnt.get(lane, 0) + 16
            o.val = self.lane_cnt[lane]
        self.ops[eng].append(o)
        self.last_in_stream[o.stream] = o
        return o

    def barrier(self):
        lasts = list(self.last_in_stream.values())
        for p in lasts:
            p.needs = True
        for e in ENGS:
            o = Op()
            o.eng, o.fn, o.lane, o.needs, o.val = e, None, None, False, None
            o.stream = e
            o.deps = [p for p in lasts]
            self.ops[e].append(o)
        self.lastw = {}
        self.readers = {}

    def emit(self, nc):
        for e in ENGS:
            c = 0
            for o in self.ops[e]:
                if o.lane is None and o.needs and o.fn is not None:
                    c += 1
                    o.val = c
                elif o.lane is None:
                    o.val = None
        with ExitStack() as es:
            sems = {}
            for e in ENGS[:4]:
                sems[e] = es.enter_context(nc.semaphore("sem_" + e))
            for ln in self.lane_cnt:
                sems[ln] = es.enter_context(nc.semaphore("lane_" + str(ln)))
            block = es.enter_context(nc.Block())

            def run(e, engine):
                known = {}
                for o in self.ops[e]:
                    need = {}
                    for p in o.deps:
                        if p is o:
                            continue
                        if p.lane is not None and p.lane == o.lane:
                            continue
                        if p.lane is None:
                            if p.eng == e and (e == "pe" or not SAME_SYNC):
                                continue
                            if p.val is None:
                                continue
                        key = p.stream
                        if p.val > need.get(key, 0):
                            need[key] = p.val
                    for key, val in need.items():
                        if known.get(key, 0) >= val:
                            continue
                        engine.wait_ge(sems[key], val)
                        known[key] = val
                    if o.fn is None:
                        continue
                    inst = o.fn(engine)
                    if o.lane is not None:
                        inst.then_inc(sems[o.lane], 16)
                    elif o.needs:
                        inst.then_inc(sems[e], 1)

            @block.tensor
            def _(eng):
                run("pe", eng)

            @block.scalar
            def _(eng):
                run("act", eng)

            @block.vector
            def _(eng):
                run("dve", eng)

            @block.gpsimd
            def _(eng):
                run("pool", eng)

            @block.sync
            def _(eng):
                run("sp", eng)


class Arena:
    def __init__(self, t, nelem):
        self.t = t
        self.n = nelem
        self.off = 0
        self.uid = 0

    def mark(self):
        return self.off

    def reset(self, m):
        self.off = m

    def alloc(self, free_shape, dtype, name):
        n = int(np.prod(free_shape))
        if dtype == F32:
            self.off += self.off % 2
            ap = self.t[:, self.off:self.off + 2 * n].bitcast(F32)
            self.off += 2 * n
        else:
            ap = self.t[:, self.off:self.off + n]
            self.off += n
        assert self.off <= self.n, ("SBUF arena overflow", name, self.off, self.n)
        if len(free_shape) == 2:
            ap = ap.rearrange("p (a b) -> p a b", a=free_shape[0])
        elif len(free_shape) == 3:
            ap = ap.rearrange("p (a b c) -> p a b c", a=free_shape[0], b=free_shape[1])
        self.uid += 1
        return ap


class Builder:
    def __init__(self, nc):
        self.nc = nc
        self.P = Prog()
        self.es = ExitStack()
        ARENA_ELEMS = 103000
        at = self.es.enter_context(nc.sbuf_tensor("arena", [128, ARENA_ELEMS], BF16))
        self.A = Arena(at, ARENA_ELEMS)
        self.banks = [self.es.enter_context(nc.psum_tensor(f"bank{i}", [128, 512], F32)) for i in range(8)]
        self.din = {}
        self.dout = {}
        self.rot = {}

    def inp(self, name, shape):
        if name not in self.din:
            self.din[name] = self.nc.dram_tensor(name, list(shape), F32, kind="ExternalInput").ap()
        return self.din[name]

    def outp(self, name, shape):
        self.dout[name] = self.nc.dram_tensor(name, list(shape), F32, kind="ExternalOutput").ap()
        return self.dout[name]

    def scratch(self, name, shape):
        return self.nc.dram_tensor(name, list(shape), F32, kind="Internal").ap()

    def mm(self, out, lhsT, rhs, start, stop, reads, writes):
        return self.P.add("pe", lambda e: e.matmul(out, lhsT, rhs, start=start, stop=stop), reads, writes)

    def tr(self, out, in_, ident, reads, writes):
        return self.P.add("pe", lambda e: e.transpose(out, in_, ident), reads, writes)

    def act(self, out, in_, func, reads, writes, scale=1.0, bias=0.0):
        return self.P.add("act", lambda e: e.activation(out=out, in_=in_, func=func, bias=bias, scale=scale), reads, writes)

    def tt(self, eng, out, in0, in1, op, reads, writes):
        return self.P.add(eng, lambda e: e.tensor_tensor(out=out, in0=in0, in1=in1, op=op), reads, writes)

    def ts(self, eng, out, in0, s1, s2, op0, op1, reads, writes):
        return self.P.add(eng, lambda e: e.tensor_scalar(out=out, in0=in0, scalar1=s1, scalar2=s2, op0=op0, op1=op1), reads, writes)

    def stt(self, eng, out, in0, scalar, in1, op0, op1, reads, writes):
        return self.P.add(eng, lambda e: e.scalar_tensor_tensor(out=out, in0=in0, scalar=scalar, in1=in1, op0=op0, op1=op1), reads, writes)

    def ts1(self, eng, out, in_, scalar, op, reads, writes):
        return self.P.add(eng, lambda e: e.tensor_single_scalar(out=out, in_=in_, scalar=scalar, op=op), reads, writes)

    def copy(self, eng, out, in_, reads, writes):
        return self.P.add(eng, lambda e: e.tensor_copy(out=out, in_=in_), reads, writes)

    def dma(self, eng, out, in_, reads, writes, lane):
        return self.P.add(eng, lambda e: e.dma_start(out=out, in_=in_), reads, writes, lane=lane)

    def rsqrt(self, out, in_, eps, reads, writes):
        self.act(out, in_, AF.Ln, reads, writes, scale=1.0, bias=eps)
        self.act(out, out, AF.Exp, writes, writes, scale=-0.5)

    def nxt(self, name, n):
        i = self.rot.get(name, 0)
        self.rot[name] = i + 1
        return i % n

    def setup_consts(self):
        A = self.A
        cst = self.inp("consts", [128, 256])
        self.ident_f = A.alloc((128,), F32, "ident_f")
        self.ident_b = A.alloc((128,), BF16, "ident_b")
        self.ones_b = A.alloc((128,), BF16, "ones_b")
        self.dma("sp", self.ident_f, cst[:, 0:128], [], ["ident_f"], "c0")
        self.dma("pool", self.ident_b, cst[:, 0:128], [], ["ident_b"], "c1")
        self.dma("pool", self.ones_b, cst[:, 128:256], [], ["ones_b"], "c1")
        self.par = []
        for l in range(DEPTH):
            p = A.alloc((NPAR,), F32, f"par{l}")
            self.dma("sp", p, self.inp(f"par{l}", [128, NPAR]), [], [f"par{l}"], "c0")
            self.par.append(p)

    def phase_inln(self, x_ap, xpre_ap, h_dst):
        A, P = self.A, self.P
        m0 = A.mark()
        xt = [A.alloc((1024,), F32, "xt") for _ in range(2)]
        xn = [A.alloc((1024,), F32, "xn") for _ in range(2)]
        st = [A.alloc((2, 6), F32, "st") for _ in range(2)]
        mv = [A.alloc((2,), F32, "mv") for _ in range(2)]
        rs = [A.alloc((1,), F32, "rs") for _ in range(2)]
        stage = [A.alloc((8, 128), F32, "stage") for _ in range(2)]
        par = self.par[0]
        subt = [(xpre_ap, 0, 16, 0)] + [(x_ap, s * 128, 128, NPRE + s * 128) for s in range(BLK // 128)]
        for i, (src, r0, n, toff) in enumerate(subt):
            k = i % 2
            self.dma("sp", xt[k][:n, :], src[r0:r0 + n, :], [], [("xt", k)], ("xt", k))
            for c in range(2):
                P.add("dve", lambda e, k=k, c=c, n=n: e.bn_stats(out=st[k][:n, c, :], in_=xt[k][:n, c * 512:(c + 1) * 512]),
                      [("xt", k)], [("st", k, c)])
            P.add("dve", lambda e, k=k, n=n: e.bn_aggr(out=mv[k][:n, :], in_=st[k][:n, :, :].rearrange("p a b -> p (a b)")),
                  [("st", k, 0), ("st", k, 1)], [("mv", k)])
            self.rsqrt(rs[k][:n, :], mv[k][:n, 1:2], LN_EPS, [("mv", k)], [("rs", k)])
            self.ts("dve", xn[k][:n, :], xt[k][:n, :], mv[k][:n, 0:1], rs[k][:n, 0:1], ALU.subtract, ALU.mult,
                    [("xt", k), ("mv", k), ("rs", k)], [("xn", k)])
            for half in range(2):
                bk = self.nxt("inln_bank", 2)
                ps = self.banks[bk]
                for j in range(4):
                    ft = half * 4 + j
                    self.tr(ps[:, j * 128:j * 128 + n], xn[k][:n, ft * 128:(ft + 1) * 128], self.ident_f[:n, :n],
                            [("xn", k), "ident_f"], [("bank", bk)])
                for j in range(4):
                    ft = half * 4 + j
                    self.act(stage[k][:, ft, :n], ps[:, j * 128:j * 128 + n], AF.Identity,
                             [("bank", bk), "par0"], [("stage", k)],
                             scale=par[:, PAR_LNIN_G + ft:PAR_LNIN_G + ft + 1], bias=par[:, PAR_LNIN_B + ft:PAR_LNIN_B + ft + 1])
            self.dma("sp", h_dst[:, :, toff:toff + n], stage[k][:, :, :n], [("stage", k)], [], ("stg", k))
        P.barrier()
        A.reset(m0)

    def load_hb(self, h_src):
        if not hasattr(self, "hb"):
            self.hb = self.A.alloc((NFT, T), BF16, "hb")
        for ti, (o, n) in enumerate(TILES):
            self.dma("pool", self.hb[:, :, o:o + n], h_src[:, :, o:o + n], [],
                     [("hb", ft, ti) for ft in range(NFT)], ("hbld", ti))

    def ln_alloc(self, ntile):
        A = self.A
        self.yp = [A.alloc((8, 512), F32, "yp") for _ in range(ntile)]
        self.sq = [A.alloc((512,), BF16, "sq") for _ in range(2)]
        self.yb = [A.alloc((512,), BF16, "yb") for _ in range(2)]
        self.mean = [A.alloc((512,), F32, "mean") for _ in range(ntile)]
        self.rstd = [A.alloc((512,), F32, "rstd") for _ in range(ntile)]
        self.nmr = [A.alloc((512,), F32, "nmr") for _ in range(ntile)]

    def ln_accum(self, li, fo, n, py, pb, cres):
        yp = self.yp[li]
        self.stt("dve", yp[:, fo, :n], py[:, :n], cres, yp[:, fo, :n], ALU.mult, ALU.add,
                 [("bank", pb), ("yp", li)], [("ypf", li, fo)])
        q = self.nxt("sq", 2)
        self.act(self.sq[q][:, :n], yp[:, fo, :n], AF.Square, [("ypf", li, fo)], [("sq", q)])
        self.act(self.yb[q][:, :n], yp[:, fo, :n], AF.Copy, [("ypf", li, fo)], [("yb", q)])
        bs, bq = 2 + 2 * li, 3 + 2 * li
        self.mm(self.banks[bs][:, :n], self.ones_b, self.yb[q][:, :n], fo == 0, fo == NFT - 1, ["ones_b", ("yb", q)], [("bank", bs)])
        self.mm(self.banks[bq][:, :n], self.ones_b, self.sq[q][:, :n], fo == 0, fo == NFT - 1, ["ones_b", ("sq", q)], [("bank", bq)])

    def stats_finish(self, bs, bq, n, inv, eps, mn, rd, nm, key):
        self.ts1("dve", mn[:, :n], self.banks[bs][:, :n], inv, ALU.mult, [("bank", bs)], [("mean", key)])
        self.tt("dve", nm[:, :n], mn[:, :n], mn[:, :n], ALU.mult, [("mean", key)], [("nmr", key)])
        self.stt("dve", rd[:, :n], self.banks[bq][:, :n], inv, nm[:, :n], ALU.mult, ALU.subtract,
                 [("bank", bq), ("nmr", key)], [("rstd", key)])
        self.rsqrt(rd[:, :n], rd[:, :n], eps, [("rstd", key)], [("rstd", key)])
        self.stt("dve", nm[:, :n], mn[:, :n], -1.0, rd[:, :n], ALU.mult, ALU.mult, [("mean", key), ("rstd", key)], [("nmr", key)])

    def ln_finish(self, l, lnidx, li, ti, h_dst, eng="dve"):
        par = self.par[l]
        gcol = PAR_LN_G + lnidx * 8
        bcol = PAR_LN_B + lnidx * 8
        o, n = TILES[ti]
        yp, hb = self.yp[li], self.hb
        bs, bq = 2 + 2 * li, 3 + 2 * li
        mn, rd, nm = self.mean[li], self.rstd[li], self.nmr[li]
        self.stats_finish(bs, bq, n, 1.0 / D, EPS2, mn, rd, nm, li)
        for fo in range(NFT):
            self.tt(eng, yp[:, fo, :n], yp[:, fo, :n], rd[:, :n], ALU.mult, [("ypf", li, fo), ("rstd", li)], [("ypf", li, fo)])
            self.tt(eng, yp[:, fo, :n], yp[:, fo, :n], nm[:, :n], ALU.add, [("ypf", li, fo), ("nmr", li)], [("ypf", li, fo)])
            self.act(hb[:, fo, o:o + n], yp[:, fo, :n], AF.Identity, [("ypf", li, fo), f"par{l}"], [("hb", fo, ti)],
                     scale=par[:, gcol + fo:gcol + fo + 1], bias=par[:, bcol + fo:bcol + fo + 1])
            self.act(yp[:, fo, :n], yp[:, fo, :n], AF.Identity, [("ypf", li, fo), f"par{l}"], [("ypf", li, fo)],
                     scale=par[:, gcol + fo:gcol + fo + 1], bias=par[:, bcol + fo:bcol + fo + 1])
        self.dma("sp", h_dst[:, :, o:o + n], yp[:, :, :n], [("ypf", li, fo) for fo in range(NFT)] + [("yp", li)],
                 [("yp", li)], ("ypst", li))

    def phase_ffn(self, l, w13, w2, lnidx, h_src, h_dst):
        A, P = self.A, self.P
        m0 = A.mark()
        hid = A.alloc((NHT, 1040), BF16, "hid")
        w13s = [A.alloc((8, 2, 128), BF16, "w13s") for _ in range(3)]
        w2s = [A.alloc((NHT, 128), BF16, "w2s") for _ in range(2)]
        self.ln_alloc(3)
        sg = [A.alloc((512,), F32, "sg") for _ in range(4)]
        hb = self.hb
        w13v = w13.rearrange("(kt p) c -> p kt c", p=128)
        CRES = 0.5 / ALPHA
        for hi, tiles in enumerate(HALVES):
            hoff = HALF_OFF[hi]
            for li, ti in enumerate(tiles):
                o, n = TILES[ti]
                self.dma("sp", self.yp[li][:, :, :n], h_src[:, :, o:o + n], [], [("yp", li)], ("ypld", li))
            for m in range(NHT):
                mw = 128 if m < NHT - 1 else 64
                s = self.nxt("w13s", 3)
                self.dma("pool", w13s[s][:, :, 0, :mw], w13v[:, :, m * 128:m * 128 + mw], [], [("w13s", s)], ("w13", s))
                self.dma("pool", w13s[s][:, :, 1, :mw], w13v[:, :, DFF + m * 128:DFF + m * 128 + mw], [], [("w13s", s)], ("w13", s))
                for ti in tiles:
                    o, n = TILES[ti]
                    pb = self.nxt("upbank", 4) * 2
                    pa, pu = self.banks[pb], self.banks[pb + 1]
                    for kt in range(NFT):
                        self.mm(pa[:mw, :n], w13s[s][:, kt, 0, :mw], hb[:, kt, o:o + n], kt == 0, kt == NFT - 1,
                                [("w13s", s), ("hb", kt, ti)], [("bank", pb)])
                    for kt in range(NFT):
                        self.mm(pu[:mw, :n], w13s[s][:, kt, 1, :mw], hb[:, kt, o:o + n], kt == 0, kt == NFT - 1,
                                [("w13s", s), ("hb", kt, ti)], [("bank", pb + 1)])
                    g = self.nxt("sg", 4)
                    self.act(sg[g][:mw, :n], pa[:mw, :n], AF.Silu, [("bank", pb)], [("sg", g)])
                    self.tt("dve", hid[:mw, m, o - hoff:o - hoff + n], sg[g][:mw, :n], pu[:mw, :n], ALU.mult,
                            [("sg", g), ("bank", pb + 1)], [("hid", m, ti)])
            for fo in range(NFT):
                s = self.nxt("w2s", 2)
                self.dma("pool", w2s[s][:, 0:NHT - 1, :], w2[0:(NHT - 1) * 128, fo * 128:(fo + 1) * 128].rearrange("(kt p) c -> p kt c", p=128),
                         [], [("w2s", s)], ("w2", s))
                self.dma("pool", w2s[s][0:64, NHT - 1, :], w2[(NHT - 1) * 128:DFF, fo * 128:(fo + 1) * 128], [], [("w2s", s)], ("w2", s))
                for li, ti in enumerate(tiles):
                    o, n = TILES[ti]
                    pb = self.nxt("dnbank", 2)
                    py = self.banks[pb]
                    for m in range(NHT):
                        mw = 128 if m < NHT - 1 else 64
                        self.mm(py[:, :n], w2s[s][:mw, m, :], hid[:mw, m, o - hoff:o - hoff + n], m == 0, m == NHT - 1,
                                [("w2s", s), ("hid", m, ti)], [("bank", pb)])
                    self.ln_accum(li, fo, n, py, pb, CRES)
            for li, ti in enumerate(tiles):
                self.ln_finish(l, lnidx, li, ti, h_dst)
        P.barrier()
        A.reset(m0)

    def rope(self, ps, pbkey, rp, rkey, n, dst, dkey, sum_eng="dve"):
        a, b = self.ropeA, self.ropeB
        self.tt("dve", a[:, :n], ps[:, :n], rp[:, 0, :n], ALU.mult, [pbkey, rkey], ["ropeA"])
        self.tt("dve", b[0:64, :n], ps[64:128, :n], rp[64:128, 1, :n], ALU.mult, [pbkey, rkey], ["ropeB0"])
        self.tt("dve", b[64:128, :n], ps[0:64, :n], rp[0:64, 1, :n], ALU.mult, [pbkey, rkey], ["ropeB1"])
        self.tt(sum_eng, dst, a[:, :n], b[:, :n], ALU.add, ["ropeA", "ropeB0", "ropeB1"], [dkey])

    def load_rope(self, rope_d, ti):
        o, n = TILES[ti]
        k = self.nxt("ropet", 2)
        self.dma("sp", self.ropet[k][:, :, :n], rope_d[:, :, o:o + n], [], [("ropet", k)], ("ropet", k))
        return self.ropet[k], ("ropet", k)

    def phase_kv(self, l, w_in, rope_d, vtab_d, send_d):
        A, P = self.A, self.P
        m0 = A.mark()
        hb = self.hb
        wk = A.alloc((8, 512), BF16, "wk")
        wv = A.alloc((8, 512), BF16, "wv")
        wh = A.alloc((8, 768), BF16, "wh")
        self.ropet = [A.alloc((2, 512), F32, "ropet") for _ in range(2)]
        self.ropeA = A.alloc((512,), F32, "ropeA")
        self.ropeB = A.alloc((512,), F32, "ropeB")
        vtab = A.alloc((17, 2, 4), F32, "vtab")
        kT = A.alloc((4, 512), BF16, "kT")
        vfull = [A.alloc((4, 128), BF16, "vfull") for _ in range(2)]
        kTok = [A.alloc((4, 128), BF16, "kTok") for _ in range(2)]
        send = A.alloc((640,), F32, "send")
        sgm = A.alloc((2, 32), F32, "sgm")
        w_inv = w_in.rearrange("(kt p) c -> p kt c", p=128)
        self.dma("pool", wk, w_inv[:, :, 1280:1792], [], ["wk"], "wk")
        self.dma("pool", wv, w_inv[:, :, 1792:2304], [], ["wv"], "wv")
        self.dma("pool", wh, w_inv[:, :, 0:768], [], ["wh"], "wh")
        self.dma("sp", vtab, vtab_d.rearrange("p (a b c) -> p a b c", a=17, b=2), [], ["vtab"], "vtab")
        b7 = self.banks[7][:, :].bitcast(BF16)
        SB = 4
        sidx = 0
        for ti, (o, n) in enumerate(TILES):
            rp, rkey = self.load_rope(rope_d, ti)
            for h in range(NHEAD):
                pb = self.nxt("kvbank", 2)
                ps = self.banks[pb]
                for kt in range(NFT):
                    self.mm(ps[:, :n], wk[:, kt, h * 128:(h + 1) * 128], hb[:, kt, o:o + n], kt == 0, kt == NFT - 1,
                            ["wk", ("hb", kt, ti)], [("bank", pb)])
                self.rope(ps, ("bank", pb), rp, rkey, n, kT[:, h, :n], ("kT", h))
            nsub = max(1, n // 128)
            for sub in range(nsub):
                ns = min(n, 128)
                c0 = o + sub * 128
                pb = 2 + self.nxt("kvbank2", 2)
                ps = self.banks[pb]
                for kt in range(NFT):
                    self.mm(ps[:ns, :], hb[:, kt, c0:c0 + ns], wv[:, kt, :], kt == 0, kt == NFT - 1,
                            ["wv", ("hb", kt, ti)], [("bank", pb)])
                k2 = self.nxt("vfull", 2)
                self.tt("dve", vfull[k2][:ns, :, :], ps[:ns, :].rearrange("p (h e) -> p h e", h=4),
                        vtab[:ns, sidx, 1, :].unsqueeze(2).to_broadcast([ns, 4, 128]), ALU.mult,
                        [("bank", pb), "vtab"], [("vfull", k2)])
                for h in range(NHEAD):
                    self.tr(b7[:ns, h * 128:(h + 1) * 128], kT[:, h, sub * 128:sub * 128 + ns], self.ident_b,
                            [("kT", h), "ident_b"], [("bank", 7)])
                self.act(kTok[k2][:ns, :, :], b7[:ns, 0:512].rearrange("p (h e) -> p h e", h=4), AF.Copy, [("bank", 7)], [("kTok", k2)])
                for h in range(NHEAD):
                    self.mm(self.banks[SB][:, h * 128:(h + 1) * 128], kTok[k2][:ns, h, :], vfull[k2][:ns, h, :], sidx == 0, sidx == 16,
                            [("kTok", k2), ("vfull", k2)], [("bank", SB)])
                sidx += 1
        HB = 5
        ps = self.banks[HB]
        for mt in range(6):
            for kt in range(NFT):
                self.mm(ps[:, mt * 32:(mt + 1) * 32], wh[:, kt, mt * 128:(mt + 1) * 128], hb[:, kt, T - 32:T], kt == 0, kt == NFT - 1,
                        ["wh", ("hb", kt, 4)], [("bank", HB)])
        self.act(send[:, 0:512], self.banks[SB][:, :], AF.Copy, [("bank", SB)], ["send_s"])
        self.act(sgm[:, :, :], ps[:, 128:192].rearrange("p (a b) -> p a b", a=2), AF.Sigmoid, [("bank", HB)], ["sgm"])
        self.tt("dve", send[:, 512:576].rearrange("p (a b) -> p a b", a=2), sgm[:, :, :], ps[:, 64:128].rearrange("p (a b) -> p a b", a=2),
                ALU.mult, ["sgm", ("bank", HB)], ["send_u"])
        self.act(send[:, 576:640], ps[:, 0:64], AF.Copy, [("bank", HB)], ["send_p"])
        self.dma("sp", send_d[:, :], send, ["send_s", "send_u", "send_p"], [], "send")
        P.barrier()
        A.reset(m0)

    def phase_mix(self, l, w_in, pool_w, conv_pw, w_out, rope_d, vtab_d, tab_d, sprev_srcs, hprev_src, h_src, h_dst):
        A, P = self.A, self.P
        m0 = A.mark()
        hb = self.hb
        par = self.par[l]
        tab = A.alloc((NTAB,), F32, "tab")
        vtab = A.alloc((17, 2, 4), F32, "vtab")
        self.ropet = [A.alloc((2, 512), F32, "ropet") for _ in range(2)]
        self.ropeA = A.alloc((512,), F32, "ropeA")
        self.ropeB = A.alloc((512,), F32, "ropeB")
        sprev = A.alloc((3, 512), F32, "sprev")
        hprev = A.alloc((128,), F32, "hprev")
        S = A.alloc((4, 128), F32, "S")
        Stmp = A.alloc((4, 128), F32, "Stmp")
        Sb = A.alloc((4, 128), BF16, "Sb")
        wsl = [A.alloc((8, 128), BF16, "wsl") for _ in range(4)]
        wv = A.alloc((8, 512), BF16, "wv")
        wbd = A.alloc((2, 128), BF16, "wbd")
        wpw = A.alloc((2, 256), BF16, "wpw")
        XP = A.alloc((2, 15 + 512), F32, "XP")
        S2e = A.alloc((526,), F32, "S2e")
        S4e = A.alloc((524,), F32, "S4e")
        S8e = A.alloc((520,), F32, "S8e")
        Mb = A.alloc((512,), F32, "Mb")
        ypool = A.alloc((2, 512), BF16, "ypool")
        U = A.alloc((2, 30 + 512), BF16, "U")
        Utmp = A.alloc((2, 30), BF16, "Utmp")
        cdiag = A.alloc((2, 31, 128), BF16, "cdiag")
        acc = A.alloc((2, 512), F32, "acc")
        sgt = A.alloc((512,), F32, "sgt")
        cact = A.alloc((2, 512), BF16, "cact")
        qrot = A.alloc((512,), F32, "qrot")
        qT = A.alloc((4, 512), BF16, "qT")
        qdec = A.alloc((4, 512), BF16, "qdec")
        kT = A.alloc((4, 512), BF16, "kT")
        sgate = A.alloc((4, 512), F32, "sgate")
        vbf = A.alloc((4, 4, 128), BF16, "vbf")
        vdec = A.alloc((4, 4, 128), BF16, "vdec")
        kTok = A.alloc((4, 4, 128), BF16, "kTok")
        sdT = A.alloc((4, 4, 128), BF16, "sdT")
        ycat = A.alloc((8, 512), BF16, "ycat")
        t1 = A.alloc((512,), F32, "t1")
        self.ln_alloc(1)
        b7 = self.banks[7][:, :].bitcast(BF16)
        w_inv = w_in.rearrange("(kt p) c -> p kt c", p=128)
        w_outv = w_out.rearrange("(kt p) c -> p kt c", p=128)

        def tcol(c0, w):
            return tab[:, c0:c0 + w]

        self.dma("sp", tab, tab_d[:, :], [], ["tab"], "tab")
        self.dma("sp", vtab, vtab_d.rearrange("p (a b c) -> p a b c", a=17, b=2), [], ["vtab"], "vtab")
        for i in range(3):
            self.dma("sp", sprev[:, i, :], sprev_srcs[i], [], ["sprev"], "sprev")
        self.dma("sp", hprev, hprev_src, [], ["hprev"], "hprev")
        self.dma("pool", wv, w_inv[:, :, 1792:2304], [], ["wv"], "wv")
        P.add("dve", lambda e: e.memset(wbd[:, :, :], 0.0), [], ["wbd"])
        for tl in range(2):
            for a in range(2):
                self.dma("pool", wbd[64 * a:64 * a + 64, tl, 64 * a:64 * a + 64], pool_w[2 * tl + a, :, :], ["wbd"], [("wbd2", tl, a)], "wbd")
        self.dma("pool", wpw, conv_pw.rearrange("(kt p) c -> p kt c", p=128), [], ["wpw"], "wpw")
        for tl in range(2):
            for j in range(31):
                self.ts1("dve", cdiag[:, tl, j, :], self.ident_f, par[:, PAR_CONV_W + tl * 31 + j:PAR_CONV_W + tl * 31 + j + 1], ALU.mult,
                         ["ident_f", f"par{l}"], [("cdiag", tl)])
        P.add("dve", lambda e: e.memset(XP[:, :, 0:15], 0.0), [], ["XPh"])
        P.add("dve", lambda e: e.memset(U[:, :, 0:30], 0.0), [], ["Uh"])
        for i in range(3):
            cb = tab[:, TB_COEF + 4 * i:TB_COEF + 4 * i + 4].unsqueeze(2).to_broadcast([128, 4, 128])
            src = sprev[:, i, :].rearrange("p (h e) -> p h e", h=4)
            if i == 0:
                self.tt("dve", S[:, :, :], src, cb, ALU.mult, ["sprev", "tab"], ["S"])
            else:
                self.tt("dve", Stmp[:, :, :], src, cb, ALU.mult, ["sprev", "tab"], ["Stmp"])
                self.tt("dve", S[:, :, :], S[:, :, :], Stmp[:, :, :], ALU.add, ["S", "Stmp"], ["S"])
        self.act(Sb[:, :, :], S[:, :, :], AF.Copy, ["S"], ["Sb"])
        dtab = tab[:, TB_DTAB:TB_DTAB + 512].rearrange("p (h e) -> p h e", h=4)
        ff = tab[:, TB_FLAG:TB_FLAG + 1]
        nf = tab[:, TB_FLAG + 1:TB_FLAG + 2]

        sidx = 0
        STOP = getattr(self, "mix_stop", 99)
        NT = getattr(self, "mix_tiles", len(TILES))
        for ti, (o, n) in enumerate(TILES[:NT]):
            rp, rkey = self.load_rope(rope_d, ti)
            self.dma("sp", self.yp[0][:, :, :n], h_src[:, :, o:o + n], [], [("yp", 0)], ("ypld", 0))
            nsub = max(1, n // 128)
            ns = min(n, 128)

            def proj(col0):
                s = self.nxt("wsl", 4)
                self.dma("pool", wsl[s], w_inv[:, :, col0:col0 + 128], [], [("wsl", s)], ("wsl", s))
                pb = self.nxt("pjbank", 3)
                ps = self.banks[pb]
                for kt in range(NFT):
                    self.mm(ps[:, :n], wsl[s][:, kt, :], hb[:, kt, o:o + n], kt == 0, kt == NFT - 1,
                            [("wsl", s), ("hb", kt, ti)], [("bank", pb)])
                return ps, ("bank", pb)

            for tl in range(2):
                ps, pk = proj(0 + tl * 128)
                self.act(XP[:, tl, 15:15 + n], ps[:, :n], AF.Copy, [pk], [("XP", tl)])
                x = XP[:, tl, :]
                hk = [("XP", tl), "XPh"]
                self.tt(PENG, S2e[:, 0:n + 14], x[:, 1:15 + n], x[:, 0:14 + n], ALU.add, hk, ["S2e"])
                self.tt(PENG, S4e[:, 0:n + 12], S2e[:, 2:n + 14], S2e[:, 0:n + 12], ALU.add, ["S2e"], ["S4e"])
                self.tt(PENG, S8e[:, 0:n + 8], S4e[:, 4:n + 12], S4e[:, 0:n + 8], ALU.add, ["S4e"], ["S8e"])
                pc = lambda w: tab[:, TB_PC + tl * 4 + w:TB_PC + tl * 4 + w + 1]
                self.ts1("dve", Mb[:, :n], S2e[:, 14:14 + n], pc(0), ALU.mult, ["S2e", "tab"], ["Mb"])
                self.stt("dve", Mb[:, :n], S4e[:, 12:12 + n], pc(1), Mb[:, :n], ALU.mult, ALU.add, ["S4e", "Mb", "tab"], ["Mb"])
                self.stt("dve", Mb[:, :n], S8e[:, 8:8 + n], pc(2), Mb[:, :n], ALU.mult, ALU.add, ["S8e", "Mb", "tab"], ["Mb"])
                self.stt("dve", Mb[:, :n], S8e[:, 8:8 + n], pc(3), Mb[:, :n], ALU.mult, ALU.add, ["S8e", "Mb", "tab"], ["Mb"])
                self.stt("dve", Mb[:, :n], S8e[:, 0:n], pc(3), Mb[:, :n], ALU.mult, ALU.add, ["S8e", "Mb", "tab"], ["Mb"])
                if ti == 0:
                    self.tt(PENG, Mb[:, :n], Mb[:, :n], tab[:, TB_PCORR + tl * 16:TB_PCORR + tl * 16 + 16], ALU.mult, ["Mb", "tab"], ["Mb"])
                self.tt("dve", ypool[:, tl, :n], Mb[:, :n], x[:, 15:15 + n], ALU.subtract, ["Mb", ("XP", tl)], [("ypool", tl)])
                self.copy("dve", XP[:, tl, 0:15], XP[:, tl, n:n + 15], [("XP", tl)], ["XPh", ("XP", tl)])
                if ti == 0:
                    self.ts1("dve", XP[:, tl, 0:15], XP[:, tl, 0:15], ff, ALU.mult, ["XPh", ("XP", tl), "tab"], ["XPh", ("XP", tl)])
                    self.stt("dve", XP[:, tl, 0:15], hprev[:, 64 + tl * 32 + 17:64 + tl * 32 + 32], nf, XP[:, tl, 0:15], ALU.mult, ALU.add,
                             ["hprev", "tab", "XPh", ("XP", tl)], ["XPh", ("XP", tl)])
                pb = 6
                self.mm(self.banks[pb][:, :n], wbd[:, tl, :], ypool[:, tl, :n], True, True,
                        [("wbd2", tl, 0), ("wbd2", tl, 1), "wbd", ("ypool", tl)], [("bank", pb)])
                self.act(ycat[:, tl, :n], self.banks[pb][:, :n], AF.Copy, [("bank", pb), f"par{l}"], [("ycat", tl)],
                         scale=par[:, PAR_POOL_SCALE + tl:PAR_POOL_SCALE + tl + 1])
            if STOP <= 1:
                continue
            for tl in range(2):
                psg, pkg = proj(512 + tl * 128)
                self.act(sgt[:, :n], psg[:, :n], AF.Sigmoid, [pkg], ["sgt"])
                psa, pka = proj(256 + tl * 128)
                self.tt("dve", U[:, tl, 30:30 + n], sgt[:, :n], psa[:, :n], ALU.mult, ["sgt", pka], [("U", tl)])
                uk = [("U", tl), "Uh", ("cdiag", tl)]
                cb = 4 + tl
                for j in range(31):
                    self.mm(self.banks[cb][:, :n], cdiag[:, tl, j, :], U[:, tl, j:j + n], j == 0, j == 30, uk, [("bank", cb)])
                self.act(acc[:, tl, :n], self.banks[cb][:, :n], AF.Identity, [("bank", cb), f"par{l}"], [("acc", tl)],
                         bias=par[:, PAR_CONV_DB + tl:PAR_CONV_DB + tl + 1])
                if n >= 30:
                    self.copy("dve", U[:, tl, 0:30], U[:, tl, n:n + 30], [("U", tl)], ["Uh", ("U", tl)])
                else:
                    self.copy("dve", Utmp[:, tl, :], U[:, tl, n:n + 30], [("U", tl), "Uh"], [("Utmp", tl)])
                    self.copy("dve", U[:, tl, 0:30], Utmp[:, tl, :], [("Utmp", tl)], ["Uh", ("U", tl)])
                if ti == 0:
                    self.ts1("dve", U[:, tl, 0:30], U[:, tl, 0:30], ff, ALU.mult, ["Uh", ("U", tl), "tab"], ["Uh", ("U", tl)])
                    self.stt("dve", U[:, tl, 0:30], hprev[:, tl * 32 + 2:tl * 32 + 32], nf, U[:, tl, 0:30], ALU.mult, ALU.add,
                             ["hprev", "tab", "Uh", ("U", tl)], ["Uh", ("U", tl)])
            for tl in range(2):
                q = self.nxt("sq", 2)
                self.act(self.sq[q][:, :n], acc[:, tl, :n], AF.Square, [("acc", tl)], [("sq", q)])
                self.act(self.yb[q][:, :n], acc[:, tl, :n], AF.Copy, [("acc", tl)], [("yb", q)])
                self.mm(self.banks[4][:, :n], self.ones_b, self.yb[q][:, :n], tl == 0, tl == 1, ["ones_b", ("yb", q)], [("bank", 4)])
                self.mm(self.banks[5][:, :n], self.ones_b, self.sq[q][:, :n], tl == 0, tl == 1, ["ones_b", ("sq", q)], [("bank", 5)])
            mn, rd, nm = self.mean[0], self.rstd[0], self.nmr[0]
            self.stats_finish(4, 5, n, 1.0 / 256, LN_EPS, mn, rd, nm, 0)
            for tl in range(2):
                self.tt("dve", acc[:, tl, :n], acc[:, tl, :n], rd[:, :n], ALU.mult, [("acc", tl), ("rstd", 0)], [("acc", tl)])
                self.tt("dve", acc[:, tl, :n], acc[:, tl, :n], nm[:, :n], ALU.add, [("acc", tl), ("nmr", 0)], [("acc", tl)])
                self.act(cact[:, tl, :n], acc[:, tl, :n], AF.Silu, [("acc", tl), f"par{l}"], [("cact", tl)],
                         scale=par[:, PAR_CONV_LN_G + tl:PAR_CONV_LN_G + tl + 1], bias=par[:, PAR_CONV_LN_B + tl:PAR_CONV_LN_B + tl + 1])
            for mt in range(2):
                pb = 6
                for kt in range(2):
                    self.mm(self.banks[pb][:, :n], wpw[:, kt, mt * 128:(mt + 1) * 128], cact[:, kt, :n], kt == 0, kt == 1,
                            ["wpw", ("cact", kt)], [("bank", pb)])
                self.act(ycat[:, 2 + mt, :n], self.banks[pb][:, :n], AF.Copy, [("bank", pb)], [("ycat", 2 + mt)])
            if STOP <= 2:
                continue
            for h in range(NHEAD):
                ps, pk = proj(768 + h * 128)
                self.rope(ps, pk, rp, rkey, n, qrot[:, :n], "qrot", sum_eng=PENG)
                self.act(qT[:, h, :n], qrot[:, :n], AF.Copy, ["qrot"], [("qT", h)])
                if n >= 64:
                    self.tt("dve", qdec[:, h, :n].rearrange("p (c i) -> p c i", i=64), qrot[:, :n].rearrange("p (c i) -> p c i", i=64),
                            tab[:, TB_QD + h * 64:TB_QD + h * 64 + 64].unsqueeze(1).to_broadcast([128, n // 64, 64]), ALU.mult,
                            ["qrot", "tab"], [("qdec", h)])
            if STOP <= 2.2:
                continue
            for h in range(NHEAD):
                ps, pk = proj(1280 + h * 128)
                self.rope(ps, pk, rp, rkey, n, kT[:, h, :n], ("kT", h), sum_eng=PENG)
            if STOP <= 2.4:
                continue
            for h in range(NHEAD):
                ps, pk = proj(2304 + h * 128)
                self.act(sgate[:, h, :n], ps[:, :n], AF.Silu, [pk], [("sgate", h)])
            if STOP <= 2.6:
                continue
            for sub in range(nsub):
                c0 = o + sub * 128
                pb = 3
                ps = self.banks[pb]
                for kt in range(NFT):
                    self.mm(ps[:ns, :], hb[:, kt, c0:c0 + ns], wv[:, kt, :], kt == 0, kt == NFT - 1,
                            ["wv", ("hb", kt, ti)], [("bank", pb)])
                psv = ps[:ns, :].rearrange("p (h e) -> p h e", h=4)
                self.act(vbf[:ns, sub, :, :], psv, AF.Copy, [("bank", pb)], [("vbf", sub)])
                self.tt("dve", vdec[:ns, sub, :, :], psv, vtab[:ns, sidx + sub, 0, :].unsqueeze(2).to_broadcast([ns, 4, 128]), ALU.mult,
                        [("bank", pb), "vtab"], [("vdec", sub)])
                if STOP <= 2.8:
                    continue
                for h in range(NHEAD):
                    self.tr(b7[:ns, h * 128:(h + 1) * 128], kT[:, h, sub * 128:sub * 128 + ns], self.ident_b,
                            [("kT", h), "ident_b"], [("bank", 7)])
                self.act(kTok[:ns, sub, :, :], b7[:ns, 0:512].rearrange("p (h e) -> p h e", h=4), AF.Copy, [("bank", 7)], [("kTok", sub)])
            if STOP <= 3:
                sidx += nsub
                continue
            for h in range(NHEAD):
                pb = 4 + self.nxt("scbank", 2)
                ps = self.banks[pb]
                for sub in range(nsub):
                    self.mm(ps[:ns, sub * 128:sub * 128 + ns], kT[:, h, sub * 128:sub * 128 + ns], qT[:, h, sub * 128:sub * 128 + ns], True, True,
                            [("kT", h), ("qT", h)], [("bank", pb)])
                dm = tab[:ns, TB_DM2 + h * 128:TB_DM2 + h * 128 + ns]
                self.tt("dve", sdT[:ns, h, 0:nsub, :ns], ps[:ns, 0:nsub * 128].rearrange("p (s i) -> p s i", i=128)[:, :, :ns],
                        dm.unsqueeze(1).to_broadcast([ns, nsub, ns]), ALU.mult, [("bank", pb), "tab"], [("sdT", h)])
            if STOP <= 4:
                sidx += nsub
                continue
            for sub in range(nsub):
                for h in range(NHEAD):
                    ob = self.banks[h]
                    self.mm(ob[:, sub * 128:sub * 128 + ns], vbf[:ns, sub, h, :], sdT[:ns, h, sub, :ns], True, n < 64,
                            [("vbf", sub), ("sdT", h)], [("bank", h)])
                nch = max(1, ns // 64)
                for cc in range(nch):
                    cw = min(ns, 64)
                    r0 = cc * 64
                    if n >= 64:
                        for h in range(NHEAD):
                            ob = self.banks[h]
                            cs = sub * 128 + cc * 64
                            self.mm(ob[:, cs:cs + 64], Sb[:, h, :], qdec[:, h, cs:cs + 64], False, True,
                                    ["Sb", ("qdec", h)], [("bank", h)])
                    for h in range(NHEAD):
                        self.mm(self.banks[6][:, h * 128:(h + 1) * 128], kTok[r0:r0 + cw, sub, h, :], vdec[r0:r0 + cw, sub, h, :], True, True,
                                [("kTok", sub), ("vdec", sub)], [("bank", 6)])
                    if n >= 64:
                        self.tt("dve", S[:, :, :], S[:, :, :], dtab, ALU.mult, ["S", "tab"], ["S"])
                    self.tt("dve", S[:, :, :], S[:, :, :], self.banks[6][:, :].rearrange("p (h e) -> p h e", h=4), ALU.add,
                            ["S", ("bank", 6)], ["S"])
                    self.act(Sb[:, :, :], S[:, :, :], AF.Copy, ["S"], ["Sb"])
            if STOP <= 5:
                sidx += nsub
                continue
            for h in range(NHEAD):
                ob = self.banks[h]
                q = self.nxt("sq", 2)
                self.act(self.sq[q][:, :n], ob[:, :n], AF.Square, [("bank", h)], [("sq", q)])
                self.act(self.yb[q][:, :n], ob[:, :n], AF.Copy, [("bank", h)], [("yb", q)])
                self.mm(self.banks[4][:, :n], self.ones_b, self.yb[q][:, :n], True, True, ["ones_b", ("yb", q)], [("bank", 4)])
                self.mm(self.banks[5][:, :n], self.ones_b, self.sq[q][:, :n], True, True, ["ones_b", ("sq", q)], [("bank", 5)])
                self.stats_finish(4, 5, n, 1.0 / 128, LN_EPS, mn, rd, nm, 0)
                self.tt("dve", t1[:, :n], ob[:, :n], rd[:, :n], ALU.mult, [("bank", h), ("rstd", 0)], ["t1"])
                self.tt("dve", t1[:, :n], t1[:, :n], nm[:, :n], ALU.add, ["t1", ("nmr", 0)], ["t1"])
                self.stt("dve", ycat[:, 4 + h, :n], t1[:, :n], par[:, PAR_GN_G + h:PAR_GN_G + h + 1], sgate[:, h, :n], ALU.mult, ALU.mult,
                         ["t1", ("sgate", h), f"par{l}"], [("ycat", 4 + h)])
            if STOP <= 6:
                sidx += nsub
                continue
            for fo in range(NFT):
                s = self.nxt("wsl", 4)
                self.dma("pool", wsl[s], w_outv[:, :, fo * 128:(fo + 1) * 128], [], [("wsl", s)], ("wsl", s))
                pb = self.nxt("dnbank", 2)
                py = self.banks[pb]
                for kt in range(NFT):
                    self.mm(py[:, :n], wsl[s][:, kt, :], ycat[:, kt, :n], kt == 0, kt == NFT - 1,
                            [("wsl", s), ("ycat", kt)], [("bank", pb)])
                self.ln_accum(0, fo, n, py, pb, 1.0 / ALPHA)
            self.ln_finish(l, 1, 0, ti, h_dst, eng=PENG)
            sidx += nsub
        P.barrier()
        A.reset(m0)

    def phase_final(self, h_src, out_d):
        A, P = self.A, self.P
        m0 = A.mark()
        xin = [A.alloc((8, 128), F32, "xin") for _ in range(2)]
        xo = [A.alloc((1024,), F32, "xo") for _ in range(2)]
        for s in range(BLK // 128):
            k = s % 2
            c0 = NPRE + s * 128
            self.dma("sp", xin[k], h_src[:, :, c0:c0 + 128], [], [("xin", k)], ("xin", k))
            for half in range(2):
                bk = self.nxt("finbank", 2)
                ps = self.banks[bk]
                for j in range(4):
                    ft = half * 4 + j
                    self.tr(ps[:, j * 128:(j + 1) * 128], xin[k][:, ft, :], self.ident_f, [("xin", k), "ident_f"], [("bank", bk)])
                self.act(xo[k][:, half * 512:(half + 1) * 512], ps[:, :], AF.Copy, [("bank", bk)], [("xo", k, half)])
            self.dma("sp", out_d[s * 128:(s + 1) * 128, :], xo[k], [("xo", k, 0), ("xo", k, 1)], [("xo", k, 0), ("xo", k, 1)], ("xo", k))
        P.barrier()
        A.reset(m0)


PAR_LN_G = 0
PAR_LN_B = 24
PAR_LNIN_G = 48
PAR_LNIN_B = 56
PAR_POOL_SCALE = 64
PAR_CONV_DB = 66
PAR_CONV_LN_G = 68
PAR_CONV_LN_B = 70
PAR_GN_G = 72
PAR_CONV_W = 76
NPAR = 138

TB_DM2 = 0
TB_QD = 512
TB_DTAB = 768
TB_COEF = 1280
TB_FLAG = 1292
TB_PC = 1294
TB_PCORR = 1302
NTAB = 1334


def pack_params(inp, l):
    p = np.zeros((128, NPAR), np.float32)
    for i in range(3):
        p[:, PAR_LN_G + 8 * i:PAR_LN_G + 8 * i + 8] = inp["ln_g"][l, i].reshape(8, 128).T
        p[:, PAR_LN_B + 8 * i:PAR_LN_B + 8 * i + 8] = inp["ln_b"][l, i].reshape(8, 128).T
    p[:, PAR_LNIN_G:PAR_LNIN_G + 8] = inp["ln_in_g"].reshape(8, 128).T
    p[:, PAR_LNIN_B:PAR_LNIN_B + 8] = inp["ln_in_b"].reshape(8, 128).T
    p[:, PAR_POOL_SCALE:PAR_POOL_SCALE + 2] = inp["pool_scale"][l].reshape(2, 128).T
    p[:, PAR_CONV_DB:PAR_CONV_DB + 2] = inp["conv_db"][l].reshape(2, 128).T
    p[:, PAR_CONV_LN_G:PAR_CONV_LN_G + 2] = inp["conv_ln_g"][l].reshape(2, 128).T
    p[:, PAR_CONV_LN_B:PAR_CONV_LN_B + 2] = inp["conv_ln_b"][l].reshape(2, 128).T
    p[:, PAR_GN_G:PAR_GN_G + 4] = inp["ret_gn_g"][l].reshape(4, 128).T
    cw = inp["conv_dw"][l]
    for tl in range(2):
        p[:, PAR_CONV_W + tl * 31:PAR_CONV_W + tl * 31 + 31] = cw[:, tl * 128:(tl + 1) * 128].T
    return p


def consts_arr():
    c = np.zeros((128, 256), np.float32)
    c[:, 0:128] = np.eye(128, dtype=np.float32)
    c[:, 128:256] = 1.0
    return c


def make_tables(jj):
    first = 1.0 if jj == 0 else 0.0
    g = np.array(GAMMAS, np.float64)
    pos = np.concatenate([np.arange(NPRE), NPRE + BLK * jj + np.arange(BLK)]).astype(np.float32)
    inv_freq = (np.float32(10000.0) ** (-np.arange(0, DH, 2, dtype=np.float32) / np.float32(DH))).astype(np.float32)
    ang = (pos[:, None] * inv_freq[None, :]).astype(np.float32)
    cos, sin = np.cos(ang).T, np.sin(ang).T
    rope = np.zeros((128, 2, T), np.float32)
    rope[0:64, 0], rope[64:128, 0] = cos, cos
    rope[0:64, 1], rope[64:128, 1] = sin, -sin
    vtab = np.zeros((128, 17, 2, 4), np.float64)
    ip = np.arange(16)
    for h in range(4):
        vtab[:16, 0, 0, h] = first * g[h] ** (15 - ip)
        vtab[:16, 0, 1, h] = first * g[h] ** (BLK + 15 - ip)
        for s in range(1, 17):
            nidx = (s - 1) * 128 + np.arange(128)
            vtab[:, s, 0, h] = g[h] ** (63 - (nidx % 64))
            vtab[:, s, 1, h] = g[h] ** (BLK - 1 - nidx)
    tab = np.zeros((128, NTAB), np.float64)
    j = np.arange(128)[:, None]
    i = np.arange(128)[None, :]
    same = (j // 64) == (i // 64)
    for h in range(4):
        tab[:, TB_DM2 + h * 128:TB_DM2 + (h + 1) * 128] = np.where(same, g[h] ** np.abs(i - j), 0.0) * QSCALE
        tab[:, TB_QD + h * 64:TB_QD + (h + 1) * 64] = (g[h] ** (np.arange(64) + 1.0))[None, :] * QSCALE
        tab[:, TB_DTAB + h * 128:TB_DTAB + (h + 1) * 128] = g[h] ** 64
        for s in range(3):
            tab[:, TB_COEF + 4 * s + h] = (g[h] ** (BLK * (jj - 1 - s))) if s < jj else 0.0
    tab[:, TB_FLAG] = first
    tab[:, TB_FLAG + 1] = 1.0 - first
    wins = (2, 4, 8, 16)
    for tl in range(2):
        for p in range(128):
            grp = (tl * 128 + p) // 64
            w = wins[grp]
            tab[p, TB_PC + tl * 4 + grp] = 1.0 / w
            tt_ = np.arange(16)
            tab[p, TB_PCORR + tl * 16:TB_PCORR + tl * 16 + 16] = w / np.minimum(tt_ + 1, w)
    return rope, vtab.reshape(128, 136).astype(np.float32), tab.astype(np.float32)


def _decl_common(B):
    B.setup_consts()
    rope = B.inp("rope", [128, 2, T])
    vtab = B.inp("vtab", [128, 136])
    tab = B.inp("tab", [128, NTAB])
    return rope, vtab, tab


def _w(B, l):
    return dict(
        ffn1_w13=B.inp(f"ffn1_w13_{l}", [D, 2 * DFF]), ffn1_w2=B.inp(f"ffn1_w2_{l}", [DFF, D]),
        ffn2_w13=B.inp(f"ffn2_w13_{l}", [D, 2 * DFF]), ffn2_w2=B.inp(f"ffn2_w2_{l}", [DFF, D]),
        w_in=B.inp(f"w_in_{l}", [D, DIN]), w_out=B.inp(f"w_out_{l}", [D, D]),
        pool_w=B.inp(f"pool_w_{l}", [4, 64, 64]), conv_pw=B.inp(f"conv_pw_{l}", [256, 256]))


def build_launch(kind):
    nc = bass.Bass("TRN2", target_bir_lowering=False)
    B = Builder(nc)
    if kind == 0:
        x = B.inp("x", [BLK, D])
        xpre = B.inp("xpre", [NPRE, D])
        rope, vtab, tab = _decl_common(B)
        w13, w2, w_in = B.inp("ffn1_w13_0", [D, 2 * DFF]), B.inp("ffn1_w2_0", [DFF, D]), B.inp("w_in_0", [D, DIN])
        h0 = B.scratch("h0", [128, NFT, T])
        h1 = B.outp("h1", [128, NFT, T])
        send = B.outp("send", [128, 640])
        B.phase_inln(x, xpre, h0)
        B.load_hb(h0)
        B.phase_ffn(0, w13, w2, 0, h0, h1)
        B.phase_kv(0, w_in, rope, vtab, send)
    else:
        l = kind - 1
        hin = B.inp("hin", [128, NFT, T])
        sprev = B.inp("sprev", [128, 3, 640])
        hprev = B.inp("hprev", [128, 128])
        rope, vtab, tab = _decl_common(B)
        w_in, w_out = B.inp(f"w_in_{l}", [D, DIN]), B.inp(f"w_out_{l}", [D, D])
        pool_w, conv_pw = B.inp(f"pool_w_{l}", [4, 64, 64]), B.inp(f"conv_pw_{l}", [256, 256])
        f2a, f2b = B.inp(f"ffn2_w13_{l}", [D, 2 * DFF]), B.inp(f"ffn2_w2_{l}", [DFF, D])
        hA = B.scratch("hA", [128, NFT, T])
        hB = B.scratch("hB", [128, NFT, T])
        B.load_hb(hin)
        B.phase_mix(l, w_in, pool_w, conv_pw, w_out, rope, vtab, tab, [sprev[:, i, 0:512] for i in range(3)], hprev[:, :], hin, hA)
        B.phase_ffn(l, f2a, f2b, 2, hA, hB)
        if l + 1 < DEPTH:
            w13, w2, w_in2 = B.inp(f"ffn1_w13_{l + 1}", [D, 2 * DFF]), B.inp(f"ffn1_w2_{l + 1}", [DFF, D]), B.inp(f"w_in_{l + 1}", [D, DIN])
            h1 = B.outp("h1", [128, NFT, T])
            send = B.outp("send", [128, 640])
            B.phase_ffn(l + 1, w13, w2, 0, hB, h1)
            B.phase_kv(l + 1, w_in2, rope, vtab, send)
        else:
            out = B.outp("out", [BLK, D])
            B.phase_final(hB, out)
    B.P.emit(nc)
    return nc, B


def build_mix_debug(l, stop, ntiles):
    nc = bass.Bass("TRN2", target_bir_lowering=False)
    B = Builder(nc)
    B.mix_stop, B.mix_tiles = stop, ntiles
    hin = B.inp("hin", [128, NFT, T])
    sprev = B.inp("sprev", [128, 3, 640])
    hprev = B.inp("hprev", [128, 128])
    rope, vtab, tab = _decl_common(B)
    w_in, w_out = B.inp(f"w_in_{l}", [D, DIN]), B.inp(f"w_out_{l}", [D, D])
    pool_w, conv_pw = B.inp(f"pool_w_{l}", [4, 64, 64]), B.inp(f"conv_pw_{l}", [256, 256])
    hA = B.outp("h1", [128, NFT, T])
    B.load_hb(hin)
    B.phase_mix(l, w_in, pool_w, conv_pw, w_out, rope, vtab, tab, [sprev[:, i, 0:512] for i in range(3)], hprev[:, :], hin, hA)
    B.P.emit(nc)
    return nc, B


def build_fused():
    nc = bass.Bass("TRN2", target_bir_lowering=False)
    B = Builder(nc)
    x = B.inp("x", [4 * BLK, D])
    xpre = B.inp("xpre", [4, NPRE, D])
    zeros = B.inp("zeros", [128, 640])
    B.setup_consts()
    ropes = [B.inp(f"rope{b}", [128, 2, T]) for b in range(4)]
    vtabs = [B.inp(f"vtab{b}", [128, 136]) for b in range(4)]
    tabs = [B.inp(f"tab{b}", [128, NTAB]) for b in range(4)]
    W = [_w(B, l) for l in range(DEPTH)]
    out = B.outp("out", [4 * BLK, D])
    hX = [B.scratch(f"hX{b}", [128, NFT, T]) for b in range(4)]
    hY = [B.scratch(f"hY{b}", [128, NFT, T]) for b in range(4)]
    hA = B.scratch("hA", [128, NFT, T])
    hB = B.scratch("hB", [128, NFT, T])
    send = [[B.scratch(f"send{l}_{b}", [128, 640]) for b in range(4)] for l in range(DEPTH)]
    for b in range(4):
        B.phase_inln(x[b * BLK:(b + 1) * BLK, :], xpre[b], hX[b])
    for b in range(4):
        B.load_hb(hX[b])
        B.phase_ffn(0, W[0]["ffn1_w13"], W[0]["ffn1_w2"], 0, hX[b], hY[b])
        B.phase_kv(0, W[0]["w_in"], ropes[b], vtabs[b], send[0][b])
    for l in range(DEPTH):
        for b in range(4):
            B.load_hb(hY[b])
            sp = [send[l][i][:, 0:512] if i < b else zeros[:, 0:512] for i in range(3)]
            hp = send[l][b - 1][:, 512:640] if b > 0 else zeros[:, 512:640]
            B.phase_mix(l, W[l]["w_in"], W[l]["pool_w"], W[l]["conv_pw"], W[l]["w_out"], ropes[b], vtabs[b], tabs[b], sp, hp, hY[b], hA)
            B.phase_ffn(l, W[l]["ffn2_w13"], W[l]["ffn2_w2"], 2, hA, hB)
            if l + 1 < DEPTH:
                B.phase_ffn(l + 1, W[l + 1]["ffn1_w13"], W[l + 1]["ffn1_w2"], 0, hB, hY[b])
                B.phase_kv(l + 1, W[l + 1]["w_in"], ropes[b], vtabs[b], send[l + 1][b])
            else:
                B.phase_final(hB, out[b * BLK:(b + 1) * BLK, :])
    B.P.emit(nc)
    return nc, B


def kernel_fused(inp):
    nc, B = _get("fused")
    cst = consts_arr()
    tabs = [make_tables(jj) for jj in range(4)]
    base = {"consts": cst, "zeros": np.zeros((128, 640), np.float32)}
    for l in range(DEPTH):
        base[f"par{l}"] = pack_params(inp, l)
        for n in ["ffn1_w13", "ffn1_w2", "ffn2_w13", "ffn2_w2", "w_in", "w_out", "pool_w", "conv_pw"]:
            base[f"{n}_{l}"] = np.ascontiguousarray(inp[n][l])
    for b in range(4):
        base[f"rope{b}"], base[f"vtab{b}"], base[f"tab{b}"] = tabs[b]
    xpre = np.zeros((4, NPRE, D), np.float32)
    xpre[0] = inp["meta"]
    maps = []
    for c in range(8):
        m = dict(base)
        m["x"] = np.ascontiguousarray(inp["x"][c % 2])
        m["xpre"] = xpre
        maps.append({k: m[k] for k in B.din})
    res = run_bass_kernel_spmd(nc, maps, core_ids=list(range(8))).results
    return np.stack([res[0]["out"], res[1]["out"]], axis=0).astype(np.float32)


_CACHE = {}


def _get(kind):
    if kind not in _CACHE:
        _CACHE[kind] = build_fused() if kind == "fused" else build_launch(kind)
    return _CACHE[kind]


FUSED = True


def _exchange(sends):
    sprevs, hprevs = [], []
    for c in range(8):
        bi, jj = divmod(c, 4)
        sp = np.zeros((128, 3, 640), np.float32)
        for s in range(jj):
            sp[:, s, :] = sends[bi * 4 + s]
        hp = np.zeros((128, 128), np.float32)
        if jj > 0:
            hp[:] = sends[c - 1][:, 512:640]
        sprevs.append(sp)
        hprevs.append(hp)
    return sprevs, hprevs


def kernel(**inputs):
    inp = {k: np.asarray(v) for k, v in inputs.items()}
    if FUSED:
        return kernel_fused(inp)
    x = inp["x"]
    cst = consts_arr()
    pars = [pack_params(inp, l) for l in range(DEPTH)]
    tabs = [make_tables(jj) for jj in range(4)]
    common = []
    for c in range(8):
        bi, jj = divmod(c, 4)
        rope, vtab, tab = tabs[jj]
        common.append({"consts": cst, "par0": pars[0], "par1": pars[1], "rope": rope, "vtab": vtab, "tab": tab})

    def wsel(names, l):
        return {f"{n}_{l}": np.ascontiguousarray(inp[n][l]) for n in names}

    nc, B = _get(0)
    maps = []
    for c in range(8):
        bi, jj = divmod(c, 4)
        m = dict(common[c])
        m["x"] = np.ascontiguousarray(x[bi, jj * BLK:(jj + 1) * BLK])
        m["xpre"] = inp["meta"] if jj == 0 else np.zeros((NPRE, D), np.float32)
        m.update(wsel(["ffn1_w13", "ffn1_w2", "w_in"], 0))
        maps.append({k: m[k] for k in B.din})
    res = run_bass_kernel_spmd(nc, maps, core_ids=list(range(8))).results
    out = None
    for l in range(DEPTH):
        nc, B = _get(l + 1)
        sprevs, hprevs = _exchange([res[c]["send"] for c in range(8)])
        maps = []
        for c in range(8):
            m = dict(common[c])
            m["hin"] = res[c]["h1"]
            m["sprev"] = sprevs[c]
            m["hprev"] = hprevs[c]
            m.update(wsel(["w_in", "w_out", "pool_w", "conv_pw", "ffn2_w13", "ffn2_w2"], l))
            if l + 1 < DEPTH:
                m.update(wsel(["ffn1_w13", "ffn1_w2", "w_in"], l + 1))
            maps.append({k: m[k] for k in B.din})
        res = run_bass_kernel_spmd(nc, maps, core_ids=list(range(8))).results
    out = np.stack([np.concatenate([res[bi * 4 + jj]["out"] for jj in range(4)], axis=0) for bi in range(2)], axis=0)
    return out.astype(np.float32)
```

```python
import numpy as np
from contextlib import ExitStack
import concourse.bass as bass
import concourse.mybir as mybir
from concourse.bass_utils import run_bass_kernel_spmd

F32 = mybir.dt.float32
BF16 = mybir.dt.bfloat16
AF = mybir.ActivationFunctionType
ALU = mybir.AluOpType

D = 1024
NFT = 8
DFF = 2752
NHT = 22
DIN = 2816
NPRE = 16
BLK = 2048
T = NPRE + BLK
TILES = [(0, 16), (16, 512), (528, 512), (1040, 512), (1552, 512)]
HALVES = [(0, 1, 2), (3, 4)]
HALF_OFF = [0, 1040]
HALF_LEN = [1040, 1024]
DEPTH = 2
ALPHA = (2.0 * DEPTH) ** 0.25
LN_EPS = 1e-5
EPS2 = LN_EPS / (ALPHA * ALPHA)
CHUNK = 64
NHEAD = 4
DH = 128
GAMMAS = [1.0 - 2.0 ** (-5.0 - h) for h in range(NHEAD)]
QSCALE = DH ** -0.5

ENGS = ("pe", "act", "dve", "pool", "sp")
import os
SAME_SYNC = os.environ.get("SAME_SYNC", "1") == "1"
PENG = os.environ.get("PENG", "dve")
SCHED = os.environ.get("SCHED", "1") == "1"


class Op:
    __slots__ = ("eng", "fn", "deps", "lane", "needs", "val", "stream", "seg", "dur", "idx", "succ", "pending", "ready", "fin")


class Prog:
    def __init__(self):
        self.ops = {e: [] for e in ENGS}
        self.lastw = {}
        self.readers = {}
        self.lane_cnt = {}
        self.last_in_stream = {}
        self.seg = 0
        self.count = 0

    def add(self, eng, fn, reads=(), writes=(), lane=None, dur=0.5):
        o = Op()
        o.eng, o.fn, o.lane, o.needs, o.val = eng, fn, lane, False, None
        o.seg, o.dur, o.idx = self.seg, dur, self.count
        self.count += 1
        o.stream = lane if lane is not None else eng
        deps = {}
        for r in reads:
            p = self.lastw.get(r)
            if p is not None:
                deps[id(p)] = p
            if isinstance(r, tuple) and r[0] == "bank":
                for p in self.readers.get(r, ()):
                    if p.stream != o.stream:
                        deps[id(p)] = p
        for r in writes:
            p = self.lastw.get(r)
            if p is not None:
                deps[id(p)] = p
            for p in self.readers.get(r, ()):
                deps[id(p)] = p
        o.deps = list(deps.values())
        for p in o.deps:
            p.needs = True
        for r in writes:
            self.lastw[r] = o
            self.readers[r] = []
        for r in reads:
            self.readers.setdefault(r, []).append(o)
        if lane is not None:
            self.lane_cnt[lane] = self.lane_cnt.get(lane, 0) + 16
            o.val = self.lane_cnt[lane]
        self.ops[eng].append(o)
        self.last_in_stream[o.stream] = o
        return o

    def barrier(self):
        lasts = list(self.last_in_stream.values())
        for p in lasts:
            p.needs = True
        for e in ENGS:
            o = Op()
            o.eng, o.fn, o.lane, o.needs, o.val = e, None, None, False, None
            o.stream = e
            o.seg, o.dur, o.idx = self.seg, 0.0, self.count
            o.deps = [p for p in lasts]
            self.ops[e].append(o)
        self.count += 1
        self.seg += 1
        self.lastw = {}
        self.readers = {}

    def schedule(self, window=24, hop=1.8):
        REORD = ("pe", "act", "dve")
        nseg = self.seg + 1
        per = {e: [[] for _ in range(nseg)] for e in ENGS}
        bars = {e: [None] * nseg for e in ENGS}
        for e in ENGS:
            for o in self.ops[e]:
                if o.fn is None:
                    bars[e][o.seg] = o
                else:
                    per[e][o.seg].append(o)
        new = {e: [] for e in ENGS}
        for sg in range(nseg):
            ops = [o for e in ENGS for o in per[e][sg]]
            for o in ops:
                o.succ, o.pending, o.ready, o.fin = [], 0, 0.0, None
            inseg = set(id(o) for o in ops)
            for o in ops:
                for p in o.deps:
                    if id(p) in inseg and p is not o:
                        p.succ.append(o)
                        o.pending += 1
            queues = {e: list(per[e][sg]) for e in ENGS}
            tfree = {e: 0.0 for e in ENGS}
            order = {e: [] for e in ENGS}
            remaining = len(ops)
            while remaining:
                best = None
                for e in ENGS:
                    q = queues[e]
                    if not q:
                        continue
                    lim = window if e in REORD else 1
                    cnt = 0
                    for k, o in enumerate(q):
                        if cnt >= lim:
                            break
                        cnt += 1
                        if o.pending:
                            continue
                        st = max(tfree[e], o.ready)
                        key = (st, o.idx)
                        if best is None or key < best[0]:
                            best = (key, e, k, o)
                if best is None:
                    raise RuntimeError("scheduler deadlock")
                (st, _), e, k, o = best
                del queues[e][k]
                order[e].append(o)
                if o.lane is not None:
                    tfree[e] = st + 0.3
                    o.fin = st + o.dur
                else:
                    o.fin = st + o.dur + (0.15 if e == "pe" else 0.0)
                    tfree[e] = st + o.dur
                for y in o.succ:
                    y.pending -= 1
                    if o.lane is None and o.eng == y.eng:
                        h = 0.0 if e == "pe" else 0.35
                    else:
                        h = hop
                    if o.fin + h > y.ready:
                        y.ready = o.fin + h
                remaining -= 1
            lasts = []
            for e in ENGS:
                if e in ("sp", "pool"):
                    seen = {}
                    for o in order[e]:
                        seen[o.stream] = o
                    lasts.extend(seen.values())
                elif order[e]:
                    lasts.append(order[e][-1])
            for p in lasts:
                p.needs = True
            for e in ENGS:
                new[e].extend(order[e])
                b = bars[e][sg]
                if b is not None:
                    b.deps = list(lasts)
                    new[e].append(b)
        self.ops = new

    def emit(self, nc):
        if SCHED:
            self.schedule()
        for e in ENGS:
            c = 0
            for o in self.ops[e]:
                if o.lane is None and o.needs and o.fn is not None:
                    c += 1
                    o.val = c
                elif o.lane is None:
                    o.val = None
        with ExitStack() as es:
            sems = {}
            for e in ENGS[:4]:
                sems[e] = es.enter_context(nc.semaphore("sem_" + e))
            for ln in self.lane_cnt:
                sems[ln] = es.enter_context(nc.semaphore("lane_" + str(ln)))
            block = es.enter_context(nc.Block())

            def run(e, engine):
                known = {}
                for o in self.ops[e]:
                    need = {}
                    for p in o.deps:
                        if p is o:
                            continue
                        if p.lane is not None and p.lane == o.lane:
                            continue
                        if p.lane is None:
                            if p.eng == e and (e == "pe" or not SAME_SYNC):
                                continue
                            if p.val is None:
                                continue
                        key = p.stream
                        if p.val > need.get(key, 0):
                            need[key] = p.val
                    for key, val in need.items():
                        if known.get(key, 0) >= val:
                            continue
                        engine.wait_ge(sems[key], val)
                        known[key] = val
                    if o.fn is None:
                        continue
                    inst = o.fn(engine)
                    if o.lane is not None:
                        inst.then_inc(sems[o.lane], 16)
                    elif o.needs:
                        inst.then_inc(sems[e], 1)

            @block.tensor
            def _(eng):
                run("pe", eng)

            @block.scalar
            def _(eng):
                run("act", eng)

            @block.vector
            def _(eng):
                run("dve", eng)

            @block.gpsimd
            def _(eng):
                run("pool", eng)

            @block.sync
            def _(eng):
                run("sp", eng)


class Arena:
    def __init__(self, t, nelem):
        self.t = t
        self.n = nelem
        self.off = 0
        self.uid = 0

    def mark(self):
        return self.off

    def reset(self, m):
        self.off = m

    def alloc(self, free_shape, dtype, name):
        n = int(np.prod(free_shape))
        if dtype == F32:
            self.off += self.off % 2
            ap = self.t[:, self.off:self.off + 2 * n].bitcast(F32)
            self.off += 2 * n
        else:
            ap = self.t[:, self.off:self.off + n]
            self.off += n
        assert self.off <= self.n, ("SBUF arena overflow", name, self.off, self.n)
        if len(free_shape) == 2:
            ap = ap.rearrange("p (a b) -> p a b", a=free_shape[0])
        elif len(free_shape) == 3:
            ap = ap.rearrange("p (a b c) -> p a b c", a=free_shape[0], b=free_shape[1])
        self.uid += 1
        return ap


class Builder:
    def __init__(self, nc):
        self.nc = nc
        self.P = Prog()
        self.es = ExitStack()
        ARENA_ELEMS = 103000
        at = self.es.enter_context(nc.sbuf_tensor("arena", [128, ARENA_ELEMS], BF16))
        self.A = Arena(at, ARENA_ELEMS)
        self.banks = [self.es.enter_context(nc.psum_tensor(f"bank{i}", [128, 512], F32)) for i in range(8)]
        self.din = {}
        self.dout = {}
        self.rot = {}

    def inp(self, name, shape):
        if name not in self.din:
            self.din[name] = self.nc.dram_tensor(name, list(shape), F32, kind="ExternalInput").ap()
        return self.din[name]

    def outp(self, name, shape):
        self.dout[name] = self.nc.dram_tensor(name, list(shape), F32, kind="ExternalOutput").ap()
        return self.dout[name]

    def scratch(self, name, shape):
        return self.nc.dram_tensor(name, list(shape), F32, kind="Internal").ap()

    @staticmethod
    def _fs(ap):
        return int(np.prod(ap.shape[1:]))

    def mm(self, out, lhsT, rhs, start, stop, reads, writes):
        return self.P.add("pe", lambda e: e.matmul(out, lhsT, rhs, start=start, stop=stop), reads, writes,
                          dur=max(self._fs(rhs), 64) * 0.00048 + 0.02)

    def tr(self, out, in_, ident, reads, writes):
        return self.P.add("pe", lambda e: e.transpose(out, in_, ident), reads, writes, dur=0.1)

    def act(self, out, in_, func, reads, writes, scale=1.0, bias=0.0):
        return self.P.add("act", lambda e: e.activation(out=out, in_=in_, func=func, bias=bias, scale=scale), reads, writes,
                          dur=self._fs(out) * 0.00095 + 0.22)

    def tt(self, eng, out, in0, in1, op, reads, writes):
        return self.P.add(eng, lambda e: e.tensor_tensor(out=out, in0=in0, in1=in1, op=op), reads, writes, dur=self._fs(out) * 0.00105 + 0.12)

    def ts(self, eng, out, in0, s1, s2, op0, op1, reads, writes):
        return self.P.add(eng, lambda e: e.tensor_scalar(out=out, in0=in0, scalar1=s1, scalar2=s2, op0=op0, op1=op1), reads, writes, dur=self._fs(out) * 0.00105 + 0.12)

    def stt(self, eng, out, in0, scalar, in1, op0, op1, reads, writes):
        return self.P.add(eng, lambda e: e.scalar_tensor_tensor(out=out, in0=in0, scalar=scalar, in1=in1, op0=op0, op1=op1), reads, writes, dur=self._fs(out) * 0.00105 + 0.12)

    def ts1(self, eng, out, in_, scalar, op, reads, writes):
        return self.P.add(eng, lambda e: e.tensor_single_scalar(out=out, in_=in_, scalar=scalar, op=op), reads, writes, dur=self._fs(out) * 0.00105 + 0.12)

    def copy(self, eng, out, in_, reads, writes):
        return self.P.add(eng, lambda e: e.tensor_copy(out=out, in_=in_), reads, writes, dur=self._fs(out) * 0.00105 + 0.12)

    def dma(self, eng, out, in_, reads, writes, lane):
        return self.P.add(eng, lambda e: e.dma_start(out=out, in_=in_), reads, writes, lane=lane,
                          dur=2.0 + int(np.prod(out.shape)) * 4 / 150e3)

    def rsqrt(self, out, in_, eps, reads, writes):
        self.act(out, in_, AF.Ln, reads, writes, scale=1.0, bias=eps)
        self.act(out, out, AF.Exp, writes, writes, scale=-0.5)

    def nxt(self, name, n):
        i = self.rot.get(name, 0)
        self.rot[name] = i + 1
        return i % n

    def setup_consts(self):
        A = self.A
        cst = self.inp("consts", [128, 256])
        self.ident_f = A.alloc((128,), F32, "ident_f")
        self.ident_b = A.alloc((128,), BF16, "ident_b")
        self.ones_b = A.alloc((128,), BF16, "ones_b")
        self.dma("sp", self.ident_f, cst[:, 0:128], [], ["ident_f"], "c0")
        self.dma("pool", self.ident_b, cst[:, 0:128], [], ["ident_b"], "c1")
        self.dma("pool", self.ones_b, cst[:, 128:256], [], ["ones_b"], "c1")
        self.par = []
        for l in range(DEPTH):
            p = A.alloc((NPAR,), F32, f"par{l}")
            self.dma("sp", p, self.inp(f"par{l}", [128, NPAR]), [], [f"par{l}"], "c0")
            self.par.append(p)

    def phase_inln(self, x_ap, xpre_ap, h_dst):
        A, P = self.A, self.P
        m0 = A.mark()
        xt = [A.alloc((1024,), F32, "xt") for _ in range(2)]
        xn = [A.alloc((1024,), F32, "xn") for _ in range(2)]
        st = [A.alloc((2, 6), F32, "st") for _ in range(2)]
        mv = [A.alloc((2,), F32, "mv") for _ in range(2)]
        rs = [A.alloc((1,), F32, "rs") for _ in range(2)]
        stage = [A.alloc((8, 128), F32, "stage") for _ in range(2)]
        par = self.par[0]
        subt = [(xpre_ap, 0, 16, 0)] + [(x_ap, s * 128, 128, NPRE + s * 128) for s in range(BLK // 128)]
        for i, (src, r0, n, toff) in enumerate(subt):
            k = i % 2
            self.dma("sp", xt[k][:n, :], src[r0:r0 + n, :], [], [("xt", k)], ("xt", k))
            for c in range(2):
                P.add("dve", lambda e, k=k, c=c, n=n: e.bn_stats(out=st[k][:n, c, :], in_=xt[k][:n, c * 512:(c + 1) * 512]),
                      [("xt", k)], [("st", k, c)])
            P.add("dve", lambda e, k=k, n=n: e.bn_aggr(out=mv[k][:n, :], in_=st[k][:n, :, :].rearrange("p a b -> p (a b)")),
                  [("st", k, 0), ("st", k, 1)], [("mv", k)])
            self.rsqrt(rs[k][:n, :], mv[k][:n, 1:2], LN_EPS, [("mv", k)], [("rs", k)])
            self.ts("dve", xn[k][:n, :], xt[k][:n, :], mv[k][:n, 0:1], rs[k][:n, 0:1], ALU.subtract, ALU.mult,
                    [("xt", k), ("mv", k), ("rs", k)], [("xn", k)])
            for half in range(2):
                bk = self.nxt("inln_bank", 2)
                ps = self.banks[bk]
                for j in range(4):
                    ft = half * 4 + j
                    self.tr(ps[:, j * 128:j * 128 + n], xn[k][:n, ft * 128:(ft + 1) * 128], self.ident_f[:n, :n],
                            [("xn", k), "ident_f"], [("bank", bk)])
                for j in range(4):
                    ft = half * 4 + j
                    self.act(stage[k][:, ft, :n], ps[:, j * 128:j * 128 + n], AF.Identity,
                             [("bank", bk), "par0"], [("stage", k)],
                             scale=par[:, PAR_LNIN_G + ft:PAR_LNIN_G + ft + 1], bias=par[:, PAR_LNIN_B + ft:PAR_LNIN_B + ft + 1])
            self.dma("sp", h_dst[:, :, toff:toff + n], stage[k][:, :, :n], [("stage", k)], [], ("stg", k))
        P.barrier()
        A.reset(m0)

    def load_hb(self, h_src):
        if not hasattr(self, "hb"):
            self.hb = self.A.alloc((NFT, T), BF16, "hb")
        for ti, (o, n) in enumerate(TILES):
            self.dma("pool", self.hb[:, :, o:o + n], h_src[:, :, o:o + n], [],
                     [("hb", ft, ti) for ft in range(NFT)], ("hbld", ti))

    def ln_alloc(self, ntile):
        A = self.A
        self.yp = [A.alloc((8, 512), F32, "yp") for _ in range(ntile)]
        self.sq = [A.alloc((512,), BF16, "sq") for _ in range(2)]
        self.yb = [A.alloc((512,), BF16, "yb") for _ in range(2)]
        self.mean = [A.alloc((512,), F32, "mean") for _ in range(ntile)]
        self.rstd = [A.alloc((512,), F32, "rstd") for _ in range(ntile)]
        self.nmr = [A.alloc((512,), F32, "nmr") for _ in range(ntile)]

    def ln_accum(self, li, fo, n, py, pb, cres):
        yp = self.yp[li]
        self.stt("dve", yp[:, fo, :n], py[:, :n], cres, yp[:, fo, :n], ALU.mult, ALU.add,
                 [("bank", pb), ("yp", li)], [("ypf", li, fo)])
        q = self.nxt("sq", 2)
        self.act(self.sq[q][:, :n], yp[:, fo, :n], AF.Square, [("ypf", li, fo)], [("sq", q)])
        self.act(self.yb[q][:, :n], yp[:, fo, :n], AF.Copy, [("ypf", li, fo)], [("yb", q)])
        bs, bq = 2 + 2 * li, 3 + 2 * li
        self.mm(self.banks[bs][:, :n], self.ones_b, self.yb[q][:, :n], fo == 0, fo == NFT - 1, ["ones_b", ("yb", q)], [("bank", bs)])
        self.mm(self.banks[bq][:, :n], self.ones_b, self.sq[q][:, :n], fo == 0, fo == NFT - 1, ["ones_b", ("sq", q)], [("bank", bq)])

    def stats_finish(self, bs, bq, n, inv, eps, mn, rd, nm, key):
        self.ts1("dve", mn[:, :n], self.banks[bs][:, :n], inv, ALU.mult, [("bank", bs)], [("mean", key)])
        self.tt("dve", nm[:, :n], mn[:, :n], mn[:, :n], ALU.mult, [("mean", key)], [("nmr", key)])
        self.stt("dve", rd[:, :n], self.banks[bq][:, :n], inv, nm[:, :n], ALU.mult, ALU.subtract,
                 [("bank", bq), ("nmr", key)], [("rstd", key)])
        self.rsqrt(rd[:, :n], rd[:, :n], eps, [("rstd", key)], [("rstd", key)])
        self.stt("dve", nm[:, :n], mn[:, :n], -1.0, rd[:, :n], ALU.mult, ALU.mult, [("mean", key), ("rstd", key)], [("nmr", key)])

    def ln_finish(self, l, lnidx, li, ti, h_dst, eng="dve"):
        par = self.par[l]
        gcol = PAR_LN_G + lnidx * 8
        bcol = PAR_LN_B + lnidx * 8
        o, n = TILES[ti]
        yp, hb = self.yp[li], self.hb
        bs, bq = 2 + 2 * li, 3 + 2 * li
        mn, rd, nm = self.mean[li], self.rstd[li], self.nmr[li]
        self.stats_finish(bs, bq, n, 1.0 / D, EPS2, mn, rd, nm, li)
        for fo in range(NFT):
            self.tt(eng, yp[:, fo, :n], yp[:, fo, :n], rd[:, :n], ALU.mult, [("ypf", li, fo), ("rstd", li)], [("ypf", li, fo)])
            self.tt(eng, yp[:, fo, :n], yp[:, fo, :n], nm[:, :n], ALU.add, [("ypf", li, fo), ("nmr", li)], [("ypf", li, fo)])
            self.act(hb[:, fo, o:o + n], yp[:, fo, :n], AF.Identity, [("ypf", li, fo), f"par{l}"], [("hb", fo, ti)],
                     scale=par[:, gcol + fo:gcol + fo + 1], bias=par[:, bcol + fo:bcol + fo + 1])
            self.act(yp[:, fo, :n], yp[:, fo, :n], AF.Identity, [("ypf", li, fo), f"par{l}"], [("ypf", li, fo)],
                     scale=par[:, gcol + fo:gcol + fo + 1], bias=par[:, bcol + fo:bcol + fo + 1])
        self.dma("sp", h_dst[:, :, o:o + n], yp[:, :, :n], [("ypf", li, fo) for fo in range(NFT)] + [("yp", li)],
                 [("yp", li)], ("ypst", li))

    def phase_ffn(self, l, w13, w2, lnidx, h_src, h_dst):
        A, P = self.A, self.P
        m0 = A.mark()
        hid = A.alloc((NHT, 1040), BF16, "hid")
        w13s = [A.alloc((8, 2, 128), BF16, "w13s") for _ in range(3)]
        w2s = [A.alloc((NHT, 128), BF16, "w2s") for _ in range(2)]
        self.ln_alloc(3)
        sg = [A.alloc((512,), F32, "sg") for _ in range(4)]
        hb = self.hb
        w13v = w13.rearrange("(kt p) c -> p kt c", p=128)
        CRES = 0.5 / ALPHA
        HZ = {0: (0,), 1: (1,), 2: (2,), 3: (0, 1), 4: (1, 2)}
        for hi, tiles in enumerate(HALVES):
            hoff = HALF_OFF[hi]
            for li, ti in enumerate(tiles):
                o, n = TILES[ti]
                self.dma("sp", self.yp[li][:, :, :n], h_src[:, :, o:o + n], [], [("yp", li)], ("ypld", li))
            for m in range(NHT):
                mw = 128 if m < NHT - 1 else 64
                s = self.nxt("w13s", 3)
                self.dma("pool", w13s[s][:, :, 0, :mw], w13v[:, :, m * 128:m * 128 + mw], [], [("w13s", s)], ("w13", s))
                self.dma("pool", w13s[s][:, :, 1, :mw], w13v[:, :, DFF + m * 128:DFF + m * 128 + mw], [], [("w13s", s)], ("w13", s))
                for ti in tiles:
                    o, n = TILES[ti]
                    pb = self.nxt("upbank", 4) * 2
                    pa, pu = self.banks[pb], self.banks[pb + 1]
                    for kt in range(NFT):
                        self.mm(pa[:mw, :n], w13s[s][:, kt, 0, :mw], hb[:, kt, o:o + n], kt == 0, kt == NFT - 1,
                                [("w13s", s), ("hb", kt, ti)], [("bank", pb)])
                    for kt in range(NFT):
                        self.mm(pu[:mw, :n], w13s[s][:, kt, 1, :mw], hb[:, kt, o:o + n], kt == 0, kt == NFT - 1,
                                [("w13s", s), ("hb", kt, ti)], [("bank", pb + 1)])
                    g = self.nxt("sg", 4)
                    self.act(sg[g][:mw, :n], pa[:mw, :n], AF.Silu, [("bank", pb)], [("sg", g)])
                    self.tt("dve", hid[:mw, m, o - hoff:o - hoff + n], sg[g][:mw, :n], pu[:mw, :n], ALU.mult,
                            [("sg", g), ("bank", pb + 1)], [("hid", m, z) for z in HZ[ti]])
            for fo in range(NFT):
                s = self.nxt("w2s", 2)
                self.dma("pool", w2s[s][:, 0:NHT - 1, :], w2[0:(NHT - 1) * 128, fo * 128:(fo + 1) * 128].rearrange("(kt p) c -> p kt c", p=128),
                         [], [("w2s", s)], ("w2", s))
                self.dma("pool", w2s[s][0:64, NHT - 1, :], w2[(NHT - 1) * 128:DFF, fo * 128:(fo + 1) * 128], [], [("w2s", s)], ("w2", s))
                for li, ti in enumerate(tiles):
                    o, n = TILES[ti]
                    pb = self.nxt("dnbank", 2)
                    py = self.banks[pb]
                    for m in range(NHT):
                        mw = 128 if m < NHT - 1 else 64
                        self.mm(py[:, :n], w2s[s][:mw, m, :], hid[:mw, m, o - hoff:o - hoff + n], m == 0, m == NHT - 1,
                                [("w2s", s)] + [("hid", m, z) for z in HZ[ti]], [("bank", pb)])
                    self.ln_accum(li, fo, n, py, pb, CRES)
            for li, ti in enumerate(tiles):
                self.ln_finish(l, lnidx, li, ti, h_dst)
        P.barrier()
        A.reset(m0)

    def rope(self, ps, pbkey, rp, rkey, n, dst, dkey, sum_eng="dve"):
        a, b = self.ropeA, self.ropeB
        self.tt("dve", a[:, :n], ps[:, :n], rp[:, 0, :n], ALU.mult, [pbkey, rkey], ["ropeA"])
        self.tt("dve", b[0:64, :n], ps[64:128, :n], rp[64:128, 1, :n], ALU.mult, [pbkey, rkey], ["ropeB0"])
        self.tt("dve", b[64:128, :n], ps[0:64, :n], rp[0:64, 1, :n], ALU.mult, [pbkey, rkey], ["ropeB1"])
        self.tt(sum_eng, dst, a[:, :n], b[:, :n], ALU.add, ["ropeA", "ropeB0", "ropeB1"], [dkey])

    def load_rope(self, rope_d, ti):
        o, n = TILES[ti]
        k = self.nxt("ropet", 2)
        self.dma("sp", self.ropet[k][:, :, :n], rope_d[:, :, o:o + n], [], [("ropet", k)], ("ropet", k))
        return self.ropet[k], ("ropet", k)

    def phase_kv(self, l, w_in, rope_d, vtab_d, send_d):
        A, P = self.A, self.P
        m0 = A.mark()
        hb = self.hb
        wk = A.alloc((8, 512), BF16, "wk")
        wv = A.alloc((8, 512), BF16, "wv")
        wh = A.alloc((8, 768), BF16, "wh")
        self.ropet = [A.alloc((2, 512), F32, "ropet") for _ in range(2)]
        self.ropeA = A.alloc((512,), F32, "ropeA")
        self.ropeB = A.alloc((512,), F32, "ropeB")
        vtab = A.alloc((17, 2, 4), F32, "vtab")
        kT = A.alloc((4, 512), BF16, "kT")
        vfull = [A.alloc((4, 128), BF16, "vfull") for _ in range(2)]
        kTok = [A.alloc((4, 128), BF16, "kTok") for _ in range(2)]
        send = A.alloc((640,), F32, "send")
        sgm = A.alloc((2, 32), F32, "sgm")
        w_inv = w_in.rearrange("(kt p) c -> p kt c", p=128)
        self.dma("pool", wk, w_inv[:, :, 1280:1792], [], ["wk"], "wk")
        self.dma("pool", wv, w_inv[:, :, 1792:2304], [], ["wv"], "wv")
        self.dma("pool", wh, w_inv[:, :, 0:768], [], ["wh"], "wh")
        self.dma("sp", vtab, vtab_d.rearrange("p (a b c) -> p a b c", a=17, b=2), [], ["vtab"], "vtab")
        b7 = self.banks[7][:, :].bitcast(BF16)
        SB = 4
        sidx = 0
        for ti, (o, n) in enumerate(TILES):
            rp, rkey = self.load_rope(rope_d, ti)
            for h in range(NHEAD):
                pb = self.nxt("kvbank", 2)
                ps = self.banks[pb]
                for kt in range(NFT):
                    self.mm(ps[:, :n], wk[:, kt, h * 128:(h + 1) * 128], hb[:, kt, o:o + n], kt == 0, kt == NFT - 1,
                            ["wk", ("hb", kt, ti)], [("bank", pb)])
                self.rope(ps, ("bank", pb), rp, rkey, n, kT[:, h, :n], ("kT", h))
            nsub = max(1, n // 128)
            for sub in range(nsub):
                ns = min(n, 128)
                c0 = o + sub * 128
                pb = 2 + self.nxt("kvbank2", 2)
                ps = self.banks[pb]
                for kt in range(NFT):
                    self.mm(ps[:ns, :], hb[:, kt, c0:c0 + ns], wv[:, kt, :], kt == 0, kt == NFT - 1,
                            ["wv", ("hb", kt, ti)], [("bank", pb)])
                k2 = self.nxt("vfull", 2)
                self.tt("dve", vfull[k2][:ns, :, :], ps[:ns, :].rearrange("p (h e) -> p h e", h=4),
                        vtab[:ns, sidx, 1, :].unsqueeze(2).to_broadcast([ns, 4, 128]), ALU.mult,
                        [("bank", pb), "vtab"], [("vfull", k2)])
                for h in range(NHEAD):
                    self.tr(b7[:ns, h * 128:(h + 1) * 128], kT[:, h, sub * 128:sub * 128 + ns], self.ident_b,
                            [("kT", h), "ident_b"], [("bank", 7)])
                self.act(kTok[k2][:ns, :, :], b7[:ns, 0:512].rearrange("p (h e) -> p h e", h=4), AF.Copy, [("bank", 7)], [("kTok", k2)])
                for h in range(NHEAD):
                    self.mm(self.banks[SB][:, h * 128:(h + 1) * 128], kTok[k2][:ns, h, :], vfull[k2][:ns, h, :], sidx == 0, sidx == 16,
                            [("kTok", k2), ("vfull", k2)], [("bank", SB)])
                sidx += 1
        HB = 5
        ps = self.banks[HB]
        for mt in range(6):
            for kt in range(NFT):
                self.mm(ps[:, mt * 32:(mt + 1) * 32], wh[:, kt, mt * 128:(mt + 1) * 128], hb[:, kt, T - 32:T], kt == 0, kt == NFT - 1,
                        ["wh", ("hb", kt, 4)], [("bank", HB)])
        self.act(send[:, 0:512], self.banks[SB][:, :], AF.Copy, [("bank", SB)], ["send_s"])
        self.act(sgm[:, :, :], ps[:, 128:192].rearrange("p (a b) -> p a b", a=2), AF.Sigmoid, [("bank", HB)], ["sgm"])
        self.tt("dve", send[:, 512:576].rearrange("p (a b) -> p a b", a=2), sgm[:, :, :], ps[:, 64:128].rearrange("p (a b) -> p a b", a=2),
                ALU.mult, ["sgm", ("bank", HB)], ["send_u"])
        self.act(send[:, 576:640], ps[:, 0:64], AF.Copy, [("bank", HB)], ["send_p"])
        self.dma("sp", send_d[:, :], send, ["send_s", "send_u", "send_p"], [], "send")
        P.barrier()
        A.reset(m0)

    def phase_mix(self, l, w_in, pool_w, conv_pw, w_out, rope_d, vtab_d, tab_d, sprev_srcs, hprev_src, h_src, h_dst):
        A, P = self.A, self.P
        m0 = A.mark()
        hb = self.hb
        par = self.par[l]
        tab = A.alloc((NTAB,), F32, "tab")
        vtab = A.alloc((17, 2, 4), F32, "vtab")
        self.ropet = [A.alloc((2, 512), F32, "ropet") for _ in range(2)]
        self.ropeA = A.alloc((512,), F32, "ropeA")
        self.ropeB = A.alloc((512,), F32, "ropeB")
        sprev = A.alloc((3, 512), F32, "sprev")
        hprev = A.alloc((128,), F32, "hprev")
        S = A.alloc((4, 128), F32, "S")
        Stmp = A.alloc((4, 128), F32, "Stmp")
        Sb = A.alloc((4, 128), BF16, "Sb")
        wsl = [A.alloc((8, 128), BF16, "wsl") for _ in range(4)]
        wv = A.alloc((8, 512), BF16, "wv")
        wbd = A.alloc((2, 128), BF16, "wbd")
        wpw = A.alloc((2, 256), BF16, "wpw")
        XP = A.alloc((2, 15 + 512), F32, "XP")
        S2e = A.alloc((526,), F32, "S2e")
        S4e = A.alloc((524,), F32, "S4e")
        S8e = A.alloc((520,), F32, "S8e")
        Mb = A.alloc((512,), F32, "Mb")
        ypool = A.alloc((2, 512), BF16, "ypool")
        U = A.alloc((2, 30 + 512), BF16, "U")
        Utmp = A.alloc((2, 30), BF16, "Utmp")
        cdiag = A.alloc((2, 31, 128), BF16, "cdiag")
        acc = A.alloc((2, 512), F32, "acc")
        sgt = A.alloc((512,), F32, "sgt")
        cact = A.alloc((2, 512), BF16, "cact")
        qrot = A.alloc((512,), F32, "qrot")
        qT = A.alloc((4, 512), BF16, "qT")
        qdec = A.alloc((4, 512), BF16, "qdec")
        kT = A.alloc((4, 512), BF16, "kT")
        sgate = A.alloc((4, 512), F32, "sgate")
        vbf = A.alloc((4, 4, 128), BF16, "vbf")
        vdec = A.alloc((4, 4, 128), BF16, "vdec")
        kTok = A.alloc((4, 4, 128), BF16, "kTok")
        sdT = A.alloc((4, 4, 128), BF16, "sdT")
        ycat = A.alloc((8, 512), BF16, "ycat")
        t1 = A.alloc((512,), F32, "t1")
        self.ln_alloc(1)
        b7 = self.banks[7][:, :].bitcast(BF16)
        w_inv = w_in.rearrange("(kt p) c -> p kt c", p=128)
        w_outv = w_out.rearrange("(kt p) c -> p kt c", p=128)

        def tcol(c0, w):
            return tab[:, c0:c0 + w]

        self.dma("sp", tab, tab_d[:, :], [], ["tab"], "tab")
        self.dma("sp", vtab, vtab_d.rearrange("p (a b c) -> p a b c", a=17, b=2), [], ["vtab"], "vtab")
        for i in range(3):
            self.dma("sp", sprev[:, i, :], sprev_srcs[i], [], ["sprev"], "sprev")
        self.dma("sp", hprev, hprev_src, [], ["hprev"], "hprev")
        self.dma("pool", wv, w_inv[:, :, 1792:2304], [], ["wv"], "wv")
        P.add("dve", lambda e: e.memset(wbd[:, :, :], 0.0), [], ["wbd"])
        for tl in range(2):
            for a in range(2):
                self.dma("pool", wbd[64 * a:64 * a + 64, tl, 64 * a:64 * a + 64], pool_w[2 * tl + a, :, :], ["wbd"], [("wbd2", tl, a)], "wbd")
        self.dma("pool", wpw, conv_pw.rearrange("(kt p) c -> p kt c", p=128), [], ["wpw"], "wpw")
        for tl in range(2):
            for j in range(31):
                self.ts1("dve", cdiag[:, tl, j, :], self.ident_f, par[:, PAR_CONV_W + tl * 31 + j:PAR_CONV_W + tl * 31 + j + 1], ALU.mult,
                         ["ident_f", f"par{l}"], [("cdiag", tl)])
        P.add("dve", lambda e: e.memset(XP[:, :, 0:15], 0.0), [], ["XPh"])
        P.add("dve", lambda e: e.memset(U[:, :, 0:30], 0.0), [], ["Uh"])
        for i in range(3):
            cb = tab[:, TB_COEF + 4 * i:TB_COEF + 4 * i + 4].unsqueeze(2).to_broadcast([128, 4, 128])
            src = sprev[:, i, :].rearrange("p (h e) -> p h e", h=4)
            if i == 0:
                self.tt("dve", S[:, :, :], src, cb, ALU.mult, ["sprev", "tab"], ["S"])
            else:
                self.tt("dve", Stmp[:, :, :], src, cb, ALU.mult, ["sprev", "tab"], ["Stmp"])
                self.tt("dve", S[:, :, :], S[:, :, :], Stmp[:, :, :], ALU.add, ["S", "Stmp"], ["S"])
        self.act(Sb[:, :, :], S[:, :, :], AF.Copy, ["S"], ["Sb"])
        dtab = tab[:, TB_DTAB:TB_DTAB + 512].rearrange("p (h e) -> p h e", h=4)
        ff = tab[:, TB_FLAG:TB_FLAG + 1]
        nf = tab[:, TB_FLAG + 1:TB_FLAG + 2]

        sidx = 0
        STOP = getattr(self, "mix_stop", 99)
        NT = getattr(self, "mix_tiles", len(TILES))
        for ti, (o, n) in enumerate(TILES[:NT]):
            rp, rkey = self.load_rope(rope_d, ti)
            self.dma("sp", self.yp[0][:, :, :n], h_src[:, :, o:o + n], [], [("yp", 0)], ("ypld", 0))
            nsub = max(1, n // 128)
            ns = min(n, 128)

            def proj(col0):
                s = self.nxt("wsl", 4)
                self.dma("pool", wsl[s], w_inv[:, :, col0:col0 + 128], [], [("wsl", s)], ("wsl", s))
                pb = self.nxt("pjbank", 3)
                ps = self.banks[pb]
                for kt in range(NFT):
                    self.mm(ps[:, :n], wsl[s][:, kt, :], hb[:, kt, o:o + n], kt == 0, kt == NFT - 1,
                            [("wsl", s), ("hb", kt, ti)], [("bank", pb)])
                return ps, ("bank", pb)

            for tl in range(2):
                ps, pk = proj(0 + tl * 128)
                self.act(XP[:, tl, 15:15 + n], ps[:, :n], AF.Copy, [pk], [("XP", tl)])
                x = XP[:, tl, :]
                hk = [("XP", tl), "XPh"]
                self.tt(PENG, S2e[:, 0:n + 14], x[:, 1:15 + n], x[:, 0:14 + n], ALU.add, hk, ["S2e"])
                self.tt(PENG, S4e[:, 0:n + 12], S2e[:, 2:n + 14], S2e[:, 0:n + 12], ALU.add, ["S2e"], ["S4e"])
                self.tt(PENG, S8e[:, 0:n + 8], S4e[:, 4:n + 12], S4e[:, 0:n + 8], ALU.add, ["S4e"], ["S8e"])
                pc = lambda w: tab[:, TB_PC + tl * 4 + w:TB_PC + tl * 4 + w + 1]
                self.ts1("dve", Mb[:, :n], S2e[:, 14:14 + n], pc(0), ALU.mult, ["S2e", "tab"], ["Mb"])
                self.stt("dve", Mb[:, :n], S4e[:, 12:12 + n], pc(1), Mb[:, :n], ALU.mult, ALU.add, ["S4e", "Mb", "tab"], ["Mb"])
                self.stt("dve", Mb[:, :n], S8e[:, 8:8 + n], pc(2), Mb[:, :n], ALU.mult, ALU.add, ["S8e", "Mb", "tab"], ["Mb"])
                self.stt("dve", Mb[:, :n], S8e[:, 8:8 + n], pc(3), Mb[:, :n], ALU.mult, ALU.add, ["S8e", "Mb", "tab"], ["Mb"])
                self.stt("dve", Mb[:, :n], S8e[:, 0:n], pc(3), Mb[:, :n], ALU.mult, ALU.add, ["S8e", "Mb", "tab"], ["Mb"])
                if ti == 0:
                    self.tt(PENG, Mb[:, :n], Mb[:, :n], tab[:, TB_PCORR + tl * 16:TB_PCORR + tl * 16 + 16], ALU.mult, ["Mb", "tab"], ["Mb"])
                self.tt("dve", ypool[:, tl, :n], Mb[:, :n], x[:, 15:15 + n], ALU.subtract, ["Mb", ("XP", tl)], [("ypool", tl)])
                self.copy("dve", XP[:, tl, 0:15], XP[:, tl, n:n + 15], [("XP", tl)], ["XPh", ("XP", tl)])
                if ti == 0:
                    self.ts1("dve", XP[:, tl, 0:15], XP[:, tl, 0:15], ff, ALU.mult, ["XPh", ("XP", tl), "tab"], ["XPh", ("XP", tl)])
                    self.stt("dve", XP[:, tl, 0:15], hprev[:, 64 + tl * 32 + 17:64 + tl * 32 + 32], nf, XP[:, tl, 0:15], ALU.mult, ALU.add,
                             ["hprev", "tab", "XPh", ("XP", tl)], ["XPh", ("XP", tl)])
                pb = 6
                self.mm(self.banks[pb][:, :n], wbd[:, tl, :], ypool[:, tl, :n], True, True,
                        [("wbd2", tl, 0), ("wbd2", tl, 1), "wbd", ("ypool", tl)], [("bank", pb)])
                self.act(ycat[:, tl, :n], self.banks[pb][:, :n], AF.Copy, [("bank", pb), f"par{l}"], [("ycat", tl)],
                         scale=par[:, PAR_POOL_SCALE + tl:PAR_POOL_SCALE + tl + 1])
            if STOP <= 1:
                continue
            for tl in range(2):
                psg, pkg = proj(512 + tl * 128)
                self.act(sgt[:, :n], psg[:, :n], AF.Sigmoid, [pkg], ["sgt"])
                psa, pka = proj(256 + tl * 128)
                self.tt("dve", U[:, tl, 30:30 + n], sgt[:, :n], psa[:, :n], ALU.mult, ["sgt", pka], [("U", tl)])
                uk = [("U", tl), "Uh", ("cdiag", tl)]
                cb = 4 + tl
                for j in range(31):
                    self.mm(self.banks[cb][:, :n], cdiag[:, tl, j, :], U[:, tl, j:j + n], j == 0, j == 30, uk, [("bank", cb)])
                self.act(acc[:, tl, :n], self.banks[cb][:, :n], AF.Identity, [("bank", cb), f"par{l}"], [("acc", tl)],
                         bias=par[:, PAR_CONV_DB + tl:PAR_CONV_DB + tl + 1])
                if n >= 30:
                    self.copy("dve", U[:, tl, 0:30], U[:, tl, n:n + 30], [("U", tl)], ["Uh", ("U", tl)])
                else:
                    self.copy("dve", Utmp[:, tl, :], U[:, tl, n:n + 30], [("U", tl), "Uh"], [("Utmp", tl)])
                    self.copy("dve", U[:, tl, 0:30], Utmp[:, tl, :], [("Utmp", tl)], ["Uh", ("U", tl)])
                if ti == 0:
                    self.ts1("dve", U[:, tl, 0:30], U[:, tl, 0:30], ff, ALU.mult, ["Uh", ("U", tl), "tab"], ["Uh", ("U", tl)])
                    self.stt("dve", U[:, tl, 0:30], hprev[:, tl * 32 + 2:tl * 32 + 32], nf, U[:, tl, 0:30], ALU.mult, ALU.add,
                             ["hprev", "tab", "Uh", ("U", tl)], ["Uh", ("U", tl)])
            for tl in range(2):
                q = self.nxt("sq", 2)
                self.act(self.sq[q][:, :n], acc[:, tl, :n], AF.Square, [("acc", tl)], [("sq", q)])
                self.act(self.yb[q][:, :n], acc[:, tl, :n], AF.Copy, [("acc", tl)], [("yb", q)])
                self.mm(self.banks[4][:, :n], self.ones_b, self.yb[q][:, :n], tl == 0, tl == 1, ["ones_b", ("yb", q)], [("bank", 4)])
                self.mm(self.banks[5][:, :n], self.ones_b, self.sq[q][:, :n], tl == 0, tl == 1, ["ones_b", ("sq", q)], [("bank", 5)])
            mn, rd, nm = self.mean[0], self.rstd[0], self.nmr[0]
            self.stats_finish(4, 5, n, 1.0 / 256, LN_EPS, mn, rd, nm, 0)
            for tl in range(2):
                self.tt("dve", acc[:, tl, :n], acc[:, tl, :n], rd[:, :n], ALU.mult, [("acc", tl), ("rstd", 0)], [("acc", tl)])
                self.tt("dve", acc[:, tl, :n], acc[:, tl, :n], nm[:, :n], ALU.add, [("acc", tl), ("nmr", 0)], [("acc", tl)])
                self.act(cact[:, tl, :n], acc[:, tl, :n], AF.Silu, [("acc", tl), f"par{l}"], [("cact", tl)],
                         scale=par[:, PAR_CONV_LN_G + tl:PAR_CONV_LN_G + tl + 1], bias=par[:, PAR_CONV_LN_B + tl:PAR_CONV_LN_B + tl + 1])
            for mt in range(2):
                pb = 6
                for kt in range(2):
                    self.mm(self.banks[pb][:, :n], wpw[:, kt, mt * 128:(mt + 1) * 128], cact[:, kt, :n], kt == 0, kt == 1,
                            ["wpw", ("cact", kt)], [("bank", pb)])
                self.act(ycat[:, 2 + mt, :n], self.banks[pb][:, :n], AF.Copy, [("bank", pb)], [("ycat", 2 + mt)])
            if STOP <= 2:
                continue
            for h in range(NHEAD):
                ps, pk = proj(768 + h * 128)
                self.rope(ps, pk, rp, rkey, n, qrot[:, :n], "qrot", sum_eng=PENG)
                self.act(qT[:, h, :n], qrot[:, :n], AF.Copy, ["qrot"], [("qT", h)])
                if n >= 64:
                    self.tt("dve", qdec[:, h, :n].rearrange("p (c i) -> p c i", i=64), qrot[:, :n].rearrange("p (c i) -> p c i", i=64),
                            tab[:, TB_QD + h * 64:TB_QD + h * 64 + 64].unsqueeze(1).to_broadcast([128, n // 64, 64]), ALU.mult,
                            ["qrot", "tab"], [("qdec", h)])
            if STOP <= 2.2:
                continue
            for h in range(NHEAD):
                ps, pk = proj(1280 + h * 128)
                self.rope(ps, pk, rp, rkey, n, kT[:, h, :n], ("kT", h), sum_eng=PENG)
            if STOP <= 2.4:
                continue
            for h in range(NHEAD):
                ps, pk = proj(2304 + h * 128)
                self.act(sgate[:, h, :n], ps[:, :n], AF.Silu, [pk], [("sgate", h)])
            if STOP <= 2.6:
                continue
            for sub in range(nsub):
                c0 = o + sub * 128
                pb = 3
                ps = self.banks[pb]
                for kt in range(NFT):
                    self.mm(ps[:ns, :], hb[:, kt, c0:c0 + ns], wv[:, kt, :], kt == 0, kt == NFT - 1,
                            ["wv", ("hb", kt, ti)], [("bank", pb)])
                psv = ps[:ns, :].rearrange("p (h e) -> p h e", h=4)
                self.act(vbf[:ns, sub, :, :], psv, AF.Copy, [("bank", pb)], [("vbf", sub)])
                self.tt("dve", vdec[:ns, sub, :, :], psv, vtab[:ns, sidx + sub, 0, :].unsqueeze(2).to_broadcast([ns, 4, 128]), ALU.mult,
                        [("bank", pb), "vtab"], [("vdec", sub)])
                if STOP <= 2.8:
                    continue
                for h in range(NHEAD):
                    self.tr(b7[:ns, h * 128:(h + 1) * 128], kT[:, h, sub * 128:sub * 128 + ns], self.ident_b,
                            [("kT", h), "ident_b"], [("bank", 7)])
                self.act(kTok[:ns, sub, :, :], b7[:ns, 0:512].rearrange("p (h e) -> p h e", h=4), AF.Copy, [("bank", 7)], [("kTok", sub)])
            if STOP <= 3:
                sidx += nsub
                continue
            for h in range(NHEAD):
                pb = 4 + self.nxt("scbank", 2)
                ps = self.banks[pb]
                for sub in range(nsub):
                    self.mm(ps[:ns, sub * 128:sub * 128 + ns], kT[:, h, sub * 128:sub * 128 + ns], qT[:, h, sub * 128:sub * 128 + ns], True, True,
                            [("kT", h), ("qT", h)], [("bank", pb)])
                dm = tab[:ns, TB_DM2 + h * 128:TB_DM2 + h * 128 + ns]
                self.tt("dve", sdT[:ns, h, 0:nsub, :ns], ps[:ns, 0:nsub * 128].rearrange("p (s i) -> p s i", i=128)[:, :, :ns],
                        dm.unsqueeze(1).to_broadcast([ns, nsub, ns]), ALU.mult, [("bank", pb), "tab"], [("sdT", h)])
            if STOP <= 4:
                sidx += nsub
                continue
            for sub in range(nsub):
                for h in range(NHEAD):
                    ob = self.banks[h]
                    self.mm(ob[:, sub * 128:sub * 128 + ns], vbf[:ns, sub, h, :], sdT[:ns, h, sub, :ns], True, n < 64,
                            [("vbf", sub), ("sdT", h)], [("bank", h)])
                nch = max(1, ns // 64)
                for cc in range(nch):
                    cw = min(ns, 64)
                    r0 = cc * 64
                    if n >= 64:
                        for h in range(NHEAD):
                            ob = self.banks[h]
                            cs = sub * 128 + cc * 64
                            self.mm(ob[:, cs:cs + 64], Sb[:, h, :], qdec[:, h, cs:cs + 64], False, True,
                                    ["Sb", ("qdec", h)], [("bank", h)])
                    for h in range(NHEAD):
                        self.mm(self.banks[6][:, h * 128:(h + 1) * 128], kTok[r0:r0 + cw, sub, h, :], vdec[r0:r0 + cw, sub, h, :], True, True,
                                [("kTok", sub), ("vdec", sub)], [("bank", 6)])
                    if n >= 64:
                        self.tt("dve", S[:, :, :], S[:, :, :], dtab, ALU.mult, ["S", "tab"], ["S"])
                    self.tt("dve", S[:, :, :], S[:, :, :], self.banks[6][:, :].rearrange("p (h e) -> p h e", h=4), ALU.add,
                            ["S", ("bank", 6)], ["S"])
                    self.act(Sb[:, :, :], S[:, :, :], AF.Copy, ["S"], ["Sb"])
            if STOP <= 5:
                sidx += nsub
                continue
            for h in range(NHEAD):
                ob = self.banks[h]
                q = self.nxt("sq", 2)
                self.act(self.sq[q][:, :n], ob[:, :n], AF.Square, [("bank", h)], [("sq", q)])
                self.act(self.yb[q][:, :n], ob[:, :n], AF.Copy, [("bank", h)], [("yb", q)])
                self.mm(self.banks[4][:, :n], self.ones_b, self.yb[q][:, :n], True, True, ["ones_b", ("yb", q)], [("bank", 4)])
                self.mm(self.banks[5][:, :n], self.ones_b, self.sq[q][:, :n], True, True, ["ones_b", ("sq", q)], [("bank", 5)])
                self.stats_finish(4, 5, n, 1.0 / 128, LN_EPS, mn, rd, nm, 0)
                self.tt("dve", t1[:, :n], ob[:, :n], rd[:, :n], ALU.mult, [("bank", h), ("rstd", 0)], ["t1"])
                self.tt("dve", t1[:, :n], t1[:, :n], nm[:, :n], ALU.add, ["t1", ("nmr", 0)], ["t1"])
                self.stt("dve", ycat[:, 4 + h, :n], t1[:, :n], par[:, PAR_GN_G + h:PAR_GN_G + h + 1], sgate[:, h, :n], ALU.mult, ALU.mult,
                         ["t1", ("sgate", h), f"par{l}"], [("ycat", 4 + h)])
            if STOP <= 6:
                sidx += nsub
                continue
            for fo in range(NFT):
                s = self.nxt("wsl", 4)
                self.dma("pool", wsl[s], w_outv[:, :, fo * 128:(fo + 1) * 128], [], [("wsl", s)], ("wsl", s))
                pb = self.nxt("dnbank", 2)
                py = self.banks[pb]
                for kt in range(NFT):
                    self.mm(py[:, :n], wsl[s][:, kt, :], ycat[:, kt, :n], kt == 0, kt == NFT - 1,
                            [("wsl", s), ("ycat", kt)], [("bank", pb)])
                self.ln_accum(0, fo, n, py, pb, 1.0 / ALPHA)
            self.ln_finish(l, 1, 0, ti, h_dst, eng=PENG)
            sidx += nsub
        P.barrier()
        A.reset(m0)

    def phase_final(self, h_src, out_d):
        A, P = self.A, self.P
        m0 = A.mark()
        xin = [A.alloc((8, 128), F32, "xin") for _ in range(2)]
        xo = [A.alloc((1024,), F32, "xo") for _ in range(2)]
        for s in range(BLK // 128):
            k = s % 2
            c0 = NPRE + s * 128
            self.dma("sp", xin[k], h_src[:, :, c0:c0 + 128], [], [("xin", k)], ("xin", k))
            for half in range(2):
                bk = self.nxt("finbank", 2)
                ps = self.banks[bk]
                for j in range(4):
                    ft = half * 4 + j
                    self.tr(ps[:, j * 128:(j + 1) * 128], xin[k][:, ft, :], self.ident_f, [("xin", k), "ident_f"], [("bank", bk)])
                self.act(xo[k][:, half * 512:(half + 1) * 512], ps[:, :], AF.Copy, [("bank", bk)], [("xo", k, half)])
            self.dma("sp", out_d[s * 128:(s + 1) * 128, :], xo[k], [("xo", k, 0), ("xo", k, 1)], [("xo", k, 0), ("xo", k, 1)], ("xo", k))
        P.barrier()
        A.reset(m0)


PAR_LN_G = 0
PAR_LN_B = 24
PAR_LNIN_G = 48
PAR_LNIN_B = 56
PAR_POOL_SCALE = 64
PAR_CONV_DB = 66
PAR_CONV_LN_G = 68
PAR_CONV_LN_B = 70
PAR_GN_G = 72
PAR_CONV_W = 76
NPAR = 138

TB_DM2 = 0
TB_QD = 512
TB_DTAB = 768
TB_COEF = 1280
TB_FLAG = 1292
TB_PC = 1294
TB_PCORR = 1302
NTAB = 1334


def pack_params(inp, l):
    p = np.zeros((128, NPAR), np.float32)
    for i in range(3):
        p[:, PAR_LN_G + 8 * i:PAR_LN_G + 8 * i + 8] = inp["ln_g"][l, i].reshape(8, 128).T
        p[:, PAR_LN_B + 8 * i:PAR_LN_B + 8 * i + 8] = inp["ln_b"][l, i].reshape(8, 128).T
    p[:, PAR_LNIN_G:PAR_LNIN_G + 8] = inp["ln_in_g"].reshape(8, 128).T
    p[:, PAR_LNIN_B:PAR_LNIN_B + 8] = inp["ln_in_b"].reshape(8, 128).T
    p[:, PAR_POOL_SCALE:PAR_POOL_SCALE + 2] = inp["pool_scale"][l].reshape(2, 128).T
    p[:, PAR_CONV_DB:PAR_CONV_DB + 2] = inp["conv_db"][l].reshape(2, 128).T
    p[:, PAR_CONV_LN_G:PAR_CONV_LN_G + 2] = inp["conv_ln_g"][l].reshape(2, 128).T
    p[:, PAR_CONV_LN_B:PAR_CONV_LN_B + 2] = inp["conv_ln_b"][l].reshape(2, 128).T
    p[:, PAR_GN_G:PAR_GN_G + 4] = inp["ret_gn_g"][l].reshape(4, 128).T
    cw = inp["conv_dw"][l]
    for tl in range(2):
        p[:, PAR_CONV_W + tl * 31:PAR_CONV_W + tl * 31 + 31] = cw[:, tl * 128:(tl + 1) * 128].T
    return p


def consts_arr():
    c = np.zeros((128, 256), np.float32)
    c[:, 0:128] = np.eye(128, dtype=np.float32)
    c[:, 128:256] = 1.0
    return c


def make_tables(jj):
    first = 1.0 if jj == 0 else 0.0
    g = np.array(GAMMAS, np.float64)
    pos = np.concatenate([np.arange(NPRE), NPRE + BLK * jj + np.arange(BLK)]).astype(np.float32)
    inv_freq = (np.float32(10000.0) ** (-np.arange(0, DH, 2, dtype=np.float32) / np.float32(DH))).astype(np.float32)
    ang = (pos[:, None] * inv_freq[None, :]).astype(np.float32)
    cos, sin = np.cos(ang).T, np.sin(ang).T
    rope = np.zeros((128, 2, T), np.float32)
    rope[0:64, 0], rope[64:128, 0] = cos, cos
    rope[0:64, 1], rope[64:128, 1] = sin, -sin
    vtab = np.zeros((128, 17, 2, 4), np.float64)
    ip = np.arange(16)
    for h in range(4):
        vtab[:16, 0, 0, h] = first * g[h] ** (15 - ip)
        vtab[:16, 0, 1, h] = first * g[h] ** (BLK + 15 - ip)
        for s in range(1, 17):
            nidx = (s - 1) * 128 + np.arange(128)
            vtab[:, s, 0, h] = g[h] ** (63 - (nidx % 64))
            vtab[:, s, 1, h] = g[h] ** (BLK - 1 - nidx)
    tab = np.zeros((128, NTAB), np.float64)
    j = np.arange(128)[:, None]
    i = np.arange(128)[None, :]
    same = (j // 64) == (i // 64)
    for h in range(4):
        tab[:, TB_DM2 + h * 128:TB_DM2 + (h + 1) * 128] = np.where(same, g[h] ** np.abs(i - j), 0.0) * QSCALE
        tab[:, TB_QD + h * 64:TB_QD + (h + 1) * 64] = (g[h] ** (np.arange(64) + 1.0))[None, :] * QSCALE
        tab[:, TB_DTAB + h * 128:TB_DTAB + (h + 1) * 128] = g[h] ** 64
        for s in range(3):
            tab[:, TB_COEF + 4 * s + h] = (g[h] ** (BLK * (jj - 1 - s))) if s < jj else 0.0
    tab[:, TB_FLAG] = first
    tab[:, TB_FLAG + 1] = 1.0 - first
    wins = (2, 4, 8, 16)
    for tl in range(2):
        for p in range(128):
            grp = (tl * 128 + p) // 64
            w = wins[grp]
            tab[p, TB_PC + tl * 4 + grp] = 1.0 / w
            tt_ = np.arange(16)
            tab[p, TB_PCORR + tl * 16:TB_PCORR + tl * 16 + 16] = w / np.minimum(tt_ + 1, w)
    return rope, vtab.reshape(128, 136).astype(np.float32), tab.astype(np.float32)


def _decl_common(B):
    B.setup_consts()
    rope = B.inp("rope", [128, 2, T])
    vtab = B.inp("vtab", [128, 136])
    tab = B.inp("tab", [128, NTAB])
    return rope, vtab, tab


def _w(B, l):
    return dict(
        ffn1_w13=B.inp(f"ffn1_w13_{l}", [D, 2 * DFF]), ffn1_w2=B.inp(f"ffn1_w2_{l}", [DFF, D]),
        ffn2_w13=B.inp(f"ffn2_w13_{l}", [D, 2 * DFF]), ffn2_w2=B.inp(f"ffn2_w2_{l}", [DFF, D]),
        w_in=B.inp(f"w_in_{l}", [D, DIN]), w_out=B.inp(f"w_out_{l}", [D, D]),
        pool_w=B.inp(f"pool_w_{l}", [4, 64, 64]), conv_pw=B.inp(f"conv_pw_{l}", [256, 256]))


def build_launch(kind):
    nc = bass.Bass("TRN2", target_bir_lowering=False)
    B = Builder(nc)
    if kind == 0:
        x = B.inp("x", [BLK, D])
        xpre = B.inp("xpre", [NPRE, D])
        rope, vtab, tab = _decl_common(B)
        w13, w2, w_in = B.inp("ffn1_w13_0", [D, 2 * DFF]), B.inp("ffn1_w2_0", [DFF, D]), B.inp("w_in_0", [D, DIN])
        h0 = B.scratch("h0", [128, NFT, T])
        h1 = B.outp("h1", [128, NFT, T])
        send = B.outp("send", [128, 640])
        B.phase_inln(x, xpre, h0)
        B.load_hb(h0)
        B.phase_ffn(0, w13, w2, 0, h0, h1)
        B.phase_kv(0, w_in, rope, vtab, send)
    else:
        l = kind - 1
        hin = B.inp("hin", [128, NFT, T])
        sprev = B.inp("sprev", [128, 3, 640])
        hprev = B.inp("hprev", [128, 128])
        rope, vtab, tab = _decl_common(B)
        w_in, w_out = B.inp(f"w_in_{l}", [D, DIN]), B.inp(f"w_out_{l}", [D, D])
        pool_w, conv_pw = B.inp(f"pool_w_{l}", [4, 64, 64]), B.inp(f"conv_pw_{l}", [256, 256])
        f2a, f2b = B.inp(f"ffn2_w13_{l}", [D, 2 * DFF]), B.inp(f"ffn2_w2_{l}", [DFF, D])
        hA = B.scratch("hA", [128, NFT, T])
        hB = B.scratch("hB", [128, NFT, T])
        B.load_hb(hin)
        B.phase_mix(l, w_in, pool_w, conv_pw, w_out, rope, vtab, tab, [sprev[:, i, 0:512] for i in range(3)], hprev[:, :], hin, hA)
        B.phase_ffn(l, f2a, f2b, 2, hA, hB)
        if l + 1 < DEPTH:
            w13, w2, w_in2 = B.inp(f"ffn1_w13_{l + 1}", [D, 2 * DFF]), B.inp(f"ffn1_w2_{l + 1}", [DFF, D]), B.inp(f"w_in_{l + 1}", [D, DIN])
            h1 = B.outp("h1", [128, NFT, T])
            send = B.outp("send", [128, 640])
            B.phase_ffn(l + 1, w13, w2, 0, hB, h1)
            B.phase_kv(l + 1, w_in2, rope, vtab, send)
        else:
            out = B.outp("out", [BLK, D])
            B.phase_final(hB, out)
    B.P.emit(nc)
    return nc, B


def build_mix_debug(l, stop, ntiles):
    nc = bass.Bass("TRN2", target_bir_lowering=False)
    B = Builder(nc)
    B.mix_stop, B.mix_tiles = stop, ntiles
    hin = B.inp("hin", [128, NFT, T])
    sprev = B.inp("sprev", [128, 3, 640])
    hprev = B.inp("hprev", [128, 128])
    rope, vtab, tab = _decl_common(B)
    w_in, w_out = B.inp(f"w_in_{l}", [D, DIN]), B.inp(f"w_out_{l}", [D, D])
    pool_w, conv_pw = B.inp(f"pool_w_{l}", [4, 64, 64]), B.inp(f"conv_pw_{l}", [256, 256])
    hA = B.outp("h1", [128, NFT, T])
    B.load_hb(hin)
    B.phase_mix(l, w_in, pool_w, conv_pw, w_out, rope, vtab, tab, [sprev[:, i, 0:512] for i in range(3)], hprev[:, :], hin, hA)
    B.P.emit(nc)
    return nc, B


def build_fused():
    nc = bass.Bass("TRN2", target_bir_lowering=False)
    B = Builder(nc)
    x = B.inp("x", [4 * BLK, D])
    xpre = B.inp("xpre", [4, NPRE, D])
    zeros = B.inp("zeros", [128, 640])
    B.setup_consts()
    ropes = [B.inp(f"rope{b}", [128, 2, T]) for b in range(4)]
    vtabs = [B.inp(f"vtab{b}", [128, 136]) for b in range(4)]
    tabs = [B.inp(f"tab{b}", [128, NTAB]) for b in range(4)]
    W = [_w(B, l) for l in range(DEPTH)]
    out = B.outp("out", [4 * BLK, D])
    hX = [B.scratch(f"hX{b}", [128, NFT, T]) for b in range(4)]
    hY = [B.scratch(f"hY{b}", [128, NFT, T]) for b in range(4)]
    hA = B.scratch("hA", [128, NFT, T])
    hB = B.scratch("hB", [128, NFT, T])
    send = [[B.scratch(f"send{l}_{b}", [128, 640]) for b in range(4)] for l in range(DEPTH)]
    for b in range(4):
        B.phase_inln(x[b * BLK:(b + 1) * BLK, :], xpre[b], hX[b])
    for b in range(4):
        B.load_hb(hX[b])
        B.phase_ffn(0, W[0]["ffn1_w13"], W[0]["ffn1_w2"], 0, hX[b], hY[b])
        B.phase_kv(0, W[0]["w_in"], ropes[b], vtabs[b], send[0][b])
    for l in range(DEPTH):
        for b in range(4):
            B.load_hb(hY[b])
            sp = [send[l][i][:, 0:512] if i < b else zeros[:, 0:512] for i in range(3)]
            hp = send[l][b - 1][:, 512:640] if b > 0 else zeros[:, 512:640]
            B.phase_mix(l, W[l]["w_in"], W[l]["pool_w"], W[l]["conv_pw"], W[l]["w_out"], ropes[b], vtabs[b], tabs[b], sp, hp, hY[b], hA)
            B.phase_ffn(l, W[l]["ffn2_w13"], W[l]["ffn2_w2"], 2, hA, hB)
            if l + 1 < DEPTH:
                B.phase_ffn(l + 1, W[l + 1]["ffn1_w13"], W[l + 1]["ffn1_w2"], 0, hB, hY[b])
                B.phase_kv(l + 1, W[l + 1]["w_in"], ropes[b], vtabs[b], send[l + 1][b])
            else:
                B.phase_final(hB, out[b * BLK:(b + 1) * BLK, :])
    B.P.emit(nc)
    return nc, B


def kernel_fused(inp):
    nc, B = _get("fused")
    cst = consts_arr()
    tabs = [make_tables(jj) for jj in range(4)]
    base = {"consts": cst, "zeros": np.zeros((128, 640), np.float32)}
    for l in range(DEPTH):
        base[f"par{l}"] = pack_params(inp, l)
        for n in ["ffn1_w13", "ffn1_w2", "ffn2_w13", "ffn2_w2", "w_in", "w_out", "pool_w", "conv_pw"]:
            base[f"{n}_{l}"] = np.ascontiguousarray(inp[n][l])
    for b in range(4):
        base[f"rope{b}"], base[f"vtab{b}"], base[f"tab{b}"] = tabs[b]
    xpre = np.zeros((4, NPRE, D), np.float32)
    xpre[0] = inp["meta"]
    maps = []
    for c in range(8):
        m = dict(base)
        m["x"] = np.ascontiguousarray(inp["x"][c % 2])
        m["xpre"] = xpre
        maps.append({k: m[k] for k in B.din})
    res = run_bass_kernel_spmd(nc, maps, core_ids=list(range(8))).results
    return np.stack([res[0]["out"], res[1]["out"]], axis=0).astype(np.float32)


_CACHE = {}


def _get(kind):
    if kind not in _CACHE:
        _CACHE[kind] = build_fused() if kind == "fused" else build_launch(kind)
    return _CACHE[kind]


FUSED = True


def _exchange(sends):
    sprevs, hprevs = [], []
    for c in range(8):
        bi, jj = divmod(c, 4)
        sp = np.zeros((128, 3, 640), np.float32)
        for s in range(jj):
            sp[:, s, :] = sends[bi * 4 + s]
        hp = np.zeros((128, 128), np.float32)
        if jj > 0:
            hp[:] = sends[c - 1][:, 512:640]
        sprevs.append(sp)
        hprevs.append(hp)
    return sprevs, hprevs


def kernel(**inputs):
    inp = {k: np.asarray(v) for k, v in inputs.items()}
    if FUSED:
        return kernel_fused(inp)
    x = inp["x"]
    cst = consts_arr()
    pars = [pack_params(inp, l) for l in range(DEPTH)]
    tabs = [make_tables(jj) for jj in range(4)]
    common = []
    for c in range(8):
        bi, jj = divmod(c, 4)
        rope, vtab, tab = tabs[jj]
        common.append({"consts": cst, "par0": pars[0], "par1": pars[1], "rope": rope, "vtab": vtab, "tab": tab})

    def wsel(names, l):
        return {f"{n}_{l}": np.ascontiguousarray(inp[n][l]) for n in names}

    nc, B = _get(0)
    maps = []
    for c in range(8):
        bi, jj = divmod(c, 4)
        m = dict(common[c])
        m["x"] = np.ascontiguousarray(x[bi, jj * BLK:(jj + 1) * BLK])
        m["xpre"] = inp["meta"] if jj == 0 else np.zeros((NPRE, D), np.float32)
        m.update(wsel(["ffn1_w13", "ffn1_w2", "w_in"], 0))
        maps.append({k: m[k] for k in B.din})
    res = run_bass_kernel_spmd(nc, maps, core_ids=list(range(8))).results
    out = None
    for l in range(DEPTH):
        nc, B = _get(l + 1)
        sprevs, hprevs = _exchange([res[c]["send"] for c in range(8)])
        maps = []
        for c in range(8):
            m = dict(common[c])
            m["hin"] = res[c]["h1"]
            m["sprev"] = sprevs[c]
            m["hprev"] = hprevs[c]
            m.update(wsel(["w_in", "w_out", "pool_w", "conv_pw", "ffn2_w13", "ffn2_w2"], l))
            if l + 1 < DEPTH:
                m.update(wsel(["ffn1_w13", "ffn1_w2", "w_in"], l + 1))
            maps.append({k: m[k] for k in B.din})
        res = run_bass_kernel_spmd(nc, maps, core_ids=list(range(8))).results
    out = np.stack([np.concatenate([res[bi * 4 + jj]["out"] for jj in range(4)], axis=0) for bi in range(2)], axis=0)
    return out.astype(np.float32)
```

```python
import numpy as np
from contextlib import ExitStack
import concourse.bass as bass
import concourse.mybir as mybir
from concourse.bass_utils import run_bass_kernel_spmd

F32 = mybir.dt.float32
BF16 = mybir.dt.bfloat16
AF = mybir.ActivationFunctionType
ALU = mybir.AluOpType

D = 1024
NFT = 8
DFF = 2752
NHT = 22
DIN = 2816
NPRE = 16
BLK = 2048
T = NPRE + BLK
TILES = [(0, 16), (16, 512), (528, 512), (1040, 512), (1552, 512)]
HALVES = [(0, 1, 2), (3, 4)]
HALF_OFF = [0, 1040]
HALF_LEN = [1040, 1024]
DEPTH = 2
ALPHA = (2.0 * DEPTH) ** 0.25
LN_EPS = 1e-5
EPS2 = LN_EPS / (ALPHA * ALPHA)
CHUNK = 64
NHEAD = 4
DH = 128
GAMMAS = [1.0 - 2.0 ** (-5.0 - h) for h in range(NHEAD)]
QSCALE = DH ** -0.5

ENGS = ("pe", "act", "dve", "pool", "sp")
import os
SAME_SYNC = os.environ.get("SAME_SYNC", "1") == "1"
PENG = os.environ.get("PENG", "dve")
SCHED = os.environ.get("SCHED", "1") == "1"


class Op:
    __slots__ = ("eng", "fn", "deps", "lane", "needs", "val", "stream", "seg", "dur", "idx", "succ", "pending", "ready", "fin")


class Prog:
    def __init__(self):
        self.ops = {e: [] for e in ENGS}
        self.lastw = {}
        self.readers = {}
        self.lane_cnt = {}
        self.last_in_stream = {}
        self.seg = 0
        self.count = 0

    def add(self, eng, fn, reads=(), writes=(), lane=None, dur=0.5):
        o = Op()
        o.eng, o.fn, o.lane, o.needs, o.val = eng, fn, lane, False, None
        o.seg, o.dur, o.idx = self.seg, dur, self.count
        self.count += 1
        o.stream = lane if lane is not None else eng
        deps = {}
        for r in reads:
            p = self.lastw.get(r)
            if p is not None:
                deps[id(p)] = p
            if isinstance(r, tuple) and r[0] == "bank":
                for p in self.readers.get(r, ()):
                    if p.stream != o.stream:
                        deps[id(p)] = p
        for r in writes:
            p = self.lastw.get(r)
            if p is not None:
                deps[id(p)] = p
            for p in self.readers.get(r, ()):
                deps[id(p)] = p
        o.deps = list(deps.values())
        for p in o.deps:
            p.needs = True
        for r in writes:
            self.lastw[r] = o
            self.readers[r] = []
        for r in reads:
            self.readers.setdefault(r, []).append(o)
        if lane is not None:
            self.lane_cnt[lane] = self.lane_cnt.get(lane, 0) + 16
            o.val = self.lane_cnt[lane]
        self.ops[eng].append(o)
        self.last_in_stream[o.stream] = o
        return o

    def barrier(self):
        lasts = list(self.last_in_stream.values())
        for p in lasts:
            p.needs = True
        for e in ENGS:
            o = Op()
            o.eng, o.fn, o.lane, o.needs, o.val = e, None, None, False, None
            o.stream = e
            o.seg, o.dur, o.idx = self.seg, 0.0, self.count
            o.deps = [p for p in lasts]
            self.ops[e].append(o)
        self.count += 1
        self.seg += 1
        self.lastw = {}
        self.readers = {}

    def schedule(self, window=int(os.environ.get("SWIN", "64")), hop=float(os.environ.get("SHOP", "2.2"))):
        REORD = ("pe", "act", "dve")
        nseg = self.seg + 1
        per = {e: [[] for _ in range(nseg)] for e in ENGS}
        bars = {e: [None] * nseg for e in ENGS}
        for e in ENGS:
            for o in self.ops[e]:
                if o.fn is None:
                    bars[e][o.seg] = o
                else:
                    per[e][o.seg].append(o)
        new = {e: [] for e in ENGS}
        for sg in range(nseg):
            ops = [o for e in ENGS for o in per[e][sg]]
            for o in ops:
                o.succ, o.pending, o.ready, o.fin = [], 0, 0.0, None
            inseg = set(id(o) for o in ops)
            for o in ops:
                for p in o.deps:
                    if id(p) in inseg and p is not o:
                        p.succ.append(o)
                        o.pending += 1
            queues = {e: list(per[e][sg]) for e in ENGS}
            tfree = {e: 0.0 for e in ENGS}
            order = {e: [] for e in ENGS}
            remaining = len(ops)
            while remaining:
                best = None
                for e in ENGS:
                    q = queues[e]
                    if not q:
                        continue
                    lim = window if e in REORD else 1
                    cnt = 0
                    for k, o in enumerate(q):
                        if cnt >= lim:
                            break
                        cnt += 1
                        if o.pending:
                            continue
                        st = max(tfree[e], o.ready)
                        key = (st, o.idx)
                        if best is None or key < best[0]:
                            best = (key, e, k, o)
                if best is None:
                    raise RuntimeError("scheduler deadlock")
                (st, _), e, k, o = best
                del queues[e][k]
                order[e].append(o)
                if o.lane is not None:
                    tfree[e] = st + 0.3
                    o.fin = st + o.dur
                else:
                    o.fin = st + o.dur + (0.15 if e == "pe" else 0.0)
                    tfree[e] = st + o.dur
                for y in o.succ:
                    y.pending -= 1
                    if o.lane is None and o.eng == y.eng:
                        h = 0.0 if e == "pe" else 0.35
                    else:
                        h = hop
                    if o.fin + h > y.ready:
                        y.ready = o.fin + h
                remaining -= 1
            lasts = []
            for e in ENGS:
                if e in ("sp", "pool"):
                    seen = {}
                    for o in order[e]:
                        seen[o.stream] = o
                    lasts.extend(seen.values())
                elif order[e]:
                    lasts.append(order[e][-1])
            for p in lasts:
                p.needs = True
            for e in ENGS:
                new[e].extend(order[e])
                b = bars[e][sg]
                if b is not None:
                    b.deps = list(lasts)
                    new[e].append(b)
        self.ops = new

    def emit(self, nc):
        if SCHED:
            self.schedule()
        for e in ENGS:
            c = 0
            for o in self.ops[e]:
                if o.lane is None and o.needs and o.fn is not None:
                    c += 1
                    o.val = c
                elif o.lane is None:
                    o.val = None
        with ExitStack() as es:
            sems = {}
            for e in ENGS[:4]:
                sems[e] = es.enter_context(nc.semaphore("sem_" + e))
            for ln in self.lane_cnt:
                sems[ln] = es.enter_context(nc.semaphore("lane_" + str(ln)))
            block = es.enter_context(nc.Block())

            def run(e, engine):
                known = {}
                for o in self.ops[e]:
                    need = {}
                    for p in o.deps:
                        if p is o:
                            continue
                        if p.lane is not None and p.lane == o.lane:
                            continue
                        if p.lane is None:
                            if p.eng == e and (e == "pe" or not SAME_SYNC):
                                continue
                            if p.val is None:
                                continue
                        key = p.stream
                        if p.val > need.get(key, 0):
                            need[key] = p.val
                    for key, val in need.items():
                        if known.get(key, 0) >= val:
                            continue
                        engine.wait_ge(sems[key], val)
                        known[key] = val
                    if o.fn is None:
                        continue
                    inst = o.fn(engine)
                    if o.lane is not None:
                        inst.then_inc(sems[o.lane], 16)
                    elif o.needs:
                        inst.then_inc(sems[e], 1)

            @block.tensor
            def _(eng):
                run("pe", eng)

            @block.scalar
            def _(eng):
                run("act", eng)

            @block.vector
            def _(eng):
                run("dve", eng)

            @block.gpsimd
            def _(eng):
                run("pool", eng)

            @block.sync
            def _(eng):
                run("sp", eng)


class Arena:
    def __init__(self, t, nelem):
        self.t = t
        self.n = nelem
        self.off = 0
        self.uid = 0

    def mark(self):
        return self.off

    def reset(self, m):
        self.off = m

    def alloc(self, free_shape, dtype, name):
        n = int(np.prod(free_shape))
        if dtype == F32:
            self.off += self.off % 2
            ap = self.t[:, self.off:self.off + 2 * n].bitcast(F32)
            self.off += 2 * n
        else:
            ap = self.t[:, self.off:self.off + n]
            self.off += n
        assert self.off <= self.n, ("SBUF arena overflow", name, self.off, self.n)
        if len(free_shape) == 2:
            ap = ap.rearrange("p (a b) -> p a b", a=free_shape[0])
        elif len(free_shape) == 3:
            ap = ap.rearrange("p (a b c) -> p a b c", a=free_shape[0], b=free_shape[1])
        self.uid += 1
        return ap


class Builder:
    def __init__(self, nc):
        self.nc = nc
        self.P = Prog()
        self.es = ExitStack()
        ARENA_ELEMS = 106000
        at = self.es.enter_context(nc.sbuf_tensor("arena", [128, ARENA_ELEMS], BF16))
        self.A = Arena(at, ARENA_ELEMS)
        self.banks = [self.es.enter_context(nc.psum_tensor(f"bank{i}", [128, 512], F32)) for i in range(8)]
        self.din = {}
        self.dout = {}
        self.rot = {}

    def inp(self, name, shape):
        if name not in self.din:
            self.din[name] = self.nc.dram_tensor(name, list(shape), F32, kind="ExternalInput").ap()
        return self.din[name]

    def outp(self, name, shape):
        self.dout[name] = self.nc.dram_tensor(name, list(shape), F32, kind="ExternalOutput").ap()
        return self.dout[name]

    def scratch(self, name, shape):
        return self.nc.dram_tensor(name, list(shape), F32, kind="Internal").ap()

    @staticmethod
    def _fs(ap):
        return int(np.prod(ap.shape[1:]))

    def mm(self, out, lhsT, rhs, start, stop, reads, writes):
        return self.P.add("pe", lambda e: e.matmul(out, lhsT, rhs, start=start, stop=stop), reads, writes,
                          dur=max(self._fs(rhs), 64) * 0.00048 + 0.02)

    def tr(self, out, in_, ident, reads, writes):
        return self.P.add("pe", lambda e: e.transpose(out, in_, ident), reads, writes, dur=0.1)

    def act(self, out, in_, func, reads, writes, scale=1.0, bias=0.0):
        return self.P.add("act", lambda e: e.activation(out=out, in_=in_, func=func, bias=bias, scale=scale), reads, writes,
                          dur=self._fs(out) * 0.00095 + 0.22)

    def tt(self, eng, out, in0, in1, op, reads, writes):
        return self.P.add(eng, lambda e: e.tensor_tensor(out=out, in0=in0, in1=in1, op=op), reads, writes, dur=self._fs(out) * 0.00105 + 0.12)

    def ts(self, eng, out, in0, s1, s2, op0, op1, reads, writes):
        return self.P.add(eng, lambda e: e.tensor_scalar(out=out, in0=in0, scalar1=s1, scalar2=s2, op0=op0, op1=op1), reads, writes, dur=self._fs(out) * 0.00105 + 0.12)

    def stt(self, eng, out, in0, scalar, in1, op0, op1, reads, writes):
        return self.P.add(eng, lambda e: e.scalar_tensor_tensor(out=out, in0=in0, scalar=scalar, in1=in1, op0=op0, op1=op1), reads, writes, dur=self._fs(out) * 0.00105 + 0.12)

    def ts1(self, eng, out, in_, scalar, op, reads, writes):
        return self.P.add(eng, lambda e: e.tensor_single_scalar(out=out, in_=in_, scalar=scalar, op=op), reads, writes, dur=self._fs(out) * 0.00105 + 0.12)

    def copy(self, eng, out, in_, reads, writes):
        return self.P.add(eng, lambda e: e.tensor_copy(out=out, in_=in_), reads, writes, dur=self._fs(out) * 0.00105 + 0.12)

    def dma(self, eng, out, in_, reads, writes, lane):
        return self.P.add(eng, lambda e: e.dma_start(out=out, in_=in_), reads, writes, lane=lane,
                          dur=2.0 + int(np.prod(out.shape)) * 4 / 150e3)

    def rsqrt(self, out, in_, eps, reads, writes):
        self.act(out, in_, AF.Ln, reads, writes, scale=1.0, bias=eps)
        self.act(out, out, AF.Exp, writes, writes, scale=-0.5)

    def nxt(self, name, n):
        i = self.rot.get(name, 0)
        self.rot[name] = i + 1
        return i % n

    def setup_consts(self):
        A = self.A
        cst = self.inp("consts", [128, 256])
        self.ident_f = A.alloc((128,), F32, "ident_f")
        self.ident_b = A.alloc((128,), BF16, "ident_b")
        self.ones_b = A.alloc((128,), BF16, "ones_b")
        self.dma("sp", self.ident_f, cst[:, 0:128], [], ["ident_f"], "c0")
        self.dma("pool", self.ident_b, cst[:, 0:128], [], ["ident_b"], "c1")
        self.dma("pool", self.ones_b, cst[:, 128:256], [], ["ones_b"], "c1")
        self.par = []
        for l in range(DEPTH):
            p = A.alloc((NPAR,), F32, f"par{l}")
            self.dma("sp", p, self.inp(f"par{l}", [128, NPAR]), [], [f"par{l}"], "c0")
            self.par.append(p)

    def phase_inln(self, x_ap, xpre_ap, h_dst):
        A, P = self.A, self.P
        m0 = A.mark()
        xt = [A.alloc((1024,), F32, "xt") for _ in range(2)]
        xn = [A.alloc((1024,), F32, "xn") for _ in range(2)]
        st = [A.alloc((2, 6), F32, "st") for _ in range(2)]
        mv = [A.alloc((2,), F32, "mv") for _ in range(2)]
        rs = [A.alloc((1,), F32, "rs") for _ in range(2)]
        stage = [A.alloc((8, 128), F32, "stage") for _ in range(2)]
        par = self.par[0]
        subt = [(xpre_ap, 0, 16, 0)] + [(x_ap, s * 128, 128, NPRE + s * 128) for s in range(BLK // 128)]
        for i, (src, r0, n, toff) in enumerate(subt):
            k = i % 2
            self.dma("sp", xt[k][:n, :], src[r0:r0 + n, :], [], [("xt", k)], ("xt", k))
            for c in range(2):
                P.add("dve", lambda e, k=k, c=c, n=n: e.bn_stats(out=st[k][:n, c, :], in_=xt[k][:n, c * 512:(c + 1) * 512]),
                      [("xt", k)], [("st", k, c)])
            P.add("dve", lambda e, k=k, n=n: e.bn_aggr(out=mv[k][:n, :], in_=st[k][:n, :, :].rearrange("p a b -> p (a b)")),
                  [("st", k, 0), ("st", k, 1)], [("mv", k)])
            self.rsqrt(rs[k][:n, :], mv[k][:n, 1:2], LN_EPS, [("mv", k)], [("rs", k)])
            self.ts("dve", xn[k][:n, :], xt[k][:n, :], mv[k][:n, 0:1], rs[k][:n, 0:1], ALU.subtract, ALU.mult,
                    [("xt", k), ("mv", k), ("rs", k)], [("xn", k)])
            for half in range(2):
                bk = self.nxt("inln_bank", 2)
                ps = self.banks[bk]
                for j in range(4):
                    ft = half * 4 + j
                    self.tr(ps[:, j * 128:j * 128 + n], xn[k][:n, ft * 128:(ft + 1) * 128], self.ident_f[:n, :n],
                            [("xn", k), "ident_f"], [("bank", bk)])
                for j in range(4):
                    ft = half * 4 + j
                    self.act(stage[k][:, ft, :n], ps[:, j * 128:j * 128 + n], AF.Identity,
                             [("bank", bk), "par0"], [("stage", k)],
                             scale=par[:, PAR_LNIN_G + ft:PAR_LNIN_G + ft + 1], bias=par[:, PAR_LNIN_B + ft:PAR_LNIN_B + ft + 1])
            self.dma("sp", h_dst[:, :, toff:toff + n], stage[k][:, :, :n], [("stage", k)], [], ("stg", k))
        P.barrier()
        A.reset(m0)

    def load_hb(self, h_src):
        if not hasattr(self, "hb"):
            self.hb = self.A.alloc((NFT, T), BF16, "hb")
        for ti, (o, n) in enumerate(TILES):
            self.dma("pool", self.hb[:, :, o:o + n], h_src[:, :, o:o + n], [],
                     [("hb", ft, ti) for ft in range(NFT)], ("hbld", ti))

    def ln_alloc(self, ntile, nstat=None):
        A = self.A
        nstat = nstat or ntile
        self.yp = [A.alloc((8, 512), F32, "yp") for _ in range(ntile)]
        self.sq = [A.alloc((512,), BF16, "sq") for _ in range(2)]
        self.yb = [A.alloc((512,), BF16, "yb") for _ in range(2)]
        self.mean = [A.alloc((512,), F32, "mean") for _ in range(nstat)]
        self.rstd = [A.alloc((512,), F32, "rstd") for _ in range(nstat)]
        self.nmr = [A.alloc((512,), F32, "nmr") for _ in range(nstat)]

    def ln_accum(self, li, fo, n, py, pb, cres):
        yp = self.yp[li]
        self.stt("dve", yp[:, fo, :n], py[:, :n], cres, yp[:, fo, :n], ALU.mult, ALU.add,
                 [("bank", pb), ("yp", li)], [("ypf", li, fo)])
        q = self.nxt("sq", 2)
        self.act(self.sq[q][:, :n], yp[:, fo, :n], AF.Square, [("ypf", li, fo)], [("sq", q)])
        self.act(self.yb[q][:, :n], yp[:, fo, :n], AF.Copy, [("ypf", li, fo)], [("yb", q)])
        bs, bq = 2 + 2 * li, 3 + 2 * li
        self.mm(self.banks[bs][:, :n], self.ones_b, self.yb[q][:, :n], fo == 0, fo == NFT - 1, ["ones_b", ("yb", q)], [("bank", bs)])
        self.mm(self.banks[bq][:, :n], self.ones_b, self.sq[q][:, :n], fo == 0, fo == NFT - 1, ["ones_b", ("sq", q)], [("bank", bq)])

    def stats_finish(self, bs, bq, n, inv, eps, mn, rd, nm, key):
        self.ts1("dve", mn[:, :n], self.banks[bs][:, :n], inv, ALU.mult, [("bank", bs)], [("mean", key)])
        self.tt("dve", nm[:, :n], mn[:, :n], mn[:, :n], ALU.mult, [("mean", key)], [("nmr", key)])
        self.stt("dve", rd[:, :n], self.banks[bq][:, :n], inv, nm[:, :n], ALU.mult, ALU.subtract,
                 [("bank", bq), ("nmr", key)], [("rstd", key)])
        self.rsqrt(rd[:, :n], rd[:, :n], eps, [("rstd", key)], [("rstd", key)])
        self.stt("dve", nm[:, :n], mn[:, :n], -1.0, rd[:, :n], ALU.mult, ALU.mult, [("mean", key), ("rstd", key)], [("nmr", key)])

    def ln_finish(self, l, lnidx, li, ti, h_dst, eng="dve"):
        par = self.par[l]
        gcol = PAR_LN_G + lnidx * 8
        bcol = PAR_LN_B + lnidx * 8
        o, n = TILES[ti]
        yp, hb = self.yp[li], self.hb
        bs, bq = 2 + 2 * li, 3 + 2 * li
        mn, rd, nm = self.mean[li], self.rstd[li], self.nmr[li]
        self.stats_finish(bs, bq, n, 1.0 / D, EPS2, mn, rd, nm, li)
        for fo in range(NFT):
            self.tt(eng, yp[:, fo, :n], yp[:, fo, :n], rd[:, :n], ALU.mult, [("ypf", li, fo), ("rstd", li)], [("ypf", li, fo)])
            self.tt(eng, yp[:, fo, :n], yp[:, fo, :n], nm[:, :n], ALU.add, [("ypf", li, fo), ("nmr", li)], [("ypf", li, fo)])
            self.act(hb[:, fo, o:o + n], yp[:, fo, :n], AF.Identity, [("ypf", li, fo), f"par{l}"], [("hb", fo, ti)],
                     scale=par[:, gcol + fo:gcol + fo + 1], bias=par[:, bcol + fo:bcol + fo + 1])
            self.act(yp[:, fo, :n], yp[:, fo, :n], AF.Identity, [("ypf", li, fo), f"par{l}"], [("ypf", li, fo)],
                     scale=par[:, gcol + fo:gcol + fo + 1], bias=par[:, bcol + fo:bcol + fo + 1])
        self.dma("sp", h_dst[:, :, o:o + n], yp[:, :, :n], [("ypf", li, fo) for fo in range(NFT)] + [("yp", li)],
                 [("yp", li)], ("ypst", li))

    def phase_ffn(self, l, w13, w2, lnidx, h_src, h_dst):
        A, P = self.A, self.P
        m0 = A.mark()
        hid = A.alloc((NHT, 1040), BF16, "hid")
        w13s = [A.alloc((8, 2, 128), BF16, "w13s") for _ in range(3)]
        w2s = [A.alloc((NHT, 128), BF16, "w2s") for _ in range(2)]
        self.ln_alloc(3)
        sg = [A.alloc((512,), F32, "sg") for _ in range(4)]
        hb = self.hb
        w13v = w13.rearrange("(kt p) c -> p kt c", p=128)
        CRES = 0.5 / ALPHA
        HZ = {0: (0,), 1: (1,), 2: (2,), 3: (0, 1), 4: (1, 2)}
        for hi, tiles in enumerate(HALVES):
            hoff = HALF_OFF[hi]
            for li, ti in enumerate(tiles):
                o, n = TILES[ti]
                self.dma("sp", self.yp[li][:, :, :n], h_src[:, :, o:o + n], [], [("yp", li)], ("ypld", li))
            for m in range(NHT):
                mw = 128 if m < NHT - 1 else 64
                s = self.nxt("w13s", 3)
                self.dma("pool", w13s[s][:, :, 0, :mw], w13v[:, :, m * 128:m * 128 + mw], [], [("w13s", s)], ("w13", s))
                self.dma("pool", w13s[s][:, :, 1, :mw], w13v[:, :, DFF + m * 128:DFF + m * 128 + mw], [], [("w13s", s)], ("w13", s))
                for ti in tiles:
                    o, n = TILES[ti]
                    pb = self.nxt("upbank", 4) * 2
                    pa, pu = self.banks[pb], self.banks[pb + 1]
                    for kt in range(NFT):
                        self.mm(pa[:mw, :n], w13s[s][:, kt, 0, :mw], hb[:, kt, o:o + n], kt == 0, kt == NFT - 1,
                                [("w13s", s), ("hb", kt, ti)], [("bank", pb)])
                    for kt in range(NFT):
                        self.mm(pu[:mw, :n], w13s[s][:, kt, 1, :mw], hb[:, kt, o:o + n], kt == 0, kt == NFT - 1,
                                [("w13s", s), ("hb", kt, ti)], [("bank", pb + 1)])
                    g = self.nxt("sg", 4)
                    self.act(sg[g][:mw, :n], pa[:mw, :n], AF.Silu, [("bank", pb)], [("sg", g)])
                    self.tt("dve", hid[:mw, m, o - hoff:o - hoff + n], sg[g][:mw, :n], pu[:mw, :n], ALU.mult,
                            [("sg", g), ("bank", pb + 1)], [("hid", m, z) for z in HZ[ti]])
            for fo in range(NFT):
                s = self.nxt("w2s", 2)
                self.dma("pool", w2s[s][:, 0:NHT - 1, :], w2[0:(NHT - 1) * 128, fo * 128:(fo + 1) * 128].rearrange("(kt p) c -> p kt c", p=128),
                         [], [("w2s", s)], ("w2", s))
                self.dma("pool", w2s[s][0:64, NHT - 1, :], w2[(NHT - 1) * 128:DFF, fo * 128:(fo + 1) * 128], [], [("w2s", s)], ("w2", s))
                for li, ti in enumerate(tiles):
                    o, n = TILES[ti]
                    pb = self.nxt("dnbank", 2)
                    py = self.banks[pb]
                    for m in range(NHT):
                        mw = 128 if m < NHT - 1 else 64
                        self.mm(py[:, :n], w2s[s][:mw, m, :], hid[:mw, m, o - hoff:o - hoff + n], m == 0, m == NHT - 1,
                                [("w2s", s)] + [("hid", m, z) for z in HZ[ti]], [("bank", pb)])
                    self.ln_accum(li, fo, n, py, pb, CRES)
            for li, ti in enumerate(tiles):
                self.ln_finish(l, lnidx, li, ti, h_dst)
        P.barrier()
        A.reset(m0)

    def rope(self, ps, pbkey, rp, rkey, n, dst, dkey, sum_eng="dve"):
        k = self.nxt("ropetmp", len(self.ropeA))
        a, b = self.ropeA[k], self.ropeB[k]
        self.tt("dve", a[:, :n], ps[:, :n], rp[:, 0, :n], ALU.mult, [pbkey, rkey], [("ropeA", k)])
        self.tt("dve", b[0:64, :n], ps[64:128, :n], rp[64:128, 1, :n], ALU.mult, [pbkey, rkey], [("ropeB0", k)])
        self.tt("dve", b[64:128, :n], ps[0:64, :n], rp[0:64, 1, :n], ALU.mult, [pbkey, rkey], [("ropeB1", k)])
        self.tt(sum_eng, dst, a[:, :n], b[:, :n], ALU.add, [("ropeA", k), ("ropeB0", k), ("ropeB1", k)], [dkey])

    def load_rope(self, rope_d, ti):
        o, n = TILES[ti]
        k = self.nxt("ropet", 2)
        self.dma("sp", self.ropet[k][:, :, :n], rope_d[:, :, o:o + n], [], [("ropet", k)], ("ropet", k))
        return self.ropet[k], ("ropet", k)

    def phase_kv(self, l, w_in, rope_d, vtab_d, send_d):
        A, P = self.A, self.P
        m0 = A.mark()
        hb = self.hb
        wk = A.alloc((8, 512), BF16, "wk")
        wv = A.alloc((8, 512), BF16, "wv")
        wh = A.alloc((8, 768), BF16, "wh")
        self.ropet = [A.alloc((2, 512), F32, "ropet") for _ in range(2)]
        self.ropeA = [A.alloc((512,), F32, "ropeA") for _ in range(2)]
        self.ropeB = [A.alloc((512,), F32, "ropeB") for _ in range(2)]
        vtab = A.alloc((17, 2, 4), F32, "vtab")
        kT = A.alloc((4, 512), BF16, "kT")
        vfull = [A.alloc((4, 128), BF16, "vfull") for _ in range(2)]
        kTok = [A.alloc((4, 128), BF16, "kTok") for _ in range(2)]
        send = A.alloc((640,), F32, "send")
        sgm = A.alloc((2, 32), F32, "sgm")
        w_inv = w_in.rearrange("(kt p) c -> p kt c", p=128)
        self.dma("pool", wk, w_inv[:, :, 1280:1792], [], ["wk"], "wk")
        self.dma("pool", wv, w_inv[:, :, 1792:2304], [], ["wv"], "wv")
        self.dma("pool", wh, w_inv[:, :, 0:768], [], ["wh"], "wh")
        self.dma("sp", vtab, vtab_d.rearrange("p (a b c) -> p a b c", a=17, b=2), [], ["vtab"], "vtab")
        b7 = self.banks[7][:, :].bitcast(BF16)
        SB = 4
        sidx = 0
        for ti, (o, n) in enumerate(TILES):
            rp, rkey = self.load_rope(rope_d, ti)
            for h in range(NHEAD):
                pb = self.nxt("kvbank", 2)
                ps = self.banks[pb]
                for kt in range(NFT):
                    self.mm(ps[:, :n], wk[:, kt, h * 128:(h + 1) * 128], hb[:, kt, o:o + n], kt == 0, kt == NFT - 1,
                            ["wk", ("hb", kt, ti)], [("bank", pb)])
                self.rope(ps, ("bank", pb), rp, rkey, n, kT[:, h, :n], ("kT", h))
            nsub = max(1, n // 128)
            for sub in range(nsub):
                ns = min(n, 128)
                c0 = o + sub * 128
                pb = 2 + self.nxt("kvbank2", 2)
                ps = self.banks[pb]
                for kt in range(NFT):
                    self.mm(ps[:ns, :], hb[:, kt, c0:c0 + ns], wv[:, kt, :], kt == 0, kt == NFT - 1,
                            ["wv", ("hb", kt, ti)], [("bank", pb)])
                k2 = self.nxt("vfull", 2)
                self.tt("dve", vfull[k2][:ns, :, :], ps[:ns, :].rearrange("p (h e) -> p h e", h=4),
                        vtab[:ns, sidx, 1, :].unsqueeze(2).to_broadcast([ns, 4, 128]), ALU.mult,
                        [("bank", pb), "vtab"], [("vfull", k2)])
                for h in range(NHEAD):
                    self.tr(b7[:ns, h * 128:(h + 1) * 128], kT[:, h, sub * 128:sub * 128 + ns], self.ident_b,
                            [("kT", h), "ident_b"], [("bank", 7)])
                self.act(kTok[k2][:ns, :, :], b7[:ns, 0:512].rearrange("p (h e) -> p h e", h=4), AF.Copy, [("bank", 7)], [("kTok", k2)])
                for h in range(NHEAD):
                    self.mm(self.banks[SB][:, h * 128:(h + 1) * 128], kTok[k2][:ns, h, :], vfull[k2][:ns, h, :], sidx == 0, sidx == 16,
                            [("kTok", k2), ("vfull", k2)], [("bank", SB)])
                sidx += 1
        HB = 5
        ps = self.banks[HB]
        for mt in range(6):
            for kt in range(NFT):
                self.mm(ps[:, mt * 32:(mt + 1) * 32], wh[:, kt, mt * 128:(mt + 1) * 128], hb[:, kt, T - 32:T], kt == 0, kt == NFT - 1,
                        ["wh", ("hb", kt, 4)], [("bank", HB)])
        self.act(send[:, 0:512], self.banks[SB][:, :], AF.Copy, [("bank", SB)], ["send_s"])
        self.act(sgm[:, :, :], ps[:, 128:192].rearrange("p (a b) -> p a b", a=2), AF.Sigmoid, [("bank", HB)], ["sgm"])
        self.tt("dve", send[:, 512:576].rearrange("p (a b) -> p a b", a=2), sgm[:, :, :], ps[:, 64:128].rearrange("p (a b) -> p a b", a=2),
                ALU.mult, ["sgm", ("bank", HB)], ["send_u"])
        self.act(send[:, 576:640], ps[:, 0:64], AF.Copy, [("bank", HB)], ["send_p"])
        self.dma("sp", send_d[:, :], send, ["send_s", "send_u", "send_p"], [], "send")
        P.barrier()
        A.reset(m0)

    def phase_mix(self, l, w_in, pool_w, conv_pw, w_out, rope_d, vtab_d, tab_d, sprev_srcs, hprev_src, h_src, h_dst):
        A, P = self.A, self.P
        m0 = A.mark()
        hb = self.hb
        par = self.par[l]
        tab = A.alloc((NTAB,), F32, "tab")
        vtab = A.alloc((17, 2, 4), F32, "vtab")
        self.ropet = [A.alloc((2, 512), F32, "ropet") for _ in range(2)]
        self.ropeA = [A.alloc((512,), F32, "ropeA") for _ in range(2)]
        self.ropeB = [A.alloc((512,), F32, "ropeB") for _ in range(2)]
        sprev = A.alloc((3, 512), F32, "sprev")
        hprev = A.alloc((128,), F32, "hprev")
        S = A.alloc((4, 128), F32, "S")
        Stmp = A.alloc((4, 128), F32, "Stmp")
        Sb = A.alloc((4, 128), BF16, "Sb")
        wsl = [A.alloc((8, 128), BF16, "wsl") for _ in range(3)]
        wv = A.alloc((8, 512), BF16, "wv")
        wbd = A.alloc((2, 128), BF16, "wbd")
        wpw = A.alloc((2, 256), BF16, "wpw")
        XP = A.alloc((2, 15 + 512), F32, "XP")
        S2e_l = [A.alloc((526,), F32, "S2e")] * 2
        S4e_l = [A.alloc((524,), F32, "S4e")] * 2
        S8e_l = [A.alloc((520,), F32, "S8e")] * 2
        Mb_l = [A.alloc((512,), F32, "Mb")] * 2
        ypool = A.alloc((2, 512), BF16, "ypool")
        U = A.alloc((2, 30 + 512), BF16, "U")
        Utmp = A.alloc((2, 30), BF16, "Utmp")
        cdiag = A.alloc((2, 31, 128), BF16, "cdiag")
        acc = A.alloc((2, 512), F32, "acc")
        sgt_l = [A.alloc((512,), F32, "sgt") for _ in range(2)]
        cact = A.alloc((2, 512), BF16, "cact")
        qrot_l = [A.alloc((512,), F32, "qrot")] * 2
        qT = A.alloc((4, 512), BF16, "qT")
        qdec = A.alloc((4, 512), BF16, "qdec")
        kT = A.alloc((4, 512), BF16, "kT")
        sgate = A.alloc((4, 512), F32, "sgate")
        vbf = A.alloc((4, 4, 128), BF16, "vbf")
        vdec = A.alloc((4, 4, 128), BF16, "vdec")
        kTok = A.alloc((4, 4, 128), BF16, "kTok")
        sdT = A.alloc((4, 4, 128), BF16, "sdT")
        ycat = A.alloc((8, 512), BF16, "ycat")
        t1_l = [A.alloc((512,), F32, "t1")] * 2
        self.ln_alloc(1, 2)
        b7 = self.banks[7][:, :].bitcast(BF16)
        w_inv = w_in.rearrange("(kt p) c -> p kt c", p=128)
        w_outv = w_out.rearrange("(kt p) c -> p kt c", p=128)

        def tcol(c0, w):
            return tab[:, c0:c0 + w]

        self.dma("sp", tab, tab_d[:, :], [], ["tab"], "tab")
        self.dma("sp", vtab, vtab_d.rearrange("p (a b c) -> p a b c", a=17, b=2), [], ["vtab"], "vtab")
        for i in range(3):
            self.dma("sp", sprev[:, i, :], sprev_srcs[i], [], ["sprev"], "sprev")
        self.dma("sp", hprev, hprev_src, [], ["hprev"], "hprev")
        self.dma("pool", wv, w_inv[:, :, 1792:2304], [], ["wv"], "wv")
        P.add("dve", lambda e: e.memset(wbd[:, :, :], 0.0), [], ["wbd"])
        for tl in range(2):
            for a in range(2):
                self.dma("pool", wbd[64 * a:64 * a + 64, tl, 64 * a:64 * a + 64], pool_w[2 * tl + a, :, :], ["wbd"], [("wbd2", tl, a)], "wbd")
        self.dma("pool", wpw, conv_pw.rearrange("(kt p) c -> p kt c", p=128), [], ["wpw"], "wpw")
        for tl in range(2):
            for j in range(31):
                self.ts1("dve", cdiag[:, tl, j, :], self.ident_f, par[:, PAR_CONV_W + tl * 31 + j:PAR_CONV_W + tl * 31 + j + 1], ALU.mult,
                         ["ident_f", f"par{l}"], [("cdiag", tl)])
        P.add("dve", lambda e: e.memset(XP[:, :, 0:15], 0.0), [], ["XPh"])
        P.add("dve", lambda e: e.memset(U[:, :, 0:30], 0.0), [], ["Uh"])
        for i in range(3):
            cb = tab[:, TB_COEF + 4 * i:TB_COEF + 4 * i + 4].unsqueeze(2).to_broadcast([128, 4, 128])
            src = sprev[:, i, :].rearrange("p (h e) -> p h e", h=4)
            if i == 0:
                self.tt("dve", S[:, :, :], src, cb, ALU.mult, ["sprev", "tab"], ["S"])
            else:
                self.tt("dve", Stmp[:, :, :], src, cb, ALU.mult, ["sprev", "tab"], ["Stmp"])
                self.tt("dve", S[:, :, :], S[:, :, :], Stmp[:, :, :], ALU.add, ["S", "Stmp"], ["S"])
        self.act(Sb[:, :, :], S[:, :, :], AF.Copy, ["S"], ["Sb"])
        dtab = tab[:, TB_DTAB:TB_DTAB + 512].rearrange("p (h e) -> p h e", h=4)
        ff = tab[:, TB_FLAG:TB_FLAG + 1]
        nf = tab[:, TB_FLAG + 1:TB_FLAG + 2]

        sidx = 0
        STOP = getattr(self, "mix_stop", 99)
        NT = getattr(self, "mix_tiles", len(TILES))
        for ti, (o, n) in enumerate(TILES[:NT]):
            rp, rkey = self.load_rope(rope_d, ti)
            self.dma("sp", self.yp[0][:, :, :n], h_src[:, :, o:o + n], [], [("yp", 0)], ("ypld", 0))
            nsub = max(1, n // 128)
            ns = min(n, 128)

            def proj(col0):
                s = self.nxt("wsl", 3)
                self.dma("pool", wsl[s], w_inv[:, :, col0:col0 + 128], [], [("wsl", s)], ("wsl", s))
                pb = self.nxt("pjbank", 3)
                ps = self.banks[pb]
                for kt in range(NFT):
                    self.mm(ps[:, :n], wsl[s][:, kt, :], hb[:, kt, o:o + n], kt == 0, kt == NFT - 1,
                            [("wsl", s), ("hb", kt, ti)], [("bank", pb)])
                return ps, ("bank", pb)

            for tl in range(2):
                ps, pk = proj(0 + tl * 128)
                self.act(XP[:, tl, 15:15 + n], ps[:, :n], AF.Copy, [pk], [("XP", tl)])
                x = XP[:, tl, :]
                hk = [("XP", tl), "XPh"]
                S2e, S4e, S8e, Mb = S2e_l[tl], S4e_l[tl], S8e_l[tl], Mb_l[tl]
                self.tt(PENG, S2e[:, 0:n + 14], x[:, 1:15 + n], x[:, 0:14 + n], ALU.add, hk, ["S2e"])
                self.tt(PENG, S4e[:, 0:n + 12], S2e[:, 2:n + 14], S2e[:, 0:n + 12], ALU.add, ["S2e"], ["S4e"])
                self.tt(PENG, S8e[:, 0:n + 8], S4e[:, 4:n + 12], S4e[:, 0:n + 8], ALU.add, ["S4e"], ["S8e"])
                pc = lambda w: tab[:, TB_PC + tl * 4 + w:TB_PC + tl * 4 + w + 1]
                self.ts1("dve", Mb[:, :n], S2e[:, 14:14 + n], pc(0), ALU.mult, ["S2e", "tab"], ["Mb"])
                self.stt("dve", Mb[:, :n], S4e[:, 12:12 + n], pc(1), Mb[:, :n], ALU.mult, ALU.add, ["S4e", "Mb", "tab"], ["Mb"])
                self.stt("dve", Mb[:, :n], S8e[:, 8:8 + n], pc(2), Mb[:, :n], ALU.mult, ALU.add, ["S8e", "Mb", "tab"], ["Mb"])
                self.stt("dve", Mb[:, :n], S8e[:, 8:8 + n], pc(3), Mb[:, :n], ALU.mult, ALU.add, ["S8e", "Mb", "tab"], ["Mb"])
                self.stt("dve", Mb[:, :n], S8e[:, 0:n], pc(3), Mb[:, :n], ALU.mult, ALU.add, ["S8e", "Mb", "tab"], ["Mb"])
                if ti == 0:
                    self.tt(PENG, Mb[:, :n], Mb[:, :n], tab[:, TB_PCORR + tl * 16:TB_PCORR + tl * 16 + 16], ALU.mult, ["Mb", "tab"], ["Mb"])
                self.tt("dve", ypool[:, tl, :n], Mb[:, :n], x[:, 15:15 + n], ALU.subtract, ["Mb", ("XP", tl)], [("ypool", tl)])
                self.copy("dve", XP[:, tl, 0:15], XP[:, tl, n:n + 15], [("XP", tl)], ["XPh", ("XP", tl)])
                if ti == 0:
                    self.ts1("dve", XP[:, tl, 0:15], XP[:, tl, 0:15], ff, ALU.mult, ["XPh", ("XP", tl), "tab"], ["XPh", ("XP", tl)])
                    self.stt("dve", XP[:, tl, 0:15], hprev[:, 64 + tl * 32 + 17:64 + tl * 32 + 32], nf, XP[:, tl, 0:15], ALU.mult, ALU.add,
                             ["hprev", "tab", "XPh", ("XP", tl)], ["XPh", ("XP", tl)])
                pb = 6
                self.mm(self.banks[pb][:, :n], wbd[:, tl, :], ypool[:, tl, :n], True, True,
                        [("wbd2", tl, 0), ("wbd2", tl, 1), "wbd", ("ypool", tl)], [("bank", pb)])
                self.act(ycat[:, tl, :n], self.banks[pb][:, :n], AF.Copy, [("bank", pb), f"par{l}"], [("ycat", tl)],
                         scale=par[:, PAR_POOL_SCALE + tl:PAR_POOL_SCALE + tl + 1])
            if STOP <= 1:
                continue
            for tl in range(2):
                psg, pkg = proj(512 + tl * 128)
                sgt = sgt_l[tl]
                self.act(sgt[:, :n], psg[:, :n], AF.Sigmoid, [pkg], [("sgt", tl)])
                psa, pka = proj(256 + tl * 128)
                self.tt("dve", U[:, tl, 30:30 + n], sgt[:, :n], psa[:, :n], ALU.mult, [("sgt", tl), pka], [("U", tl)])
                uk = [("U", tl), "Uh", ("cdiag", tl)]
                cb = 4 + tl
                for j in range(31):
                    self.mm(self.banks[cb][:, :n], cdiag[:, tl, j, :], U[:, tl, j:j + n], j == 0, j == 30, uk, [("bank", cb)])
                self.act(acc[:, tl, :n], self.banks[cb][:, :n], AF.Identity, [("bank", cb), f"par{l}"], [("acc", tl)],
                         bias=par[:, PAR_CONV_DB + tl:PAR_CONV_DB + tl + 1])
                if n >= 30:
                    self.copy("dve", U[:, tl, 0:30], U[:, tl, n:n + 30], [("U", tl)], ["Uh", ("U", tl)])
                else:
                    self.copy("dve", Utmp[:, tl, :], U[:, tl, n:n + 30], [("U", tl), "Uh"], [("Utmp", tl)])
                    self.copy("dve", U[:, tl, 0:30], Utmp[:, tl, :], [("Utmp", tl)], ["Uh", ("U", tl)])
                if ti == 0:
                    self.ts1("dve", U[:, tl, 0:30], U[:, tl, 0:30], ff, ALU.mult, ["Uh", ("U", tl), "tab"], ["Uh", ("U", tl)])
                    self.stt("dve", U[:, tl, 0:30], hprev[:, tl * 32 + 2:tl * 32 + 32], nf, U[:, tl, 0:30], ALU.mult, ALU.add,
                             ["hprev", "tab", "Uh", ("U", tl)], ["Uh", ("U", tl)])
            for tl in range(2):
                q = self.nxt("sq", 2)
                self.act(self.sq[q][:, :n], acc[:, tl, :n], AF.Square, [("acc", tl)], [("sq", q)])
                self.act(self.yb[q][:, :n], acc[:, tl, :n], AF.Copy, [("acc", tl)], [("yb", q)])
                self.mm(self.banks[4][:, :n], self.ones_b, self.yb[q][:, :n], tl == 0, tl == 1, ["ones_b", ("yb", q)], [("bank", 4)])
                self.mm(self.banks[5][:, :n], self.ones_b, self.sq[q][:, :n], tl == 0, tl == 1, ["ones_b", ("sq", q)], [("bank", 5)])
            mn, rd, nm = self.mean[0], self.rstd[0], self.nmr[0]
            self.stats_finish(4, 5, n, 1.0 / 256, LN_EPS, mn, rd, nm, 0)
            for tl in range(2):
                self.tt("dve", acc[:, tl, :n], acc[:, tl, :n], rd[:, :n], ALU.mult, [("acc", tl), ("rstd", 0)], [("acc", tl)])
                self.tt("dve", acc[:, tl, :n], acc[:, tl, :n], nm[:, :n], ALU.add, [("acc", tl), ("nmr", 0)], [("acc", tl)])
                self.act(cact[:, tl, :n], acc[:, tl, :n], AF.Silu, [("acc", tl), f"par{l}"], [("cact", tl)],
                         scale=par[:, PAR_CONV_LN_G + tl:PAR_CONV_LN_G + tl + 1], bias=par[:, PAR_CONV_LN_B + tl:PAR_CONV_LN_B + tl + 1])
            for mt in range(2):
                pb = 6
                for kt in range(2):
                    self.mm(self.banks[pb][:, :n], wpw[:, kt, mt * 128:(mt + 1) * 128], cact[:, kt, :n], kt == 0, kt == 1,
                            ["wpw", ("cact", kt)], [("bank", pb)])
                self.act(ycat[:, 2 + mt, :n], self.banks[pb][:, :n], AF.Copy, [("bank", pb)], [("ycat", 2 + mt)])
            if STOP <= 2:
                continue
            for h in range(NHEAD):
                ps, pk = proj(768 + h * 128)
                qrot = qrot_l[h % 2]
                qk = "qrot"
                self.rope(ps, pk, rp, rkey, n, qrot[:, :n], qk, sum_eng=PENG)
                self.act(qT[:, h, :n], qrot[:, :n], AF.Copy, [qk], [("qT", h)])
                if n >= 64:
                    self.tt("dve", qdec[:, h, :n].rearrange("p (c i) -> p c i", i=64), qrot[:, :n].rearrange("p (c i) -> p c i", i=64),
                            tab[:, TB_QD + h * 64:TB_QD + h * 64 + 64].unsqueeze(1).to_broadcast([128, n // 64, 64]), ALU.mult,
                            [qk, "tab"], [("qdec", h)])
            if STOP <= 2.2:
                continue
            for h in range(NHEAD):
                ps, pk = proj(1280 + h * 128)
                self.rope(ps, pk, rp, rkey, n, kT[:, h, :n], ("kT", h), sum_eng=PENG)
            if STOP <= 2.4:
                continue
            for h in range(NHEAD):
                ps, pk = proj(2304 + h * 128)
                self.act(sgate[:, h, :n], ps[:, :n], AF.Silu, [pk], [("sgate", h)])
            if STOP <= 2.6:
                continue
            for sub in range(nsub):
                c0 = o + sub * 128
                pb = 3
                ps = self.banks[pb]
                for kt in range(NFT):
                    self.mm(ps[:ns, :], hb[:, kt, c0:c0 + ns], wv[:, kt, :], kt == 0, kt == NFT - 1,
                            ["wv", ("hb", kt, ti)], [("bank", pb)])
                psv = ps[:ns, :].rearrange("p (h e) -> p h e", h=4)
                self.act(vbf[:ns, sub, :, :], psv, AF.Copy, [("bank", pb)], [("vbf", sub)])
                self.tt("dve", vdec[:ns, sub, :, :], psv, vtab[:ns, sidx + sub, 0, :].unsqueeze(2).to_broadcast([ns, 4, 128]), ALU.mult,
                        [("bank", pb), "vtab"], [("vdec", sub)])
                if STOP <= 2.8:
                    continue
                for h in range(NHEAD):
                    self.tr(b7[:ns, h * 128:(h + 1) * 128], kT[:, h, sub * 128:sub * 128 + ns], self.ident_b,
                            [("kT", h), "ident_b"], [("bank", 7)])
                self.act(kTok[:ns, sub, :, :], b7[:ns, 0:512].rearrange("p (h e) -> p h e", h=4), AF.Copy, [("bank", 7)], [("kTok", sub)])
            if STOP <= 3:
                sidx += nsub
                continue
            for h in range(NHEAD):
                pb = 4 + self.nxt("scbank", 2)
                ps = self.banks[pb]
                for sub in range(nsub):
                    self.mm(ps[:ns, sub * 128:sub * 128 + ns], kT[:, h, sub * 128:sub * 128 + ns], qT[:, h, sub * 128:sub * 128 + ns], True, True,
                            [("kT", h), ("qT", h)], [("bank", pb)])
                dm = tab[:ns, TB_DM2 + h * 128:TB_DM2 + h * 128 + ns]
                self.tt("dve", sdT[:ns, h, 0:nsub, :ns], ps[:ns, 0:nsub * 128].rearrange("p (s i) -> p s i", i=128)[:, :, :ns],
                        dm.unsqueeze(1).to_broadcast([ns, nsub, ns]), ALU.mult, [("bank", pb), "tab"], [("sdT", h)])
            if STOP <= 4:
                sidx += nsub
                continue
            for sub in range(nsub):
                for h in range(NHEAD):
                    ob = self.banks[h]
                    self.mm(ob[:, sub * 128:sub * 128 + ns], vbf[:ns, sub, h, :], sdT[:ns, h, sub, :ns], True, n < 64,
                            [("vbf", sub), ("sdT", h)], [("bank", h)])
                nch = max(1, ns // 64)
                for cc in range(nch):
                    cw = min(ns, 64)
                    r0 = cc * 64
                    if n >= 64:
                        for h in range(NHEAD):
                            ob = self.banks[h]
                            cs = sub * 128 + cc * 64
                            self.mm(ob[:, cs:cs + 64], Sb[:, h, :], qdec[:, h, cs:cs + 64], False, True,
                                    ["Sb", ("qdec", h)], [("bank", h)])
                    for h in range(NHEAD):
                        self.mm(self.banks[6][:, h * 128:(h + 1) * 128], kTok[r0:r0 + cw, sub, h, :], vdec[r0:r0 + cw, sub, h, :], True, True,
                                [("kTok", sub), ("vdec", sub)], [("bank", 6)])
                    if n >= 64:
                        self.tt("dve", S[:, :, :], S[:, :, :], dtab, ALU.mult, ["S", "tab"], ["S"])
                    self.tt("dve", S[:, :, :], S[:, :, :], self.banks[6][:, :].rearrange("p (h e) -> p h e", h=4), ALU.add,
                            ["S", ("bank", 6)], ["S"])
                    self.act(Sb[:, :, :], S[:, :, :], AF.Copy, ["S"], ["Sb"])
            if STOP <= 5:
                sidx += nsub
                continue
            for h in range(NHEAD):
                ob = self.banks[h]
                q = self.nxt("sq", 2)
                sk = h % 2
                mn, rd, nm = self.mean[sk], self.rstd[sk], self.nmr[sk]
                b0 = 4 + 2 * (h % 2)
                t1 = t1_l[h % 2]
                tk = "t1"
                self.act(self.sq[q][:, :n], ob[:, :n], AF.Square, [("bank", h)], [("sq", q)])
                self.act(self.yb[q][:, :n], ob[:, :n], AF.Copy, [("bank", h)], [("yb", q)])
                self.mm(self.banks[b0][:, :n], self.ones_b, self.yb[q][:, :n], True, True, ["ones_b", ("yb", q)], [("bank", b0)])
                self.mm(self.banks[b0 + 1][:, :n], self.ones_b, self.sq[q][:, :n], True, True, ["ones_b", ("sq", q)], [("bank", b0 + 1)])
                self.stats_finish(b0, b0 + 1, n, 1.0 / 128, LN_EPS, mn, rd, nm, sk)
                self.tt("dve", t1[:, :n], ob[:, :n], rd[:, :n], ALU.mult, [("bank", h), ("rstd", sk)], [tk])
                self.tt("dve", t1[:, :n], t1[:, :n], nm[:, :n], ALU.add, [tk, ("nmr", sk)], [tk])
                self.stt("dve", ycat[:, 4 + h, :n], t1[:, :n], par[:, PAR_GN_G + h:PAR_GN_G + h + 1], sgate[:, h, :n], ALU.mult, ALU.mult,
                         [tk, ("sgate", h), f"par{l}"], [("ycat", 4 + h)])
            if STOP <= 6:
                sidx += nsub
                continue
            for fo in range(NFT):
                s = self.nxt("wsl", 3)
                self.dma("pool", wsl[s], w_outv[:, :, fo * 128:(fo + 1) * 128], [], [("wsl", s)], ("wsl", s))
                pb = self.nxt("dnbank", 2)
                py = self.banks[pb]
                for kt in range(NFT):
                    self.mm(py[:, :n], wsl[s][:, kt, :], ycat[:, kt, :n], kt == 0, kt == NFT - 1,
                            [("wsl", s), ("ycat", kt)], [("bank", pb)])
                self.ln_accum(0, fo, n, py, pb, 1.0 / ALPHA)
            self.ln_finish(l, 1, 0, ti, h_dst, eng=PENG)
            sidx += nsub
        P.barrier()
        A.reset(m0)

    def phase_final(self, h_src, out_d):
        A, P = self.A, self.P
        m0 = A.mark()
        xin = [A.alloc((8, 128), F32, "xin") for _ in range(2)]
        xo = [A.alloc((1024,), F32, "xo") for _ in range(2)]
        for s in range(BLK // 128):
            k = s % 2
            c0 = NPRE + s * 128
            self.dma("sp", xin[k], h_src[:, :, c0:c0 + 128], [], [("xin", k)], ("xin", k))
            for half in range(2):
                bk = self.nxt("finbank", 2)
                ps = self.banks[bk]
                for j in range(4):
                    ft = half * 4 + j
                    self.tr(ps[:, j * 128:(j + 1) * 128], xin[k][:, ft, :], self.ident_f, [("xin", k), "ident_f"], [("bank", bk)])
                self.act(xo[k][:, half * 512:(half + 1) * 512], ps[:, :], AF.Copy, [("bank", bk)], [("xo", k, half)])
            self.dma("sp", out_d[s * 128:(s + 1) * 128, :], xo[k], [("xo", k, 0), ("xo", k, 1)], [("xo", k, 0), ("xo", k, 1)], ("xo", k))
        P.barrier()
        A.reset(m0)


PAR_LN_G = 0
PAR_LN_B = 24
PAR_LNIN_G = 48
PAR_LNIN_B = 56
PAR_POOL_SCALE = 64
PAR_CONV_DB = 66
PAR_CONV_LN_G = 68
PAR_CONV_LN_B = 70
PAR_GN_G = 72
PAR_CONV_W = 76
NPAR = 138

TB_DM2 = 0
TB_QD = 512
TB_DTAB = 768
TB_COEF = 1280
TB_FLAG = 1292
TB_PC = 1294
TB_PCORR = 1302
NTAB = 1334


def pack_params(inp, l):
    p = np.zeros((128, NPAR), np.float32)
    for i in range(3):
        p[:, PAR_LN_G + 8 * i:PAR_LN_G + 8 * i + 8] = inp["ln_g"][l, i].reshape(8, 128).T
        p[:, PAR_LN_B + 8 * i:PAR_LN_B + 8 * i + 8] = inp["ln_b"][l, i].reshape(8, 128).T
    p[:, PAR_LNIN_G:PAR_LNIN_G + 8] = inp["ln_in_g"].reshape(8, 128).T
    p[:, PAR_LNIN_B:PAR_LNIN_B + 8] = inp["ln_in_b"].reshape(8, 128).T
    p[:, PAR_POOL_SCALE:PAR_POOL_SCALE + 2] = inp["pool_scale"][l].reshape(2, 128).T
    p[:, PAR_CONV_DB:PAR_CONV_DB + 2] = inp["conv_db"][l].reshape(2, 128).T
    p[:, PAR_CONV_LN_G:PAR_CONV_LN_G + 2] = inp["conv_ln_g"][l].reshape(2, 128).T
    p[:, PAR_CONV_LN_B:PAR_CONV_LN_B + 2] = inp["conv_ln_b"][l].reshape(2, 128).T
    p[:, PAR_GN_G:PAR_GN_G + 4] = inp["ret_gn_g"][l].reshape(4, 128).T
    cw = inp["conv_dw"][l]
    for tl in range(2):
        p[:, PAR_CONV_W + tl * 31:PAR_CONV_W + tl * 31 + 31] = cw[:, tl * 128:(tl + 1) * 128].T
    return p


def consts_arr():
    c = np.zeros((128, 256), np.float32)
    c[:, 0:128] = np.eye(128, dtype=np.float32)
    c[:, 128:256] = 1.0
    return c


def make_tables(jj):
    first = 1.0 if jj == 0 else 0.0
    g = np.array(GAMMAS, np.float64)
    pos = np.concatenate([np.arange(NPRE), NPRE + BLK * jj + np.arange(BLK)]).astype(np.float32)
    inv_freq = (np.float32(10000.0) ** (-np.arange(0, DH, 2, dtype=np.float32) / np.float32(DH))).astype(np.float32)
    ang = (pos[:, None] * inv_freq[None, :]).astype(np.float32)
    cos, sin = np.cos(ang).T, np.sin(ang).T
    rope = np.zeros((128, 2, T), np.float32)
    rope[0:64, 0], rope[64:128, 0] = cos, cos
    rope[0:64, 1], rope[64:128, 1] = sin, -sin
    vtab = np.zeros((128, 17, 2, 4), np.float64)
    ip = np.arange(16)
    for h in range(4):
        vtab[:16, 0, 0, h] = first * g[h] ** (15 - ip)
        vtab[:16, 0, 1, h] = first * g[h] ** (BLK + 15 - ip)
        for s in range(1, 17):
            nidx = (s - 1) * 128 + np.arange(128)
            vtab[:, s, 0, h] = g[h] ** (63 - (nidx % 64))
            vtab[:, s, 1, h] = g[h] ** (BLK - 1 - nidx)
    tab = np.zeros((128, NTAB), np.float64)
    j = np.arange(128)[:, None]
    i = np.arange(128)[None, :]
    same = (j // 64) == (i // 64)
    for h in range(4):
        tab[:, TB_DM2 + h * 128:TB_DM2 + (h + 1) * 128] = np.where(same, g[h] ** np.abs(i - j), 0.0) * QSCALE
        tab[:, TB_QD + h * 64:TB_QD + (h + 1) * 64] = (g[h] ** (np.arange(64) + 1.0))[None, :] * QSCALE
        tab[:, TB_DTAB + h * 128:TB_DTAB + (h + 1) * 128] = g[h] ** 64
        for s in range(3):
            tab[:, TB_COEF + 4 * s + h] = (g[h] ** (BLK * (jj - 1 - s))) if s < jj else 0.0
    tab[:, TB_FLAG] = first
    tab[:, TB_FLAG + 1] = 1.0 - first
    wins = (2, 4, 8, 16)
    for tl in range(2):
        for p in range(128):
            grp = (tl * 128 + p) // 64
            w = wins[grp]
            tab[p, TB_PC + tl * 4 + grp] = 1.0 / w
            tt_ = np.arange(16)
            tab[p, TB_PCORR + tl * 16:TB_PCORR + tl * 16 + 16] = w / np.minimum(tt_ + 1, w)
    return rope, vtab.reshape(128, 136).astype(np.float32), tab.astype(np.float32)


def _decl_common(B):
    B.setup_consts()
    rope = B.inp("rope", [128, 2, T])
    vtab = B.inp("vtab", [128, 136])
    tab = B.inp("tab", [128, NTAB])
    return rope, vtab, tab


def _w(B, l):
    return dict(
        ffn1_w13=B.inp(f"ffn1_w13_{l}", [D, 2 * DFF]), ffn1_w2=B.inp(f"ffn1_w2_{l}", [DFF, D]),
        ffn2_w13=B.inp(f"ffn2_w13_{l}", [D, 2 * DFF]), ffn2_w2=B.inp(f"ffn2_w2_{l}", [DFF, D]),
        w_in=B.inp(f"w_in_{l}", [D, DIN]), w_out=B.inp(f"w_out_{l}", [D, D]),
        pool_w=B.inp(f"pool_w_{l}", [4, 64, 64]), conv_pw=B.inp(f"conv_pw_{l}", [256, 256]))


def build_launch(kind):
    nc = bass.Bass("TRN2", target_bir_lowering=False)
    B = Builder(nc)
    if kind == 0:
        x = B.inp("x", [BLK, D])
        xpre = B.inp("xpre", [NPRE, D])
        rope, vtab, tab = _decl_common(B)
        w13, w2, w_in = B.inp("ffn1_w13_0", [D, 2 * DFF]), B.inp("ffn1_w2_0", [DFF, D]), B.inp("w_in_0", [D, DIN])
        h0 = B.scratch("h0", [128, NFT, T])
        h1 = B.outp("h1", [128, NFT, T])
        send = B.outp("send", [128, 640])
        B.phase_inln(x, xpre, h0)
        B.load_hb(h0)
        B.phase_ffn(0, w13, w2, 0, h0, h1)
        B.phase_kv(0, w_in, rope, vtab, send)
    else:
        l = kind - 1
        hin = B.inp("hin", [128, NFT, T])
        sprev = B.inp("sprev", [128, 3, 640])
        hprev = B.inp("hprev", [128, 128])
        rope, vtab, tab = _decl_common(B)
        w_in, w_out = B.inp(f"w_in_{l}", [D, DIN]), B.inp(f"w_out_{l}", [D, D])
        pool_w, conv_pw = B.inp(f"pool_w_{l}", [4, 64, 64]), B.inp(f"conv_pw_{l}", [256, 256])
        f2a, f2b = B.inp(f"ffn2_w13_{l}", [D, 2 * DFF]), B.inp(f"ffn2_w2_{l}", [DFF, D])
        hA = B.scratch("hA", [128, NFT, T])
        hB = B.scratch("hB", [128, NFT, T])
        B.load_hb(hin)
        B.phase_mix(l, w_in, pool_w, conv_pw, w_out, rope, vtab, tab, [sprev[:, i, 0:512] for i in range(3)], hprev[:, :], hin, hA)
        B.phase_ffn(l, f2a, f2b, 2, hA, hB)
        if l + 1 < DEPTH:
            w13, w2, w_in2 = B.inp(f"ffn1_w13_{l + 1}", [D, 2 * DFF]), B.inp(f"ffn1_w2_{l + 1}", [DFF, D]), B.inp(f"w_in_{l + 1}", [D, DIN])
            h1 = B.outp("h1", [128, NFT, T])
            send = B.outp("send", [128, 640])
            B.phase_ffn(l + 1, w13, w2, 0, hB, h1)
            B.phase_kv(l + 1, w_in2, rope, vtab, send)
        else:
            out = B.outp("out", [BLK, D])
            B.phase_final(hB, out)
    B.P.emit(nc)
    return nc, B


def build_mix_debug(l, stop, ntiles):
    nc = bass.Bass("TRN2", target_bir_lowering=False)
    B = Builder(nc)
    B.mix_stop, B.mix_tiles = stop, ntiles
    hin = B.inp("hin", [128, NFT, T])
    sprev = B.inp("sprev", [128, 3, 640])
    hprev = B.inp("hprev", [128, 128])
    rope, vtab, tab = _decl_common(B)
    w_in, w_out = B.inp(f"w_in_{l}", [D, DIN]), B.inp(f"w_out_{l}", [D, D])
    pool_w, conv_pw = B.inp(f"pool_w_{l}", [4, 64, 64]), B.inp(f"conv_pw_{l}", [256, 256])
    hA = B.outp("h1", [128, NFT, T])
    B.load_hb(hin)
    B.phase_mix(l, w_in, pool_w, conv_pw, w_out, rope, vtab, tab, [sprev[:, i, 0:512] for i in range(3)], hprev[:, :], hin, hA)
    B.P.emit(nc)
    return nc, B


def build_fused():
    nc = bass.Bass("TRN2", target_bir_lowering=False)
    B = Builder(nc)
    x = B.inp("x", [4 * BLK, D])
    xpre = B.inp("xpre", [4, NPRE, D])
    zeros = B.inp("zeros", [128, 640])
    B.setup_consts()
    ropes = [B.inp(f"rope{b}", [128, 2, T]) for b in range(4)]
    vtabs = [B.inp(f"vtab{b}", [128, 136]) for b in range(4)]
    tabs = [B.inp(f"tab{b}", [128, NTAB]) for b in range(4)]
    W = [_w(B, l) for l in range(DEPTH)]
    out = B.outp("out", [4 * BLK, D])
    hX = [B.scratch(f"hX{b}", [128, NFT, T]) for b in range(4)]
    hY = [B.scratch(f"hY{b}", [128, NFT, T]) for b in range(4)]
    hA = B.scratch("hA", [128, NFT, T])
    hB = B.scratch("hB", [128, NFT, T])
    send = [[B.scratch(f"send{l}_{b}", [128, 640]) for b in range(4)] for l in range(DEPTH)]
    for b in range(4):
        B.phase_inln(x[b * BLK:(b + 1) * BLK, :], xpre[b], hX[b])
    for b in range(4):
        B.load_hb(hX[b])
        B.phase_ffn(0, W[0]["ffn1_w13"], W[0]["ffn1_w2"], 0, hX[b], hY[b])
        B.phase_kv(0, W[0]["w_in"], ropes[b], vtabs[b], send[0][b])
    for l in range(DEPTH):
        for b in range(4):
            B.load_hb(hY[b])
            sp = [send[l][i][:, 0:512] if i < b else zeros[:, 0:512] for i in range(3)]
            hp = send[l][b - 1][:, 512:640] if b > 0 else zeros[:, 512:640]
            B.phase_mix(l, W[l]["w_in"], W[l]["pool_w"], W[l]["conv_pw"], W[l]["w_out"], ropes[b], vtabs[b], tabs[b], sp, hp, hY[b], hA)
            B.phase_ffn(l, W[l]["ffn2_w13"], W[l]["ffn2_w2"], 2, hA, hB)
            if l + 1 < DEPTH:
                B.phase_ffn(l + 1, W[l + 1]["ffn1_w13"], W[l + 1]["ffn1_w2"], 0, hB, hY[b])
                B.phase_kv(l + 1, W[l + 1]["w_in"], ropes[b], vtabs[b], send[l + 1][b])
            else:
                B.phase_final(hB, out[b * BLK:(b + 1) * BLK, :])
    B.P.emit(nc)
    return nc, B


def kernel_fused(inp):
    nc, B = _get("fused")
    cst = consts_arr()
    tabs = [make_tables(jj) for jj in range(4)]
    base = {"consts": cst, "zeros": np.zeros((128, 640), np.float32)}
    for l in range(DEPTH):
        base[f"par{l}"] = pack_params(inp, l)
        for n in ["ffn1_w13", "ffn1_w2", "ffn2_w13", "ffn2_w2", "w_in", "w_out", "pool_w", "conv_pw"]:
            base[f"{n}_{l}"] = np.ascontiguousarray(inp[n][l])
    for b in range(4):
        base[f"rope{b}"], base[f"vtab{b}"], base[f"tab{b}"] = tabs[b]
    xpre = np.zeros((4, NPRE, D), np.float32)
    xpre[0] = inp["meta"]
    maps = []
    for c in range(8):
        m = dict(base)
        m["x"] = np.ascontiguousarray(inp["x"][c % 2])
        m["xpre"] = xpre
        maps.append({k: m[k] for k in B.din})
    res = run_bass_kernel_spmd(nc, maps, core_ids=list(range(8))).results
    return np.stack([res[0]["out"], res[1]["out"]], axis=0).astype(np.float32)


_CACHE = {}


def _get(kind):
    if kind not in _CACHE:
        _CACHE[kind] = build_fused() if kind == "fused" else build_launch(kind)
    return _CACHE[kind]


FUSED = True


def _exchange(sends):
    sprevs, hprevs = [], []
    for c in range(8):
        bi, jj = divmod(c, 4)
        sp = np.zeros((128, 3, 640), np.float32)
        for s in range(jj):
            sp[:, s, :] = sends[bi * 4 + s]
        hp = np.zeros((128, 128), np.float32)
        if jj > 0:
            hp[:] = sends[c - 1][:, 512:640]
        sprevs.append(sp)
        hprevs.append(hp)
    return sprevs, hprevs


def kernel(**inputs):
    inp = {k: np.asarray(v) for k, v in inputs.items()}
    if FUSED:
        return kernel_fused(inp)
    x = inp["x"]
    cst = consts_arr()
    pars = [pack_params(inp, l) for l in range(DEPTH)]
    tabs = [make_tables(jj) for jj in range(4)]
    common = []
    for c in range(8):
        bi, jj = divmod(c, 4)
        rope, vtab, tab = tabs[jj]
        common.append({"consts": cst, "par0": pars[0], "par1": pars[1], "rope": rope, "vtab": vtab, "tab": tab})

    def wsel(names, l):
        return {f"{n}_{l}": np.ascontiguousarray(inp[n][l]) for n in names}

    nc, B = _get(0)
    maps = []
    for c in range(8):
        bi, jj = divmod(c, 4)
        m = dict(common[c])
        m["x"] = np.ascontiguousarray(x[bi, jj * BLK:(jj + 1) * BLK])
        m["xpre"] = inp["meta"] if jj == 0 else np.zeros((NPRE, D), np.float32)
        m.update(wsel(["ffn1_w13", "ffn1_w2", "w_in"], 0))
        maps.append({k: m[k] for k in B.din})
    res = run_bass_kernel_spmd(nc, maps, core_ids=list(range(8))).results
    out = None
    for l in range(DEPTH):
        nc, B = _get(l + 1)
        sprevs, hprevs = _exchange([res[c]["send"] for c in range(8)])
        maps = []
        for c in range(8):
            m = dict(common[c])
            m["hin"] = res[c]["h1"]
            m["sprev"] = sprevs[c]
            m["hprev"] = hprevs[c]
            m.update(wsel(["w_in", "w_out", "pool_w", "conv_pw", "ffn2_w13", "ffn2_w2"], l))
            if l + 1 < DEPTH:
                m.update(wsel(["ffn1_w13", "ffn1_w2", "w_in"], l + 1))
            maps.append({k: m[k] for k in B.din})
        res = run_bass_kernel_spmd(nc, maps, core_ids=list(range(8))).results
    out = np.stack([np.concatenate([res[bi * 4 + jj]["out"] for jj in range(4)], axis=0) for bi in range(2)], axis=0)
    return out.astype(np.float32)
```

```python
import numpy as np
from contextlib import ExitStack
import concourse.bass as bass
import concourse.mybir as mybir
from concourse.bass_utils import run_bass_kernel_spmd

F32 = mybir.dt.float32
BF16 = mybir.dt.bfloat16
AF = mybir.ActivationFunctionType
ALU = mybir.AluOpType

D = 1024
NFT = 8
DFF = 2752
NHT = 22
DIN = 2816
NPRE = 16
BLK = 2048
T = NPRE + BLK
TILES = [(0, 16), (16, 512), (528, 512), (1040, 512), (1552, 512)]
HALVES = [(0, 1, 2), (3, 4)]
HALF_OFF = [0, 1040]
HALF_LEN = [1040, 1024]
DEPTH = 2
ALPHA = (2.0 * DEPTH) ** 0.25
LN_EPS = 1e-5
EPS2 = LN_EPS / (ALPHA * ALPHA)
CHUNK = 64
NHEAD = 4
DH = 128
GAMMAS = [1.0 - 2.0 ** (-5.0 - h) for h in range(NHEAD)]
QSCALE = DH ** -0.5

ENGS = ("pe", "act", "dve", "pool", "sp")
import os
SAME_SYNC = os.environ.get("SAME_SYNC", "1") == "1"
PENG = os.environ.get("PENG", "dve")
SCHED = os.environ.get("SCHED", "1") == "1"


class Op:
    __slots__ = ("eng", "fn", "deps", "lane", "needs", "val", "stream", "seg", "dur", "idx", "succ", "pending", "ready", "fin")


class Prog:
    def __init__(self):
        self.ops = {e: [] for e in ENGS}
        self.lastw = {}
        self.readers = {}
        self.lane_cnt = {}
        self.last_in_stream = {}
        self.seg = 0
        self.count = 0

    def add(self, eng, fn, reads=(), writes=(), lane=None, dur=0.5):
        o = Op()
        o.eng, o.fn, o.lane, o.needs, o.val = eng, fn, lane, False, None
        o.seg, o.dur, o.idx = self.seg, dur, self.count
        self.count += 1
        o.stream = lane if lane is not None else eng
        deps = {}
        for r in reads:
            p = self.lastw.get(r)
            if p is not None:
                deps[id(p)] = p
            if isinstance(r, tuple) and r[0] == "bank":
                for p in self.readers.get(r, ()):
                    if p.stream != o.stream:
                        deps[id(p)] = p
        for r in writes:
            p = self.lastw.get(r)
            if p is not None:
                deps[id(p)] = p
            for p in self.readers.get(r, ()):
                deps[id(p)] = p
        o.deps = list(deps.values())
        for p in o.deps:
            p.needs = True
        for r in writes:
            self.lastw[r] = o
            self.readers[r] = []
        for r in reads:
            self.readers.setdefault(r, []).append(o)
        if lane is not None:
            self.lane_cnt[lane] = self.lane_cnt.get(lane, 0) + 16
            o.val = self.lane_cnt[lane]
        self.ops[eng].append(o)
        self.last_in_stream[o.stream] = o
        return o

    def barrier(self):
        lasts = list(self.last_in_stream.values())
        for p in lasts:
            p.needs = True
        for e in ENGS:
            o = Op()
            o.eng, o.fn, o.lane, o.needs, o.val = e, None, None, False, None
            o.stream = e
            o.seg, o.dur, o.idx = self.seg, 0.0, self.count
            o.deps = [p for p in lasts]
            self.ops[e].append(o)
        self.count += 1
        self.seg += 1
        self.lastw = {}
        self.readers = {}

    def schedule(self, window=24, hop=1.8):
        REORD = ("pe", "act", "dve")
        nseg = self.seg + 1
        per = {e: [[] for _ in range(nseg)] for e in ENGS}
        bars = {e: [None] * nseg for e in ENGS}
        for e in ENGS:
            for o in self.ops[e]:
                if o.fn is None:
                    bars[e][o.seg] = o
                else:
                    per[e][o.seg].append(o)
        new = {e: [] for e in ENGS}
        for sg in range(nseg):
            ops = [o for e in ENGS for o in per[e][sg]]
            for o in ops:
                o.succ, o.pending, o.ready, o.fin = [], 0, 0.0, None
            inseg = set(id(o) for o in ops)
            for o in ops:
                for p in o.deps:
                    if id(p) in inseg and p is not o:
                        p.succ.append(o)
                        o.pending += 1
            queues = {e: list(per[e][sg]) for e in ENGS}
            tfree = {e: 0.0 for e in ENGS}
            order = {e: [] for e in ENGS}
            remaining = len(ops)
            while remaining:
                best = None
                for e in ENGS:
                    q = queues[e]
                    if not q:
                        continue
                    lim = window if e in REORD else 1
                    cnt = 0
                    for k, o in enumerate(q):
                        if cnt >= lim:
                            break
                        cnt += 1
                        if o.pending:
                            continue
                        st = max(tfree[e], o.ready)
                        key = (st, o.idx)
                        if best is None or key < best[0]:
                            best = (key, e, k, o)
                if best is None:
                    raise RuntimeError("scheduler deadlock")
                (st, _), e, k, o = best
                del queues[e][k]
                order[e].append(o)
                if o.lane is not None:
                    tfree[e] = st + 0.3
                    o.fin = st + o.dur
                else:
                    o.fin = st + o.dur + (0.15 if e == "pe" else 0.0)
                    tfree[e] = st + o.dur
                for y in o.succ:
                    y.pending -= 1
                    if o.lane is None and o.eng == y.eng:
                        h = 0.0 if e == "pe" else 0.35
                    else:
                        h = hop
                    if o.fin + h > y.ready:
                        y.ready = o.fin + h
                remaining -= 1
            lasts = []
            for e in ENGS:
                if e in ("sp", "pool"):
                    seen = {}
                    for o in order[e]:
                        seen[o.stream] = o
                    lasts.extend(seen.values())
                elif order[e]:
                    lasts.append(order[e][-1])
            for p in lasts:
                p.needs = True
            for e in ENGS:
                new[e].extend(order[e])
                b = bars[e][sg]
                if b is not None:
                    b.deps = list(lasts)
                    new[e].append(b)
        self.ops = new

    def emit(self, nc):
        if SCHED:
            self.schedule()
        for e in ENGS:
            c = 0
            for o in self.ops[e]:
                if o.lane is None and o.needs and o.fn is not None:
                    c += 1
                    o.val = c
                elif o.lane is None:
                    o.val = None
        with ExitStack() as es:
            sems = {}
            for e in ENGS[:4]:
                sems[e] = es.enter_context(nc.semaphore("sem_" + e))
            for ln in self.lane_cnt:
                sems[ln] = es.enter_context(nc.semaphore("lane_" + str(ln)))
            block = es.enter_context(nc.Block())

            def run(e, engine):
                known = {}
                for o in self.ops[e]:
                    need = {}
                    for p in o.deps:
                        if p is o:
                            continue
                        if p.lane is not None and p.lane == o.lane:
                            continue
                        if p.lane is None:
                            if p.eng == e and (e == "pe" or not SAME_SYNC):
                                continue
                            if p.val is None:
                                continue
                        key = p.stream
                        if p.val > need.get(key, 0):
                            need[key] = p.val
                    for key, val in need.items():
                        if known.get(key, 0) >= val:
                            continue
                        engine.wait_ge(sems[key], val)
                        known[key] = val
                    if o.fn is None:
                        continue
                    inst = o.fn(engine)
                    if o.lane is not None:
                        inst.then_inc(sems[o.lane], 16)
                    elif o.needs:
                        inst.then_inc(sems[e], 1)

            @block.tensor
            def _(eng):
                run("pe", eng)

            @block.scalar
            def _(eng):
                run("act", eng)

            @block.vector
            def _(eng):
                run("dve", eng)

            @block.gpsimd
            def _(eng):
                run("pool", eng)

            @block.sync
            def _(eng):
                run("sp", eng)


class Arena:
    def __init__(self, t, nelem):
        self.t = t
        self.n = nelem
        self.off = 0
        self.uid = 0

    def mark(self):
        return self.off

    def reset(self, m):
        self.off = m

    def alloc(self, free_shape, dtype, name):
        n = int(np.prod(free_shape))
        if dtype == F32:
            self.off += self.off % 2
            ap = self.t[:, self.off:self.off + 2 * n].bitcast(F32)
            self.off += 2 * n
        else:
            ap = self.t[:, self.off:self.off + n]
            self.off += n
        assert self.off <= self.n, ("SBUF arena overflow", name, self.off, self.n)
        if len(free_shape) == 2:
            ap = ap.rearrange("p (a b) -> p a b", a=free_shape[0])
        elif len(free_shape) == 3:
            ap = ap.rearrange("p (a b c) -> p a b c", a=free_shape[0], b=free_shape[1])
        self.uid += 1
        return ap


class Builder:
    def __init__(self, nc):
        self.nc = nc
        self.P = Prog()
        self.es = ExitStack()
        ARENA_ELEMS = 103000
        at = self.es.enter_context(nc.sbuf_tensor("arena", [128, ARENA_ELEMS], BF16))
        self.A = Arena(at, ARENA_ELEMS)
        self.banks = [self.es.enter_context(nc.psum_tensor(f"bank{i}", [128, 512], F32)) for i in range(8)]
        self.din = {}
        self.dout = {}
        self.rot = {}

    def inp(self, name, shape):
        if name not in self.din:
            self.din[name] = self.nc.dram_tensor(name, list(shape), F32, kind="ExternalInput").ap()
        return self.din[name]

    def outp(self, name, shape):
        self.dout[name] = self.nc.dram_tensor(name, list(shape), F32, kind="ExternalOutput").ap()
        return self.dout[name]

    def scratch(self, name, shape):
        return self.nc.dram_tensor(name, list(shape), F32, kind="Internal").ap()

    @staticmethod
    def _fs(ap):
        return int(np.prod(ap.shape[1:]))

    def mm(self, out, lhsT, rhs, start, stop, reads, writes):
        return self.P.add("pe", lambda e: e.matmul(out, lhsT, rhs, start=start, stop=stop), reads, writes,
                          dur=max(self._fs(rhs), 64) * 0.00048 + 0.02)

    def tr(self, out, in_, ident, reads, writes):
        return self.P.add("pe", lambda e: e.transpose(out, in_, ident), reads, writes, dur=0.1)

    def act(self, out, in_, func, reads, writes, scale=1.0, bias=0.0):
        return self.P.add("act", lambda e: e.activation(out=out, in_=in_, func=func, bias=bias, scale=scale), reads, writes,
                          dur=self._fs(out) * 0.00095 + 0.22)

    def tt(self, eng, out, in0, in1, op, reads, writes):
        return self.P.add(eng, lambda e: e.tensor_tensor(out=out, in0=in0, in1=in1, op=op), reads, writes, dur=self._fs(out) * 0.00105 + 0.12)

    def ts(self, eng, out, in0, s1, s2, op0, op1, reads, writes):
        return self.P.add(eng, lambda e: e.tensor_scalar(out=out, in0=in0, scalar1=s1, scalar2=s2, op0=op0, op1=op1), reads, writes, dur=self._fs(out) * 0.00105 + 0.12)

    def stt(self, eng, out, in0, scalar, in1, op0, op1, reads, writes):
        return self.P.add(eng, lambda e: e.scalar_tensor_tensor(out=out, in0=in0, scalar=scalar, in1=in1, op0=op0, op1=op1), reads, writes, dur=self._fs(out) * 0.00105 + 0.12)

    def ts1(self, eng, out, in_, scalar, op, reads, writes):
        return self.P.add(eng, lambda e: e.tensor_single_scalar(out=out, in_=in_, scalar=scalar, op=op), reads, writes, dur=self._fs(out) * 0.00105 + 0.12)

    def copy(self, eng, out, in_, reads, writes):
        return self.P.add(eng, lambda e: e.tensor_copy(out=out, in_=in_), reads, writes, dur=self._fs(out) * 0.00105 + 0.12)

    def dma(self, eng, out, in_, reads, writes, lane):
        return self.P.add(eng, lambda e: e.dma_start(out=out, in_=in_), reads, writes, lane=lane,
                          dur=2.0 + int(np.prod(out.shape)) * 4 / 150e3)

    def rsqrt(self, out, in_, eps, reads, writes):
        self.act(out, in_, AF.Ln, reads, writes, scale=1.0, bias=eps)
        self.act(out, out, AF.Exp, writes, writes, scale=-0.5)

    def nxt(self, name, n):
        i = self.rot.get(name, 0)
        self.rot[name] = i + 1
        return i % n

    def setup_consts(self):
        A = self.A
        cst = self.inp("consts", [128, 256])
        self.ident_f = A.alloc((128,), F32, "ident_f")
        self.ident_b = A.alloc((128,), BF16, "ident_b")
        self.ones_b = A.alloc((128,), BF16, "ones_b")
        self.dma("sp", self.ident_f, cst[:, 0:128], [], ["ident_f"], "c0")
        self.dma("pool", self.ident_b, cst[:, 0:128], [], ["ident_b"], "c1")
        self.dma("pool", self.ones_b, cst[:, 128:256], [], ["ones_b"], "c1")
        self.par = []
        for l in range(DEPTH):
            p = A.alloc((NPAR,), F32, f"par{l}")
            self.dma("sp", p, self.inp(f"par{l}", [128, NPAR]), [], [f"par{l}"], "c0")
            self.par.append(p)

    def phase_inln(self, x_ap, xpre_ap, h_dst):
        A, P = self.A, self.P
        m0 = A.mark()
        xt = [A.alloc((1024,), F32, "xt") for _ in range(2)]
        xn = [A.alloc((1024,), F32, "xn") for _ in range(2)]
        st = [A.alloc((2, 6), F32, "st") for _ in range(2)]
        mv = [A.alloc((2,), F32, "mv") for _ in range(2)]
        rs = [A.alloc((1,), F32, "rs") for _ in range(2)]
        stage = [A.alloc((8, 128), F32, "stage") for _ in range(2)]
        par = self.par[0]
        subt = [(xpre_ap, 0, 16, 0)] + [(x_ap, s * 128, 128, NPRE + s * 128) for s in range(BLK // 128)]
        for i, (src, r0, n, toff) in enumerate(subt):
            k = i % 2
            self.dma("sp", xt[k][:n, :], src[r0:r0 + n, :], [], [("xt", k)], ("xt", k))
            for c in range(2):
                P.add("dve", lambda e, k=k, c=c, n=n: e.bn_stats(out=st[k][:n, c, :], in_=xt[k][:n, c * 512:(c + 1) * 512]),
                      [("xt", k)], [("st", k, c)])
            P.add("dve", lambda e, k=k, n=n: e.bn_aggr(out=mv[k][:n, :], in_=st[k][:n, :, :].rearrange("p a b -> p (a b)")),
                  [("st", k, 0), ("st", k, 1)], [("mv", k)])
            self.rsqrt(rs[k][:n, :], mv[k][:n, 1:2], LN_EPS, [("mv", k)], [("rs", k)])
            self.ts("dve", xn[k][:n, :], xt[k][:n, :], mv[k][:n, 0:1], rs[k][:n, 0:1], ALU.subtract, ALU.mult,
                    [("xt", k), ("mv", k), ("rs", k)], [("xn", k)])
            for half in range(2):
                bk = self.nxt("inln_bank", 2)
                ps = self.banks[bk]
                for j in range(4):
                    ft = half * 4 + j
                    self.tr(ps[:, j * 128:j * 128 + n], xn[k][:n, ft * 128:(ft + 1) * 128], self.ident_f[:n, :n],
                            [("xn", k), "ident_f"], [("bank", bk)])
                for j in range(4):
                    ft = half * 4 + j
                    self.act(stage[k][:, ft, :n], ps[:, j * 128:j * 128 + n], AF.Identity,
                             [("bank", bk), "par0"], [("stage", k)],
                             scale=par[:, PAR_LNIN_G + ft:PAR_LNIN_G + ft + 1], bias=par[:, PAR_LNIN_B + ft:PAR_LNIN_B + ft + 1])
            self.dma("sp", h_dst[:, :, toff:toff + n], stage[k][:, :, :n], [("stage", k)], [], ("stg", k))
        P.barrier()
        A.reset(m0)

    def load_hb(self, h_src):
        if not hasattr(self, "hb"):
            self.hb = self.A.alloc((NFT, T), BF16, "hb")
        for ti, (o, n) in enumerate(TILES):
            self.dma("pool", self.hb[:, :, o:o + n], h_src[:, :, o:o + n], [],
                     [("hb", ft, ti) for ft in range(NFT)], ("hbld", ti))

    def ln_alloc(self, ntile):
        A = self.A
        self.yp = [A.alloc((8, 512), F32, "yp") for _ in range(ntile)]
        self.sq = [A.alloc((512,), BF16, "sq") for _ in range(2)]
        self.yb = [A.alloc((512,), BF16, "yb") for _ in range(2)]
        self.mean = [A.alloc((512,), F32, "mean") for _ in range(ntile)]
        self.rstd = [A.alloc((512,), F32, "rstd") for _ in range(ntile)]
        self.nmr = [A.alloc((512,), F32, "nmr") for _ in range(ntile)]

    def ln_accum(self, li, fo, n, py, pb, cres):
        yp = self.yp[li]
        self.stt("dve", yp[:, fo, :n], py[:, :n], cres, yp[:, fo, :n], ALU.mult, ALU.add,
                 [("bank", pb), ("yp", li)], [("ypf", li, fo)])
        q = self.nxt("sq", 2)
        self.act(self.sq[q][:, :n], yp[:, fo, :n], AF.Square, [("ypf", li, fo)], [("sq", q)])
        self.act(self.yb[q][:, :n], yp[:, fo, :n], AF.Copy, [("ypf", li, fo)], [("yb", q)])
        bs, bq = 2 + 2 * li, 3 + 2 * li
        self.mm(self.banks[bs][:, :n], self.ones_b, self.yb[q][:, :n], fo == 0, fo == NFT - 1, ["ones_b", ("yb", q)], [("bank", bs)])
        self.mm(self.banks[bq][:, :n], self.ones_b, self.sq[q][:, :n], fo == 0, fo == NFT - 1, ["ones_b", ("sq", q)], [("bank", bq)])

    def stats_finish(self, bs, bq, n, inv, eps, mn, rd, nm, key):
        self.ts1("dve", mn[:, :n], self.banks[bs][:, :n], inv, ALU.mult, [("bank", bs)], [("mean", key)])
        self.tt("dve", nm[:, :n], mn[:, :n], mn[:, :n], ALU.mult, [("mean", key)], [("nmr", key)])
        self.stt("dve", rd[:, :n], self.banks[bq][:, :n], inv, nm[:, :n], ALU.mult, ALU.subtract,
                 [("bank", bq), ("nmr", key)], [("rstd", key)])
        self.rsqrt(rd[:, :n], rd[:, :n], eps, [("rstd", key)], [("rstd", key)])
        self.stt("dve", nm[:, :n], mn[:, :n], -1.0, rd[:, :n], ALU.mult, ALU.mult, [("mean", key), ("rstd", key)], [("nmr", key)])

    def ln_finish(self, l, lnidx, li, ti, h_dst, eng="dve"):
        par = self.par[l]
        gcol = PAR_LN_G + lnidx * 8
        bcol = PAR_LN_B + lnidx * 8
        o, n = TILES[ti]
        yp, hb = self.yp[li], self.hb
        bs, bq = 2 + 2 * li, 3 + 2 * li
        mn, rd, nm = self.mean[li], self.rstd[li], self.nmr[li]
        self.stats_finish(bs, bq, n, 1.0 / D, EPS2, mn, rd, nm, li)
        for fo in range(NFT):
            self.tt(eng, yp[:, fo, :n], yp[:, fo, :n], rd[:, :n], ALU.mult, [("ypf", li, fo), ("rstd", li)], [("ypf", li, fo)])
            self.tt(eng, yp[:, fo, :n], yp[:, fo, :n], nm[:, :n], ALU.add, [("ypf", li, fo), ("nmr", li)], [("ypf", li, fo)])
            self.act(hb[:, fo, o:o + n], yp[:, fo, :n], AF.Identity, [("ypf", li, fo), f"par{l}"], [("hb", fo, ti)],
                     scale=par[:, gcol + fo:gcol + fo + 1], bias=par[:, bcol + fo:bcol + fo + 1])
            self.act(yp[:, fo, :n], yp[:, fo, :n], AF.Identity, [("ypf", li, fo), f"par{l}"], [("ypf", li, fo)],
                     scale=par[:, gcol + fo:gcol + fo + 1], bias=par[:, bcol + fo:bcol + fo + 1])
        self.dma("sp", h_dst[:, :, o:o + n], yp[:, :, :n], [("ypf", li, fo) for fo in range(NFT)] + [("yp", li)],
                 [("yp", li)], ("ypst", li))

    def phase_ffn(self, l, w13, w2, lnidx, h_src, h_dst):
        A, P = self.A, self.P
        m0 = A.mark()
        hid = A.alloc((NHT, 1040), BF16, "hid")
        w13s = [A.alloc((8, 2, 128), BF16, "w13s") for _ in range(3)]
        w2s = [A.alloc((NHT, 128), BF16, "w2s") for _ in range(2)]
        self.ln_alloc(3)
        sg = [A.alloc((512,), F32, "sg") for _ in range(4)]
        hb = self.hb
        w13v = w13.rearrange("(kt p) c -> p kt c", p=128)
        CRES = 0.5 / ALPHA
        HZ = {0: (0,), 1: (1,), 2: (2,), 3: (0, 1), 4: (1, 2)}
        for hi, tiles in enumerate(HALVES):
            hoff = HALF_OFF[hi]
            for li, ti in enumerate(tiles):
                o, n = TILES[ti]
                self.dma("sp", self.yp[li][:, :, :n], h_src[:, :, o:o + n], [], [("yp", li)], ("ypld", li))
            for m in range(NHT):
                mw = 128 if m < NHT - 1 else 64
                s = self.nxt("w13s", 3)
                self.dma("pool", w13s[s][:, :, 0, :mw], w13v[:, :, m * 128:m * 128 + mw], [], [("w13s", s)], ("w13", s))
                self.dma("pool", w13s[s][:, :, 1, :mw], w13v[:, :, DFF + m * 128:DFF + m * 128 + mw], [], [("w13s", s)], ("w13", s))
                for ti in tiles:
                    o, n = TILES[ti]
                    pb = self.nxt("upbank", 4) * 2
                    pa, pu = self.banks[pb], self.banks[pb + 1]
                    for kt in range(NFT):
                        self.mm(pa[:mw, :n], w13s[s][:, kt, 0, :mw], hb[:, kt, o:o + n], kt == 0, kt == NFT - 1,
                                [("w13s", s), ("hb", kt, ti)], [("bank", pb)])
                    for kt in range(NFT):
                        self.mm(pu[:mw, :n], w13s[s][:, kt, 1, :mw], hb[:, kt, o:o + n], kt == 0, kt == NFT - 1,
                                [("w13s", s), ("hb", kt, ti)], [("bank", pb + 1)])
                    g = self.nxt("sg", 4)
                    self.act(sg[g][:mw, :n], pa[:mw, :n], AF.Silu, [("bank", pb)], [("sg", g)])
                    self.tt("dve", hid[:mw, m, o - hoff:o - hoff + n], sg[g][:mw, :n], pu[:mw, :n], ALU.mult,
                            [("sg", g), ("bank", pb + 1)], [("hid", m, z) for z in HZ[ti]])
            for fo in range(NFT):
                s = self.nxt("w2s", 2)
                self.dma("pool", w2s[s][:, 0:NHT - 1, :], w2[0:(NHT - 1) * 128, fo * 128:(fo + 1) * 128].rearrange("(kt p) c -> p kt c", p=128),
                         [], [("w2s", s)], ("w2", s))
                self.dma("pool", w2s[s][0:64, NHT - 1, :], w2[(NHT - 1) * 128:DFF, fo * 128:(fo + 1) * 128], [], [("w2s", s)], ("w2", s))
                for li, ti in enumerate(tiles):
                    o, n = TILES[ti]
                    pb = self.nxt("dnbank", 2)
                    py = self.banks[pb]
                    for m in range(NHT):
                        mw = 128 if m < NHT - 1 else 64
                        self.mm(py[:, :n], w2s[s][:mw, m, :], hid[:mw, m, o - hoff:o - hoff + n], m == 0, m == NHT - 1,
                                [("w2s", s)] + [("hid", m, z) for z in HZ[ti]], [("bank", pb)])
                    self.ln_accum(li, fo, n, py, pb, CRES)
            for li, ti in enumerate(tiles):
                self.ln_finish(l, lnidx, li, ti, h_dst)
        P.barrier()
        A.reset(m0)

    def rope(self, ps, pbkey, rp, rkey, n, dst, dkey, sum_eng="dve"):
        a, b = self.ropeA, self.ropeB
        self.tt("dve", a[:, :n], ps[:, :n], rp[:, 0, :n], ALU.mult, [pbkey, rkey], ["ropeA"])
        self.tt("dve", b[0:64, :n], ps[64:128, :n], rp[64:128, 1, :n], ALU.mult, [pbkey, rkey], ["ropeB0"])
        self.tt("dve", b[64:128, :n], ps[0:64, :n], rp[0:64, 1, :n], ALU.mult, [pbkey, rkey], ["ropeB1"])
        self.tt(sum_eng, dst, a[:, :n], b[:, :n], ALU.add, ["ropeA", "ropeB0", "ropeB1"], [dkey])

    def load_rope(self, rope_d, ti):
        o, n = TILES[ti]
        k = self.nxt("ropet", 2)
        self.dma("sp", self.ropet[k][:, :, :n], rope_d[:, :, o:o + n], [], [("ropet", k)], ("ropet", k))
        return self.ropet[k], ("ropet", k)

    def phase_kv(self, l, w_in, rope_d, vtab_d, send_d):
        A, P = self.A, self.P
        m0 = A.mark()
        hb = self.hb
        wk = A.alloc((8, 512), BF16, "wk")
        wv = A.alloc((8, 512), BF16, "wv")
        wh = A.alloc((8, 768), BF16, "wh")
        self.ropet = [A.alloc((2, 512), F32, "ropet") for _ in range(2)]
        self.ropeA = A.alloc((512,), F32, "ropeA")
        self.ropeB = A.alloc((512,), F32, "ropeB")
        vtab = A.alloc((17, 2, 4), F32, "vtab")
        kT = A.alloc((4, 512), BF16, "kT")
        vfull = [A.alloc((4, 128), BF16, "vfull") for _ in range(2)]
        kTok = [A.alloc((4, 128), BF16, "kTok") for _ in range(2)]
        send = A.alloc((640,), F32, "send")
        sgm = A.alloc((2, 32), F32, "sgm")
        w_inv = w_in.rearrange("(kt p) c -> p kt c", p=128)
        self.dma("pool", wk, w_inv[:, :, 1280:1792], [], ["wk"], "wk")
        self.dma("pool", wv, w_inv[:, :, 1792:2304], [], ["wv"], "wv")
        self.dma("pool", wh, w_inv[:, :, 0:768], [], ["wh"], "wh")
        self.dma("sp", vtab, vtab_d.rearrange("p (a b c) -> p a b c", a=17, b=2), [], ["vtab"], "vtab")
        b7 = self.banks[7][:, :].bitcast(BF16)
        SB = 4
        sidx = 0
        for ti, (o, n) in enumerate(TILES):
            rp, rkey = self.load_rope(rope_d, ti)
            for h in range(NHEAD):
                pb = self.nxt("kvbank", 2)
                ps = self.banks[pb]
                for kt in range(NFT):
                    self.mm(ps[:, :n], wk[:, kt, h * 128:(h + 1) * 128], hb[:, kt, o:o + n], kt == 0, kt == NFT - 1,
                            ["wk", ("hb", kt, ti)], [("bank", pb)])
                self.rope(ps, ("bank", pb), rp, rkey, n, kT[:, h, :n], ("kT", h))
            nsub = max(1, n // 128)
            for sub in range(nsub):
                ns = min(n, 128)
                c0 = o + sub * 128
                pb = 2 + self.nxt("kvbank2", 2)
                ps = self.banks[pb]
                for kt in range(NFT):
                    self.mm(ps[:ns, :], hb[:, kt, c0:c0 + ns], wv[:, kt, :], kt == 0, kt == NFT - 1,
                            ["wv", ("hb", kt, ti)], [("bank", pb)])
                k2 = self.nxt("vfull", 2)
                self.tt("dve", vfull[k2][:ns, :, :], ps[:ns, :].rearrange("p (h e) -> p h e", h=4),
                        vtab[:ns, sidx, 1, :].unsqueeze(2).to_broadcast([ns, 4, 128]), ALU.mult,
                        [("bank", pb), "vtab"], [("vfull", k2)])
                for h in range(NHEAD):
                    self.tr(b7[:ns, h * 128:(h + 1) * 128], kT[:, h, sub * 128:sub * 128 + ns], self.ident_b,
                            [("kT", h), "ident_b"], [("bank", 7)])
                self.act(kTok[k2][:ns, :, :], b7[:ns, 0:512].rearrange("p (h e) -> p h e", h=4), AF.Copy, [("bank", 7)], [("kTok", k2)])
                for h in range(NHEAD):
                    self.mm(self.banks[SB][:, h * 128:(h + 1) * 128], kTok[k2][:ns, h, :], vfull[k2][:ns, h, :], sidx == 0, sidx == 16,
                            [("kTok", k2), ("vfull", k2)], [("bank", SB)])
                sidx += 1
        HB = 5
        ps = self.banks[HB]
        for mt in range(6):
            for kt in range(NFT):
                self.mm(ps[:, mt * 32:(mt + 1) * 32], wh[:, kt, mt * 128:(mt + 1) * 128], hb[:, kt, T - 32:T], kt == 0, kt == NFT - 1,
                        ["wh", ("hb", kt, 4)], [("bank", HB)])
        self.act(send[:, 0:512], self.banks[SB][:, :], AF.Copy, [("bank", SB)], ["send_s"])
        self.act(sgm[:, :, :], ps[:, 128:192].rearrange("p (a b) -> p a b", a=2), AF.Sigmoid, [("bank", HB)], ["sgm"])
        self.tt("dve", send[:, 512:576].rearrange("p (a b) -> p a b", a=2), sgm[:, :, :], ps[:, 64:128].rearrange("p (a b) -> p a b", a=2),
                ALU.mult, ["sgm", ("bank", HB)], ["send_u"])
        self.act(send[:, 576:640], ps[:, 0:64], AF.Copy, [("bank", HB)], ["send_p"])
        self.dma("sp", send_d[:, :], send, ["send_s", "send_u", "send_p"], [], "send")
        P.barrier()
        A.reset(m0)

    def phase_mix(self, l, w_in, pool_w, conv_pw, w_out, rope_d, vtab_d, tab_d, sprev_srcs, hprev_src, h_src, h_dst):
        A, P = self.A, self.P
        m0 = A.mark()
        hb = self.hb
        par = self.par[l]
        tab = A.alloc((NTAB,), F32, "tab")
        vtab = A.alloc((17, 2, 4), F32, "vtab")
        self.ropet = [A.alloc((2, 512), F32, "ropet") for _ in range(2)]
        self.ropeA = A.alloc((512,), F32, "ropeA")
        self.ropeB = A.alloc((512,), F32, "ropeB")
        sprev = A.alloc((3, 512), F32, "sprev")
        hprev = A.alloc((128,), F32, "hprev")
        S = A.alloc((4, 128), F32, "S")
        Stmp = A.alloc((4, 128), F32, "Stmp")
        Sb = A.alloc((4, 128), BF16, "Sb")
        wsl = [A.alloc((8, 128), BF16, "wsl") for _ in range(4)]
        wv = A.alloc((8, 512), BF16, "wv")
        wbd = A.alloc((2, 128), BF16, "wbd")
        wpw = A.alloc((2, 256), BF16, "wpw")
        XP = A.alloc((2, 15 + 512), F32, "XP")
        S2e = A.alloc((526,), F32, "S2e")
        S4e = A.alloc((524,), F32, "S4e")
        S8e = A.alloc((520,), F32, "S8e")
        Mb = A.alloc((512,), F32, "Mb")
        ypool = A.alloc((2, 512), BF16, "ypool")
        U = A.alloc((2, 30 + 512), BF16, "U")
        Utmp = A.alloc((2, 30), BF16, "Utmp")
        cdiag = A.alloc((2, 31, 128), BF16, "cdiag")
        acc = A.alloc((2, 512), F32, "acc")
        sgt = A.alloc((512,), F32, "sgt")
        cact = A.alloc((2, 512), BF16, "cact")
        qrot = A.alloc((512,), F32, "qrot")
        qT = A.alloc((4, 512), BF16, "qT")
        qdec = A.alloc((4, 512), BF16, "qdec")
        kT = A.alloc((4, 512), BF16, "kT")
        sgate = A.alloc((4, 512), F32, "sgate")
        vbf = A.alloc((4, 4, 128), BF16, "vbf")
        vdec = A.alloc((4, 4, 128), BF16, "vdec")
        kTok = A.alloc((4, 4, 128), BF16, "kTok")
        sdT = A.alloc((4, 4, 128), BF16, "sdT")
        ycat = A.alloc((8, 512), BF16, "ycat")
        t1 = A.alloc((512,), F32, "t1")
        self.ln_alloc(1)
        b7 = self.banks[7][:, :].bitcast(BF16)
        w_inv = w_in.rearrange("(kt p) c -> p kt c", p=128)
        w_outv = w_out.rearrange("(kt p) c -> p kt c", p=128)

        def tcol(c0, w):
            return tab[:, c0:c0 + w]

        self.dma("sp", tab, tab_d[:, :], [], ["tab"], "tab")
        self.dma("sp", vtab, vtab_d.rearrange("p (a b c) -> p a b c", a=17, b=2), [], ["vtab"], "vtab")
        for i in range(3):
            self.dma("sp", sprev[:, i, :], sprev_srcs[i], [], ["sprev"], "sprev")
        self.dma("sp", hprev, hprev_src, [], ["hprev"], "hprev")
        self.dma("pool", wv, w_inv[:, :, 1792:2304], [], ["wv"], "wv")
        P.add("dve", lambda e: e.memset(wbd[:, :, :], 0.0), [], ["wbd"])
        for tl in range(2):
            for a in range(2):
                self.dma("pool", wbd[64 * a:64 * a + 64, tl, 64 * a:64 * a + 64], pool_w[2 * tl + a, :, :], ["wbd"], [("wbd2", tl, a)], "wbd")
        self.dma("pool", wpw, conv_pw.rearrange("(kt p) c -> p kt c", p=128), [], ["wpw"], "wpw")
        for tl in range(2):
            for j in range(31):
                self.ts1("dve", cdiag[:, tl, j, :], self.ident_f, par[:, PAR_CONV_W + tl * 31 + j:PAR_CONV_W + tl * 31 + j + 1], ALU.mult,
                         ["ident_f", f"par{l}"], [("cdiag", tl)])
        P.add("dve", lambda e: e.memset(XP[:, :, 0:15], 0.0), [], ["XPh"])
        P.add("dve", lambda e: e.memset(U[:, :, 0:30], 0.0), [], ["Uh"])
        for i in range(3):
            cb = tab[:, TB_COEF + 4 * i:TB_COEF + 4 * i + 4].unsqueeze(2).to_broadcast([128, 4, 128])
            src = sprev[:, i, :].rearrange("p (h e) -> p h e", h=4)
            if i == 0:
                self.tt("dve", S[:, :, :], src, cb, ALU.mult, ["sprev", "tab"], ["S"])
            else:
                self.tt("dve", Stmp[:, :, :], src, cb, ALU.mult, ["sprev", "tab"], ["Stmp"])
                self.tt("dve", S[:, :, :], S[:, :, :], Stmp[:, :, :], ALU.add, ["S", "Stmp"], ["S"])
        self.act(Sb[:, :, :], S[:, :, :], AF.Copy, ["S"], ["Sb"])
        dtab = tab[:, TB_DTAB:TB_DTAB + 512].rearrange("p (h e) -> p h e", h=4)
        ff = tab[:, TB_FLAG:TB_FLAG + 1]
        nf = tab[:, TB_FLAG + 1:TB_FLAG + 2]

        sidx = 0
        STOP = getattr(self, "mix_stop", 99)
        NT = getattr(self, "mix_tiles", len(TILES))
        for ti, (o, n) in enumerate(TILES[:NT]):
            rp, rkey = self.load_rope(rope_d, ti)
            self.dma("sp", self.yp[0][:, :, :n], h_src[:, :, o:o + n], [], [("yp", 0)], ("ypld", 0))
            nsub = max(1, n // 128)
            ns = min(n, 128)

            def proj(col0):
                s = self.nxt("wsl", 4)
                self.dma("pool", wsl[s], w_inv[:, :, col0:col0 + 128], [], [("wsl", s)], ("wsl", s))
                pb = self.nxt("pjbank", 3)
                ps = self.banks[pb]
                for kt in range(NFT):
                    self.mm(ps[:, :n], wsl[s][:, kt, :], hb[:, kt, o:o + n], kt == 0, kt == NFT - 1,
                            [("wsl", s), ("hb", kt, ti)], [("bank", pb)])
                return ps, ("bank", pb)

            for tl in range(2):
                ps, pk = proj(0 + tl * 128)
                self.act(XP[:, tl, 15:15 + n], ps[:, :n], AF.Copy, [pk], [("XP", tl)])
                x = XP[:, tl, :]
                hk = [("XP", tl), "XPh"]
                self.tt(PENG, S2e[:, 0:n + 14], x[:, 1:15 + n], x[:, 0:14 + n], ALU.add, hk, ["S2e"])
                self.tt(PENG, S4e[:, 0:n + 12], S2e[:, 2:n + 14], S2e[:, 0:n + 12], ALU.add, ["S2e"], ["S4e"])
                self.tt(PENG, S8e[:, 0:n + 8], S4e[:, 4:n + 12], S4e[:, 0:n + 8], ALU.add, ["S4e"], ["S8e"])
                pc = lambda w: tab[:, TB_PC + tl * 4 + w:TB_PC + tl * 4 + w + 1]
                self.ts1("dve", Mb[:, :n], S2e[:, 14:14 + n], pc(0), ALU.mult, ["S2e", "tab"], ["Mb"])
                self.stt("dve", Mb[:, :n], S4e[:, 12:12 + n], pc(1), Mb[:, :n], ALU.mult, ALU.add, ["S4e", "Mb", "tab"], ["Mb"])
                self.stt("dve", Mb[:, :n], S8e[:, 8:8 + n], pc(2), Mb[:, :n], ALU.mult, ALU.add, ["S8e", "Mb", "tab"], ["Mb"])
                self.stt("dve", Mb[:, :n], S8e[:, 8:8 + n], pc(3), Mb[:, :n], ALU.mult, ALU.add, ["S8e", "Mb", "tab"], ["Mb"])
                self.stt("dve", Mb[:, :n], S8e[:, 0:n], pc(3), Mb[:, :n], ALU.mult, ALU.add, ["S8e", "Mb", "tab"], ["Mb"])
                if ti == 0:
                    self.tt(PENG, Mb[:, :n], Mb[:, :n], tab[:, TB_PCORR + tl * 16:TB_PCORR + tl * 16 + 16], ALU.mult, ["Mb", "tab"], ["Mb"])
                self.tt("dve", ypool[:, tl, :n], Mb[:, :n], x[:, 15:15 + n], ALU.subtract, ["Mb", ("XP", tl)], [("ypool", tl)])
                self.copy("dve", XP[:, tl, 0:15], XP[:, tl, n:n + 15], [("XP", tl)], ["XPh", ("XP", tl)])
                if ti == 0:
                    self.ts1("dve", XP[:, tl, 0:15], XP[:, tl, 0:15], ff, ALU.mult, ["XPh", ("XP", tl), "tab"], ["XPh", ("XP", tl)])
                    self.stt("dve", XP[:, tl, 0:15], hprev[:, 64 + tl * 32 + 17:64 + tl * 32 + 32], nf, XP[:, tl, 0:15], ALU.mult, ALU.add,
                             ["hprev", "tab", "XPh", ("XP", tl)], ["XPh", ("XP", tl)])
                pb = 6
                self.mm(self.banks[pb][:, :n], wbd[:, tl, :], ypool[:, tl, :n], True, True,
                        [("wbd2", tl, 0), ("wbd2", tl, 1), "wbd", ("ypool", tl)], [("bank", pb)])
                self.act(ycat[:, tl, :n], self.banks[pb][:, :n], AF.Copy, [("bank", pb), f"par{l}"], [("ycat", tl)],
                         scale=par[:, PAR_POOL_SCALE + tl:PAR_POOL_SCALE + tl + 1])
            if STOP <= 1:
                continue
            for tl in range(2):
                psg, pkg = proj(512 + tl * 128)
                self.act(sgt[:, :n], psg[:, :n], AF.Sigmoid, [pkg], ["sgt"])
                psa, pka = proj(256 + tl * 128)
                self.tt("dve", U[:, tl, 30:30 + n], sgt[:, :n], psa[:, :n], ALU.mult, ["sgt", pka], [("U", tl)])
                uk = [("U", tl), "Uh", ("cdiag", tl)]
                cb = 4 + tl
                for j in range(31):
                    self.mm(self.banks[cb][:, :n], cdiag[:, tl, j, :], U[:, tl, j:j + n], j == 0, j == 30, uk, [("bank", cb)])
                self.act(acc[:, tl, :n], self.banks[cb][:, :n], AF.Identity, [("bank", cb), f"par{l}"], [("acc", tl)],
                         bias=par[:, PAR_CONV_DB + tl:PAR_CONV_DB + tl + 1])
                if n >= 30:
                    self.copy("dve", U[:, tl, 0:30], U[:, tl, n:n + 30], [("U", tl)], ["Uh", ("U", tl)])
                else:
                    self.copy("dve", Utmp[:, tl, :], U[:, tl, n:n + 30], [("U", tl), "Uh"], [("Utmp", tl)])
                    self.copy("dve", U[:, tl, 0:30], Utmp[:, tl, :], [("Utmp", tl)], ["Uh", ("U", tl)])
                if ti == 0:
                    self.ts1("dve", U[:, tl, 0:30], U[:, tl, 0:30], ff, ALU.mult, ["Uh", ("U", tl), "tab"], ["Uh", ("U", tl)])
                    self.stt("dve", U[:, tl, 0:30], hprev[:, tl * 32 + 2:tl * 32 + 32], nf, U[:, tl, 0:30], ALU.mult, ALU.add,
                             ["hprev", "tab", "Uh", ("U", tl)], ["Uh", ("U", tl)])
            for tl in range(2):
                q = self.nxt("sq", 2)
                self.act(self.sq[q][:, :n], acc[:, tl, :n], AF.Square, [("acc", tl)], [("sq", q)])
                self.act(self.yb[q][:, :n], acc[:, tl, :n], AF.Copy, [("acc", tl)], [("yb", q)])
                self.mm(self.banks[4][:, :n], self.ones_b, self.yb[q][:, :n], tl == 0, tl == 1, ["ones_b", ("yb", q)], [("bank", 4)])
                self.mm(self.banks[5][:, :n], self.ones_b, self.sq[q][:, :n], tl == 0, tl == 1, ["ones_b", ("sq", q)], [("bank", 5)])
            mn, rd, nm = self.mean[0], self.rstd[0], self.nmr[0]
            self.stats_finish(4, 5, n, 1.0 / 256, LN_EPS, mn, rd, nm, 0)
            for tl in range(2):
                self.tt("dve", acc[:, tl, :n], acc[:, tl, :n], rd[:, :n], ALU.mult, [("acc", tl), ("rstd", 0)], [("acc", tl)])
                self.tt("dve", acc[:, tl, :n], acc[:, tl, :n], nm[:, :n], ALU.add, [("acc", tl), ("nmr", 0)], [("acc", tl)])
                self.act(cact[:, tl, :n], acc[:, tl, :n], AF.Silu, [("acc", tl), f"par{l}"], [("cact", tl)],
                         scale=par[:, PAR_CONV_LN_G + tl:PAR_CONV_LN_G + tl + 1], bias=par[:, PAR_CONV_LN_B + tl:PAR_CONV_LN_B + tl + 1])
            for mt in range(2):
                pb = 6
                for kt in range(2):
                    self.mm(self.banks[pb][:, :n], wpw[:, kt, mt * 128:(mt + 1) * 128], cact[:, kt, :n], kt == 0, kt == 1,
                            ["wpw", ("cact", kt)], [("bank", pb)])
                self.act(ycat[:, 2 + mt, :n], self.banks[pb][:, :n], AF.Copy, [("bank", pb)], [("ycat", 2 + mt)])
            if STOP <= 2:
                continue
            for h in range(NHEAD):
                ps, pk = proj(768 + h * 128)
                self.rope(ps, pk, rp, rkey, n, qrot[:, :n], "qrot", sum_eng=PENG)
                self.act(qT[:, h, :n], qrot[:, :n], AF.Copy, ["qrot"], [("qT", h)])
                if n >= 64:
                    self.tt("dve", qdec[:, h, :n].rearrange("p (c i) -> p c i", i=64), qrot[:, :n].rearrange("p (c i) -> p c i", i=64),
                            tab[:, TB_QD + h * 64:TB_QD + h * 64 + 64].unsqueeze(1).to_broadcast([128, n // 64, 64]), ALU.mult,
                            ["qrot", "tab"], [("qdec", h)])
            if STOP <= 2.2:
                continue
            for h in range(NHEAD):
                ps, pk = proj(1280 + h * 128)
                self.rope(ps, pk, rp, rkey, n, kT[:, h, :n], ("kT", h), sum_eng=PENG)
            if STOP <= 2.4:
                continue
            for h in range(NHEAD):
                ps, pk = proj(2304 + h * 128)
                self.act(sgate[:, h, :n], ps[:, :n], AF.Silu, [pk], [("sgate", h)])
            if STOP <= 2.6:
                continue
            for sub in range(nsub):
                c0 = o + sub * 128
                pb = 3
                ps = self.banks[pb]
                for kt in range(NFT):
                    self.mm(ps[:ns, :], hb[:, kt, c0:c0 + ns], wv[:, kt, :], kt == 0, kt == NFT - 1,
                            ["wv", ("hb", kt, ti)], [("bank", pb)])
                psv = ps[:ns, :].rearrange("p (h e) -> p h e", h=4)
                self.act(vbf[:ns, sub, :, :], psv, AF.Copy, [("bank", pb)], [("vbf", sub)])
                self.tt("dve", vdec[:ns, sub, :, :], psv, vtab[:ns, sidx + sub, 0, :].unsqueeze(2).to_broadcast([ns, 4, 128]), ALU.mult,
                        [("bank", pb), "vtab"], [("vdec", sub)])
                if STOP <= 2.8:
                    continue
                for h in range(NHEAD):
                    self.tr(b7[:ns, h * 128:(h + 1) * 128], kT[:, h, sub * 128:sub * 128 + ns], self.ident_b,
                            [("kT", h), "ident_b"], [("bank", 7)])
                self.act(kTok[:ns, sub, :, :], b7[:ns, 0:512].rearrange("p (h e) -> p h e", h=4), AF.Copy, [("bank", 7)], [("kTok", sub)])
            if STOP <= 3:
                sidx += nsub
                continue
            for h in range(NHEAD):
                pb = 4 + self.nxt("scbank", 2)
                ps = self.banks[pb]
                for sub in range(nsub):
                    self.mm(ps[:ns, sub * 128:sub * 128 + ns], kT[:, h, sub * 128:sub * 128 + ns], qT[:, h, sub * 128:sub * 128 + ns], True, True,
                            [("kT", h), ("qT", h)], [("bank", pb)])
                dm = tab[:ns, TB_DM2 + h * 128:TB_DM2 + h * 128 + ns]
                self.tt("dve", sdT[:ns, h, 0:nsub, :ns], ps[:ns, 0:nsub * 128].rearrange("p (s i) -> p s i", i=128)[:, :, :ns],
                        dm.unsqueeze(1).to_broadcast([ns, nsub, ns]), ALU.mult, [("bank", pb), "tab"], [("sdT", h)])
            if STOP <= 4:
                sidx += nsub
                continue
            for sub in range(nsub):
                for h in range(NHEAD):
                    ob = self.banks[h]
                    self.mm(ob[:, sub * 128:sub * 128 + ns], vbf[:ns, sub, h, :], sdT[:ns, h, sub, :ns], True, n < 64,
                            [("vbf", sub), ("sdT", h)], [("bank", h)])
                nch = max(1, ns // 64)
                for cc in range(nch):
                    cw = min(ns, 64)
                    r0 = cc * 64
                    if n >= 64:
                        for h in range(NHEAD):
                            ob = self.banks[h]
                            cs = sub * 128 + cc * 64
                            self.mm(ob[:, cs:cs + 64], Sb[:, h, :], qdec[:, h, cs:cs + 64], False, True,
                                    ["Sb", ("qdec", h)], [("bank", h)])
                    for h in range(NHEAD):
                        self.mm(self.banks[6][:, h * 128:(h + 1) * 128], kTok[r0:r0 + cw, sub, h, :], vdec[r0:r0 + cw, sub, h, :], True, True,
                                [("kTok", sub), ("vdec", sub)], [("bank", 6)])
                    if n >= 64:
                        self.tt("dve", S[:, :, :], S[:, :, :], dtab, ALU.mult, ["S", "tab"], ["S"])
                    self.tt("dve", S[:, :, :], S[:, :, :], self.banks[6][:, :].rearrange("p (h e) -> p h e", h=4), ALU.add,
                            ["S", ("bank", 6)], ["S"])
                    self.act(Sb[:, :, :], S[:, :, :], AF.Copy, ["S"], ["Sb"])
            if STOP <= 5:
                sidx += nsub
                continue
            for h in range(NHEAD):
                ob = self.banks[h]
                q = self.nxt("sq", 2)
                self.act(self.sq[q][:, :n], ob[:, :n], AF.Square, [("bank", h)], [("sq", q)])
                self.act(self.yb[q][:, :n], ob[:, :n], AF.Copy, [("bank", h)], [("yb", q)])
                self.mm(self.banks[4][:, :n], self.ones_b, self.yb[q][:, :n], True, True, ["ones_b", ("yb", q)], [("bank", 4)])
                self.mm(self.banks[5][:, :n], self.ones_b, self.sq[q][:, :n], True, True, ["ones_b", ("sq", q)], [("bank", 5)])
                self.stats_finish(4, 5, n, 1.0 / 128, LN_EPS, mn, rd, nm, 0)
                self.tt("dve", t1[:, :n], ob[:, :n], rd[:, :n], ALU.mult, [("bank", h), ("rstd", 0)], ["t1"])
                self.tt("dve", t1[:, :n], t1[:, :n], nm[:, :n], ALU.add, ["t1", ("nmr", 0)], ["t1"])
                self.stt("dve", ycat[:, 4 + h, :n], t1[:, :n], par[:, PAR_GN_G + h:PAR_GN_G + h + 1], sgate[:, h, :n], ALU.mult, ALU.mult,
                         ["t1", ("sgate", h), f"par{l}"], [("ycat", 4 + h)])
            if STOP <= 6:
                sidx += nsub
                continue
            for fo in range(NFT):
                s = self.nxt("wsl", 4)
                self.dma("pool", wsl[s], w_outv[:, :, fo * 128:(fo + 1) * 128], [], [("wsl", s)], ("wsl", s))
                pb = self.nxt("dnbank", 2)
                py = self.banks[pb]
                for kt in range(NFT):
                    self.mm(py[:, :n], wsl[s][:, kt, :], ycat[:, kt, :n], kt == 0, kt == NFT - 1,
                            [("wsl", s), ("ycat", kt)], [("bank", pb)])
                self.ln_accum(0, fo, n, py, pb, 1.0 / ALPHA)
            self.ln_finish(l, 1, 0, ti, h_dst, eng=PENG)
            sidx += nsub
        P.barrier()
        A.reset(m0)

    def phase_final(self, h_src, out_d):
        A, P = self.A, self.P
        m0 = A.mark()
        xin = [A.alloc((8, 128), F32, "xin") for _ in range(2)]
        xo = [A.alloc((1024,), F32, "xo") for _ in range(2)]
        for s in range(BLK // 128):
            k = s % 2
            c0 = NPRE + s * 128
            self.dma("sp", xin[k], h_src[:, :, c0:c0 + 128], [], [("xin", k)], ("xin", k))
            for half in range(2):
                bk = self.nxt("finbank", 2)
                ps = self.banks[bk]
                for j in range(4):
                    ft = half * 4 + j
                    self.tr(ps[:, j * 128:(j + 1) * 128], xin[k][:, ft, :], self.ident_f, [("xin", k), "ident_f"], [("bank", bk)])
                self.act(xo[k][:, half * 512:(half + 1) * 512], ps[:, :], AF.Copy, [("bank", bk)], [("xo", k, half)])
            self.dma("sp", out_d[s * 128:(s + 1) * 128, :], xo[k], [("xo", k, 0), ("xo", k, 1)], [("xo", k, 0), ("xo", k, 1)], ("xo", k))
        P.barrier()
        A.reset(m0)


PAR_LN_G = 0
PAR_LN_B = 24
PAR_LNIN_G = 48
PAR_LNIN_B = 56
PAR_POOL_SCALE = 64
PAR_CONV_DB = 66
PAR_CONV_LN_G = 68
PAR_CONV_LN_B = 70
PAR_GN_G = 72
PAR_CONV_W = 76
NPAR = 138

TB_DM2 = 0
TB_QD = 512
TB_DTAB = 768
TB_COEF = 1280
TB_FLAG = 1292
TB_PC = 1294
TB_PCORR = 1302
NTAB = 1334


def pack_params(inp, l):
    p = np.zeros((128, NPAR), np.float32)
    for i in range(3):
        p[:, PAR_LN_G + 8 * i:PAR_LN_G + 8 * i + 8] = inp["ln_g"][l, i].reshape(8, 128).T
        p[:, PAR_LN_B + 8 * i:PAR_LN_B + 8 * i + 8] = inp["ln_b"][l, i].reshape(8, 128).T
    p[:, PAR_LNIN_G:PAR_LNIN_G + 8] = inp["ln_in_g"].reshape(8, 128).T
    p[:, PAR_LNIN_B:PAR_LNIN_B + 8] = inp["ln_in_b"].reshape(8, 128).T
    p[:, PAR_POOL_SCALE:PAR_POOL_SCALE + 2] = inp["pool_scale"][l].reshape(2, 128).T
    p[:, PAR_CONV_DB:PAR_CONV_DB + 2] = inp["conv_db"][l].reshape(2, 128).T
    p[:, PAR_CONV_LN_G:PAR_CONV_LN_G + 2] = inp["conv_ln_g"][l].reshape(2, 128).T
    p[:, PAR_CONV_LN_B:PAR_CONV_LN_B + 2] = inp["conv_ln_b"][l].reshape(2, 128).T
    p[:, PAR_GN_G:PAR_GN_G + 4] = inp["ret_gn_g"][l].reshape(4, 128).T
    cw = inp["conv_dw"][l]
    for tl in range(2):
        p[:, PAR_CONV_W + tl * 31:PAR_CONV_W + tl * 31 + 31] = cw[:, tl * 128:(tl + 1) * 128].T
    return p


def consts_arr():
    c = np.zeros((128, 256), np.float32)
    c[:, 0:128] = np.eye(128, dtype=np.float32)
    c[:, 128:256] = 1.0
    return c


def make_tables(jj):
    first = 1.0 if jj == 0 else 0.0
    g = np.array(GAMMAS, np.float64)
    pos = np.concatenate([np.arange(NPRE), NPRE + BLK * jj + np.arange(BLK)]).astype(np.float32)
    inv_freq = (np.float32(10000.0) ** (-np.arange(0, DH, 2, dtype=np.float32) / np.float32(DH))).astype(np.float32)
    ang = (pos[:, None] * inv_freq[None, :]).astype(np.float32)
    cos, sin = np.cos(ang).T, np.sin(ang).T
    rope = np.zeros((128, 2, T), np.float32)
    rope[0:64, 0], rope[64:128, 0] = cos, cos
    rope[0:64, 1], rope[64:128, 1] = sin, -sin
    vtab = np.zeros((128, 17, 2, 4), np.float64)
    ip = np.arange(16)
    for h in range(4):
        vtab[:16, 0, 0, h] = first * g[h] ** (15 - ip)
        vtab[:16, 0, 1, h] = first * g[h] ** (BLK + 15 - ip)
        for s in range(1, 17):
            nidx = (s - 1) * 128 + np.arange(128)
            vtab[:, s, 0, h] = g[h] ** (63 - (nidx % 64))
            vtab[:, s, 1, h] = g[h] ** (BLK - 1 - nidx)
    tab = np.zeros((128, NTAB), np.float64)
    j = np.arange(128)[:, None]
    i = np.arange(128)[None, :]
    same = (j // 64) == (i // 64)
    for h in range(4):
        tab[:, TB_DM2 + h * 128:TB_DM2 + (h + 1) * 128] = np.where(same, g[h] ** np.abs(i - j), 0.0) * QSCALE
        tab[:, TB_QD + h * 64:TB_QD + (h + 1) * 64] = (g[h] ** (np.arange(64) + 1.0))[None, :] * QSCALE
        tab[:, TB_DTAB + h * 128:TB_DTAB + (h + 1) * 128] = g[h] ** 64
        for s in range(3):
            tab[:, TB_COEF + 4 * s + h] = (g[h] ** (BLK * (jj - 1 - s))) if s < jj else 0.0
    tab[:, TB_FLAG] = first
    tab[:, TB_FLAG + 1] = 1.0 - first
    wins = (2, 4, 8, 16)
    for tl in range(2):
        for p in range(128):
            grp = (tl * 128 + p) // 64
            w = wins[grp]
            tab[p, TB_PC + tl * 4 + grp] = 1.0 / w
            tt_ = np.arange(16)
            tab[p, TB_PCORR + tl * 16:TB_PCORR + tl * 16 + 16] = w / np.minimum(tt_ + 1, w)
    return rope, vtab.reshape(128, 136).astype(np.float32), tab.astype(np.float32)


def _decl_common(B):
    B.setup_consts()
    rope = B.inp("rope", [128, 2, T])
    vtab = B.inp("vtab", [128, 136])
    tab = B.inp("tab", [128, NTAB])
    return rope, vtab, tab


def _w(B, l):
    return dict(
        ffn1_w13=B.inp(f"ffn1_w13_{l}", [D, 2 * DFF]), ffn1_w2=B.inp(f"ffn1_w2_{l}", [DFF, D]),
        ffn2_w13=B.inp(f"ffn2_w13_{l}", [D, 2 * DFF]), ffn2_w2=B.inp(f"ffn2_w2_{l}", [DFF, D]),
        w_in=B.inp(f"w_in_{l}", [D, DIN]), w_out=B.inp(f"w_out_{l}", [D, D]),
        pool_w=B.inp(f"pool_w_{l}", [4, 64, 64]), conv_pw=B.inp(f"conv_pw_{l}", [256, 256]))


def build_launch(kind):
    nc = bass.Bass("TRN2", target_bir_lowering=False)
    B = Builder(nc)
    if kind == 0:
        x = B.inp("x", [BLK, D])
        xpre = B.inp("xpre", [NPRE, D])
        rope, vtab, tab = _decl_common(B)
        w13, w2, w_in = B.inp("ffn1_w13_0", [D, 2 * DFF]), B.inp("ffn1_w2_0", [DFF, D]), B.inp("w_in_0", [D, DIN])
        h0 = B.scratch("h0", [128, NFT, T])
        h1 = B.outp("h1", [128, NFT, T])
        send = B.outp("send", [128, 640])
        B.phase_inln(x, xpre, h0)
        B.load_hb(h0)
        B.phase_ffn(0, w13, w2, 0, h0, h1)
        B.phase_kv(0, w_in, rope, vtab, send)
    else:
        l = kind - 1
        hin = B.inp("hin", [128, NFT, T])
        sprev = B.inp("sprev", [128, 3, 640])
        hprev = B.inp("hprev", [128, 128])
        rope, vtab, tab = _decl_common(B)
        w_in, w_out = B.inp(f"w_in_{l}", [D, DIN]), B.inp(f"w_out_{l}", [D, D])
        pool_w, conv_pw = B.inp(f"pool_w_{l}", [4, 64, 64]), B.inp(f"conv_pw_{l}", [256, 256])
        f2a, f2b = B.inp(f"ffn2_w13_{l}", [D, 2 * DFF]), B.inp(f"ffn2_w2_{l}", [DFF, D])
        hA = B.scratch("hA", [128, NFT, T])
        hB = B.scratch("hB", [128, NFT, T])
        B.load_hb(hin)
        B.phase_mix(l, w_in, pool_w, conv_pw, w_out, rope, vtab, tab, [sprev[:, i, 0:512] for i in range(3)], hprev[:, :], hin, hA)
        B.phase_ffn(l, f2a, f2b, 2, hA, hB)
        if l + 1 < DEPTH:
            w13, w2, w_in2 = B.inp(f"ffn1_w13_{l + 1}", [D, 2 * DFF]), B.inp(f"ffn1_w2_{l + 1}", [DFF, D]), B.inp(f"w_in_{l + 1}", [D, DIN])
            h1 = B.outp("h1", [128, NFT, T])
            send = B.outp("send", [128, 640])
            B.phase_ffn(l + 1, w13, w2, 0, hB, h1)
            B.phase_kv(l + 1, w_in2, rope, vtab, send)
        else:
            out = B.outp("out", [BLK, D])
            B.phase_final(hB, out)
    B.P.emit(nc)
    return nc, B


def build_mix_debug(l, stop, ntiles):
    nc = bass.Bass("TRN2", target_bir_lowering=False)
    B = Builder(nc)
    B.mix_stop, B.mix_tiles = stop, ntiles
    hin = B.inp("hin", [128, NFT, T])
    sprev = B.inp("sprev", [128, 3, 640])
    hprev = B.inp("hprev", [128, 128])
    rope, vtab, tab = _decl_common(B)
    w_in, w_out = B.inp(f"w_in_{l}", [D, DIN]), B.inp(f"w_out_{l}", [D, D])
    pool_w, conv_pw = B.inp(f"pool_w_{l}", [4, 64, 64]), B.inp(f"conv_pw_{l}", [256, 256])
    hA = B.outp("h1", [128, NFT, T])
    B.load_hb(hin)
    B.phase_mix(l, w_in, pool_w, conv_pw, w_out, rope, vtab, tab, [sprev[:, i, 0:512] for i in range(3)], hprev[:, :], hin, hA)
    B.P.emit(nc)
    return nc, B


def build_fused():
    nc = bass.Bass("TRN2", target_bir_lowering=False)
    B = Builder(nc)
    x = B.inp("x", [4 * BLK, D])
    xpre = B.inp("xpre", [4, NPRE, D])
    zeros = B.inp("zeros", [128, 640])
    B.setup_consts()
    ropes = [B.inp(f"rope{b}", [128, 2, T]) for b in range(4)]
    vtabs = [B.inp(f"vtab{b}", [128, 136]) for b in range(4)]
    tabs = [B.inp(f"tab{b}", [128, NTAB]) for b in range(4)]
    W = [_w(B, l) for l in range(DEPTH)]
    out = B.outp("out", [4 * BLK, D])
    hX = [B.scratch(f"hX{b}", [128, NFT, T]) for b in range(4)]
    hY = [B.scratch(f"hY{b}", [128, NFT, T]) for b in range(4)]
    hA = B.scratch("hA", [128, NFT, T])
    hB = B.scratch("hB", [128, NFT, T])
    send = [[B.scratch(f"send{l}_{b}", [128, 640]) for b in range(4)] for l in range(DEPTH)]
    for b in range(4):
        B.phase_inln(x[b * BLK:(b + 1) * BLK, :], xpre[b], hX[b])
    for b in range(4):
        B.load_hb(hX[b])
        B.phase_ffn(0, W[0]["ffn1_w13"], W[0]["ffn1_w2"], 0, hX[b], hY[b])
        B.phase_kv(0, W[0]["w_in"], ropes[b], vtabs[b], send[0][b])
    for l in range(DEPTH):
        for b in range(4):
            B.load_hb(hY[b])
            sp = [send[l][i][:, 0:512] if i < b else zeros[:, 0:512] for i in range(3)]
            hp = send[l][b - 1][:, 512:640] if b > 0 else zeros[:, 512:640]
            B.phase_mix(l, W[l]["w_in"], W[l]["pool_w"], W[l]["conv_pw"], W[l]["w_out"], ropes[b], vtabs[b], tabs[b], sp, hp, hY[b], hA)
            B.phase_ffn(l, W[l]["ffn2_w13"], W[l]["ffn2_w2"], 2, hA, hB)
            if l + 1 < DEPTH:
                B.phase_ffn(l + 1, W[l + 1]["ffn1_w13"], W[l + 1]["ffn1_w2"], 0, hB, hY[b])
                B.phase_kv(l + 1, W[l + 1]["w_in"], ropes[b], vtabs[b], send[l + 1][b])
            else:
                B.phase_final(hB, out[b * BLK:(b + 1) * BLK, :])
    B.P.emit(nc)
    return nc, B


def kernel_fused(inp):
    nc, B = _get("fused")
    cst = consts_arr()
    tabs = [make_tables(jj) for jj in range(4)]
    base = {"consts": cst, "zeros": np.zeros((128, 640), np.float32)}
    for l in range(DEPTH):
        base[f"par{l}"] = pack_params(inp, l)
        for n in ["ffn1_w13", "ffn1_w2", "ffn2_w13", "ffn2_w2", "w_in", "w_out", "pool_w", "conv_pw"]:
            base[f"{n}_{l}"] = np.ascontiguousarray(inp[n][l])
    for b in range(4):
        base[f"rope{b}"], base[f"vtab{b}"], base[f"tab{b}"] = tabs[b]
    xpre = np.zeros((4, NPRE, D), np.float32)
    xpre[0] = inp["meta"]
    maps = []
    for c in range(8):
        m = dict(base)
        m["x"] = np.ascontiguousarray(inp["x"][c % 2])
        m["xpre"] = xpre
        maps.append({k: m[k] for k in B.din})
    res = run_bass_kernel_spmd(nc, maps, core_ids=list(range(8))).results
    return np.stack([res[0]["out"], res[1]["out"]], axis=0).astype(np.float32)


_CACHE = {}


def _get(kind):
    if kind not in _CACHE:
        _CACHE[kind] = build_fused() if kind == "fused" else build_launch(kind)
    return _CACHE[kind]


FUSED = False


def _exchange(sends):
    sprevs, hprevs = [], []
    for c in range(8):
        bi, jj = divmod(c, 4)
        sp = np.zeros((128, 3, 640), np.float32)
        for s in range(jj):
            sp[:, s, :] = sends[bi * 4 + s]
        hp = np.zeros((128, 128), np.float32)
        if jj > 0:
            hp[:] = sends[c - 1][:, 512:640]
        sprevs.append(sp)
        hprevs.append(hp)
    return sprevs, hprevs


def kernel(**inputs):
    inp = {k: np.asarray(v) for k, v in inputs.items()}
    if FUSED:
        return kernel_fused(inp)
    x = inp["x"]
    cst = consts_arr()
    pars = [pack_params(inp, l) for l in range(DEPTH)]
    tabs = [make_tables(jj) for jj in range(4)]
    common = []
    for c in range(8):
        bi, jj = divmod(c, 4)
        rope, vtab, tab = tabs[jj]
        common.append({"consts": cst, "par0": pars[0], "par1": pars[1], "rope": rope, "vtab": vtab, "tab": tab})

    def wsel(names, l):
        return {f"{n}_{l}": np.ascontiguousarray(inp[n][l]) for n in names}

    nc, B = _get(0)
    maps = []
    for c in range(8):
        bi, jj = divmod(c, 4)
        m = dict(common[c])
        m["x"] = np.ascontiguousarray(x[bi, jj * BLK:(jj + 1) * BLK])
        m["xpre"] = inp["meta"] if jj == 0 else np.zeros((NPRE, D), np.float32)
        m.update(wsel(["ffn1_w13", "ffn1_w2", "w_in"], 0))
        maps.append({k: m[k] for k in B.din})
    res = run_bass_kernel_spmd(nc, maps, core_ids=list(range(8))).results
    out = None
    for l in range(DEPTH):
        nc, B = _get(l + 1)
        sprevs, hprevs = _exchange([res[c]["send"] for c in range(8)])
        maps = []
        for c in range(8):
            m = dict(common[c])
            m["hin"] = res[c]["h1"]
            m["sprev"] = sprevs[c]
            m["hprev"] = hprevs[c]
            m.update(wsel(["w_in", "w_out", "pool_w", "conv_pw", "ffn2_w13", "ffn2_w2"], l))
            if l + 1 < DEPTH:
                m.update(wsel(["ffn1_w13", "ffn1_w2", "w_in"], l + 1))
            maps.append({k: m[k] for k in B.din})
        res = run_bass_kernel_spmd(nc, maps, core_ids=list(range(8))).results
    out = np.stack([np.concatenate([res[bi * 4 + jj]["out"] for jj in range(4)], axis=0) for bi in range(2)], axis=0)
    return out.astype(np.float32)
```
